# Optimizing a Trainium2 kernel written in Bass

```python
import math
import jax
import jax.numpy as jnp
from jax import lax
import numpy as np

D_MODEL = 1024
BATCH = 8
SEQ = 4096
DEPTH = 4

N_EVEN = (DEPTH + 1) // 2
N_ODD = DEPTH // 2
Q_BLOCK = 128
RMS_EPS = 1e-6
NEG_INF = -1e30
FOX_HEADS = 8
FOX_HEAD_DIM = 64
FOX_WIDTH = FOX_HEADS * FOX_HEAD_DIM
FORGET_BIAS_MEAN = 3.0
MLA_HEADS = 8
MLA_NOPE_DIM = 64
MLA_ROPE_DIM = 32
MLA_V_DIM = 64
MLA_Q_RANK = 384
MLA_KV_RANK = 256
ROPE_THETA = 10000.0
EVEN_IN_WIDTH = 3 * FOX_WIDTH + FOX_HEADS + MLA_Q_RANK + MLA_KV_RANK + MLA_ROPE_DIM
EVEN_MIX_WIDTH = FOX_WIDTH + MLA_HEADS * MLA_V_DIM
DIFF_HEADS = 8
DIFF_HEAD_DIM = 64
DIFF_V_DIM = 2 * DIFF_HEAD_DIM
ODD_IN_WIDTH = 4 * DIFF_HEADS * DIFF_HEAD_DIM + DIFF_HEADS * DIFF_V_DIM
ODD_MIX_WIDTH = DIFF_HEADS * DIFF_V_DIM
D_FF = 2816
CONV_WIDTH = 3

kernel_name = 'hybrid_fox_mla_diffattn_convffn_trunk'


def _rms_norm(x, g):
    xf = x.astype(jnp.float32)
    y = xf * lax.rsqrt(jnp.mean(xf * xf, axis=-1, keepdims=True) + RMS_EPS)
    return (y * g.astype(jnp.float32)).astype(x.dtype)


def _rope_tables(seq, dim, dtype):
    inv = ROPE_THETA ** (-jnp.arange(0, dim, 2, dtype=jnp.float32) / dim)
    ang = jnp.arange(seq, dtype=jnp.float32)[:, None] * inv[None, :]
    return jnp.cos(ang).astype(dtype), jnp.sin(ang).astype(dtype)


def _apply_rope(x, cos, sin):
    x1, x2 = jnp.split(x, 2, axis=-1)
    return jnp.concatenate([x1 * cos - x2 * sin, x1 * sin + x2 * cos], axis=-1)


def _alibi_slopes(n):
    return 2.0 ** (-8.0 * jnp.arange(1, n + 1, dtype=jnp.float32) / n)


def _causal_mask(s0, seq):
    qpos = s0 + jnp.arange(Q_BLOCK)
    kpos = jnp.arange(seq)
    return qpos[:, None] >= kpos[None, :], (qpos[:, None] - kpos[None, :]).astype(jnp.float32)


def _merge_blocks(o):
    o = jnp.moveaxis(o, 0, 1)
    return o.reshape(o.shape[0], -1, o.shape[3] * o.shape[4])


def _fox_attention(q, k, v, log_f):
    seq, dh = q.shape[1], q.shape[3]
    c = jnp.cumsum(log_f, axis=1).transpose(0, 2, 1)
    scale = dh ** -0.5

    def block(i):
        s0 = i * Q_BLOCK
        qb = lax.dynamic_slice_in_dim(q, s0, Q_BLOCK, axis=1)
        cq = lax.dynamic_slice_in_dim(c, s0, Q_BLOCK, axis=2)
        mask, _ = _causal_mask(s0, seq)
        logits = jnp.einsum('bqhd,bkhd->bhqk', qb, k).astype(jnp.float32) * scale
        logits = logits + cq[..., :, None] - c[..., None, :]
        logits = jnp.where(mask, logits, NEG_INF)
        p = jax.nn.softmax(logits, axis=-1).astype(v.dtype)
        return jnp.einsum('bhqk,bkhd->bqhd', p, v)

    return _merge_blocks(lax.map(block, jnp.arange(seq // Q_BLOCK)))


def _mla_attention(q_nope, q_rope, k_nope, k_rope, v):
    seq = q_nope.shape[1]
    scale = (MLA_NOPE_DIM + MLA_ROPE_DIM) ** -0.5

    def block(i):
        s0 = i * Q_BLOCK
        qn = lax.dynamic_slice_in_dim(q_nope, s0, Q_BLOCK, axis=1)
        qr = lax.dynamic_slice_in_dim(q_rope, s0, Q_BLOCK, axis=1)
        mask, _ = _causal_mask(s0, seq)
        logits = (jnp.einsum('bqhd,bkhd->bhqk', qn, k_nope)
                  + jnp.einsum('bqhr,bkr->bhqk', qr, k_rope)).astype(jnp.float32) * scale
        logits = jnp.where(mask, logits, NEG_INF)
        p = jax.nn.softmax(logits, axis=-1).astype(v.dtype)
        return jnp.einsum('bhqk,bkhd->bqhd', p, v)

    return _merge_blocks(lax.map(block, jnp.arange(seq // Q_BLOCK)))


def _diff_attention(q1, q2, k1, k2, v, lam, slopes):
    seq, dh = q1.shape[1], q1.shape[3]
    scale = dh ** -0.5

    def block(i):
        s0 = i * Q_BLOCK
        q1b = lax.dynamic_slice_in_dim(q1, s0, Q_BLOCK, axis=1)
        q2b = lax.dynamic_slice_in_dim(q2, s0, Q_BLOCK, axis=1)
        mask, dist = _causal_mask(s0, seq)
        alibi = -slopes[:, None, None] * dist[None]
        l1 = jnp.einsum('bqhd,bkhd->bhqk', q1b, k1).astype(jnp.float32) * scale + alibi
        l2 = jnp.einsum('bqhd,bkhd->bhqk', q2b, k2).astype(jnp.float32) * scale + alibi
        p = (jax.nn.softmax(jnp.where(mask, l1, NEG_INF), axis=-1)
             - lam * jax.nn.softmax(jnp.where(mask, l2, NEG_INF), axis=-1)).astype(v.dtype)
        return jnp.einsum('bhqk,bkhd->bqhd', p, v)

    return _merge_blocks(lax.map(block, jnp.arange(seq // Q_BLOCK)))


def _even_mixer(h, w_in, b_forget, q_norm, w_uq, kv_norm, w_ukv, w_out, cos, sin):
    bsz, seq, _ = h.shape
    proj = h @ w_in
    cuts = np.cumsum([FOX_WIDTH, FOX_WIDTH, FOX_WIDTH, FOX_HEADS, MLA_Q_RANK, MLA_KV_RANK]).tolist()
    fq, fk, fv, fg, c_q, c_kv, k_r = jnp.split(proj, cuts, axis=-1)
    shp = (bsz, seq, FOX_HEADS, FOX_HEAD_DIM)
    log_f = jax.nn.log_sigmoid((fg + b_forget).astype(jnp.float32))
    fox_out = _fox_attention(fq.reshape(shp), fk.reshape(shp), fv.reshape(shp), log_f)
    q = (_rms_norm(c_q, q_norm) @ w_uq).reshape(bsz, seq, MLA_HEADS, MLA_NOPE_DIM + MLA_ROPE_DIM)
    q_nope, q_rope = jnp.split(q, [MLA_NOPE_DIM], axis=-1)
    q_rope = _apply_rope(q_rope, cos[:, None, :], sin[:, None, :])
    kv = (_rms_norm(c_kv, kv_norm) @ w_ukv).reshape(bsz, seq, MLA_HEADS, MLA_NOPE_DIM + MLA_V_DIM)
    k_nope, v = jnp.split(kv, [MLA_NOPE_DIM], axis=-1)
    k_rope = _apply_rope(k_r, cos, sin)
    mla_out = _mla_attention(q_nope, q_rope, k_nope, k_rope, v)
    return jnp.concatenate([fox_out, mla_out], axis=-1) @ w_out


def _odd_mixer(h, w_in, lq1, lk1, lq2, lk2, subln, w_out, slopes, lambda_init):
    bsz, seq, _ = h.shape
    qk_w = DIFF_HEADS * 2 * DIFF_HEAD_DIM
    q, k, v = jnp.split(h @ w_in, [qk_w, 2 * qk_w], axis=-1)
    q = q.reshape(bsz, seq, DIFF_HEADS, 2, DIFF_HEAD_DIM)
    k = k.reshape(bsz, seq, DIFF_HEADS, 2, DIFF_HEAD_DIM)
    v = v.reshape(bsz, seq, DIFF_HEADS, DIFF_V_DIM)
    f32 = jnp.float32
    lam = (jnp.exp(jnp.sum(lq1.astype(f32) * lk1.astype(f32)))
           - jnp.exp(jnp.sum(lq2.astype(f32) * lk2.astype(f32))) + lambda_init)
    o = _diff_attention(q[..., 0, :], q[..., 1, :], k[..., 0, :], k[..., 1, :], v, lam, slopes)
    o = o.reshape(bsz, seq, DIFF_HEADS, DIFF_V_DIM)
    o = _rms_norm(o, subln) * (1.0 - lambda_init)
    return o.reshape(bsz, seq, ODD_MIX_WIDTH) @ w_out


def _causal_dwconv(a, w, b):
    seq = a.shape[1]
    ap = jnp.pad(a, ((0, 0), (CONV_WIDTH - 1, 0), (0, 0)))
    out = b
    for tap in range(CONV_WIDTH):
        out = out + ap[:, tap:tap + seq] * w[tap]
    return out


def _conv_ffn(h, w_up, conv_w, conv_b, w_down):
    gate, val = jnp.split(h @ w_up, [D_FF], axis=-1)
    act = jax.nn.gelu(_causal_dwconv(gate, conv_w, conv_b), approximate=True)
    return (act * val) @ w_down


def setup_inputs(seed: int = 0) -> dict:
    key = jax.random.key(seed)
    ks = jax.random.split(key, 23)
    f32 = jnp.float32

    def nrm(k, shape, scale):
        return jax.random.normal(k, shape, f32) * scale

    def gain(k, shape):
        return 1.0 + nrm(k, shape, 0.1)

    return {
        'x': nrm(ks[0], (BATCH, SEQ, D_MODEL), 1.0),
        'norm_mix_pre': gain(ks[1], (DEPTH, D_MODEL)),
        'norm_mix_post': gain(ks[2], (DEPTH, D_MODEL)),
        'norm_ffn_pre': gain(ks[3], (DEPTH, D_MODEL)),
        'norm_ffn_post': gain(ks[4], (DEPTH, D_MODEL)),
        'even_w_in': nrm(ks[5], (N_EVEN, D_MODEL, EVEN_IN_WIDTH), D_MODEL ** -0.5),
        'even_b_forget': FORGET_BIAS_MEAN + nrm(ks[6], (N_EVEN, FOX_HEADS), 0.5),
        'even_q_norm': gain(ks[7], (N_EVEN, MLA_Q_RANK)),
        'even_w_uq': nrm(ks[8], (N_EVEN, MLA_Q_RANK, MLA_HEADS * (MLA_NOPE_DIM + MLA_ROPE_DIM)), MLA_Q_RANK ** -0.5),
        'even_kv_norm': gain(ks[9], (N_EVEN, MLA_KV_RANK)),
        'even_w_ukv': nrm(ks[10], (N_EVEN, MLA_KV_RANK, MLA_HEADS * (MLA_NOPE_DIM + MLA_V_DIM)), MLA_KV_RANK ** -0.5),
        'even_w_out': nrm(ks[11], (N_EVEN, EVEN_MIX_WIDTH, D_MODEL), EVEN_MIX_WIDTH ** -0.5),
        'odd_w_in': nrm(ks[12], (N_ODD, D_MODEL, ODD_IN_WIDTH), D_MODEL ** -0.5),
        'odd_lambda_q1': nrm(ks[13], (N_ODD, DIFF_HEAD_DIM), 0.1),
        'odd_lambda_k1': nrm(ks[14], (N_ODD, DIFF_HEAD_DIM), 0.1),
        'odd_lambda_q2': nrm(ks[15], (N_ODD, DIFF_HEAD_DIM), 0.1),
        'odd_lambda_k2': nrm(ks[16], (N_ODD, DIFF_HEAD_DIM), 0.1),
        'odd_subln': gain(ks[17], (N_ODD, DIFF_V_DIM)),
        'odd_w_out': nrm(ks[18], (N_ODD, ODD_MIX_WIDTH, D_MODEL), ODD_MIX_WIDTH ** -0.5),
        'ffn_w_up': nrm(ks[19], (DEPTH, D_MODEL, 2 * D_FF), D_MODEL ** -0.5),
        'ffn_conv_w': nrm(ks[20], (DEPTH, CONV_WIDTH, D_FF), CONV_WIDTH ** -0.5),
        'ffn_conv_b': nrm(ks[21], (DEPTH, D_FF), 0.02),
        'ffn_w_down': nrm(ks[22], (DEPTH, D_FF, D_MODEL), D_FF ** -0.5),
    }


def reference(x, norm_mix_pre, norm_mix_post, norm_ffn_pre, norm_ffn_post,
              even_w_in, even_b_forget, even_q_norm, even_w_uq, even_kv_norm, even_w_ukv, even_w_out,
              odd_w_in, odd_lambda_q1, odd_lambda_k1, odd_lambda_q2, odd_lambda_k2, odd_subln, odd_w_out,
              ffn_w_up, ffn_conv_w, ffn_conv_b, ffn_w_down):
    cos, sin = _rope_tables(x.shape[1], MLA_ROPE_DIM, x.dtype)
    slopes = _alibi_slopes(DIFF_HEADS)
    for layer in range(DEPTH):
        i = layer // 2
        h = _rms_norm(x, norm_mix_pre[layer])
        if layer % 2 == 0:
            m = _even_mixer(h, even_w_in[i], even_b_forget[i], even_q_norm[i], even_w_uq[i],
                            even_kv_norm[i], even_w_ukv[i], even_w_out[i], cos, sin)
        else:
            lambda_init = 0.8 - 0.6 * math.exp(-0.3 * layer)
            m = _odd_mixer(h, odd_w_in[i], odd_lambda_q1[i], odd_lambda_k1[i], odd_lambda_q2[i],
                           odd_lambda_k2[i], odd_subln[i], odd_w_out[i], slopes, lambda_init)
        x = x + _rms_norm(m, norm_mix_post[layer])
        h = _rms_norm(x, norm_ffn_pre[layer])
        f = _conv_ffn(h, ffn_w_up[layer], ffn_conv_w[layer], ffn_conv_b[layer], ffn_w_down[layer])
        x = x + _rms_norm(f, norm_ffn_post[layer])
    return x
```

```python
import math
import numpy as np
import ml_dtypes
import concourse.bass as bass
import concourse.mybir as mybir
from concourse.bass_utils import run_bass_kernel_spmd

F32 = mybir.dt.float32
BF16 = mybir.dt.bfloat16
AF = mybir.ActivationFunctionType
ALU = mybir.AluOpType

D = 1024
DEPTH = 4
DFF = 2816
NJ = DFF // 128
EPS = 1e-6
TC = 512
ARENA_BYTES = 206 * 1024
NEG = -30000.0
EVEN_W = 2216
C_FQ, C_FK, C_FV, C_FG, C_CQ, C_CKV, C_KR, C_KRS = 0, 512, 1024, 1536, 1544, 1928, 2184, 2216
EVEN_WT = 2248


class Res:
    __slots__ = ("name", "writers", "readers")

    def __init__(self, name):
        self.name = name
        self.writers = []
        self.readers = []


class Op:
    __slots__ = ("eng", "fn", "deps", "dma", "semkey")

    def __init__(self, eng, fn, deps, dma, semkey):
        self.eng = eng
        self.fn = fn
        self.deps = deps
        self.dma = dma
        self.semkey = semkey


class Prog:
    def __init__(self, nc):
        self.nc = nc
        self.ops = []
        self.res = {}
        self.last_eng = {}
        self.last_dma = {}
        self.engobj = {"pe": nc.tensor, "act": nc.scalar, "dve": nc.vector, "pool": nc.gpsimd, "sp": nc.sync}

    def R(self, *key):
        r = self.res.get(key)
        if r is None:
            r = Res(key)
            self.res[key] = r
        return r

    def add(self, eng, insts, reads=(), writes=(), dma=False, semkey=None):
        if isinstance(insts, tuple):
            insts = [insts]
        i = len(self.ops)
        deps = {}
        for r in reads:
            for w in r.writers:
                deps[w] = "raw"
        for r in writes:
            for x in r.readers:
                deps.setdefault(x, "war")
        fdeps = []
        for d, kind in deps.items():
            od = self.ops[d]
            if not dma and not od.dma and od.eng == eng:
                if eng == "pe" or kind != "raw":
                    continue
            fdeps.append(d)

        def push(lst):
            if lst and not dma:
                o = self.ops[lst[-1]]
                if not o.dma and o.eng == eng:
                    lst[-1] = i
                    return
            lst.append(i)

        for r in writes:
            if r.readers or (r in reads):
                r.writers = [i]
                r.readers = []
            else:
                push(r.writers)
        for r in reads:
            if r not in writes:
                push(r.readers)
        self.ops.append(Op(eng, insts, fdeps, dma, semkey))
        if dma:
            self.last_dma[semkey] = i
        else:
            self.last_eng[eng] = i
        return i

    def dma(self, queue, out, in_, reads, writes, semkey):
        return self.add(queue, [("dma_start", dict(out=out, in_=in_))], reads, writes, dma=True, semkey=semkey)

    def barrier(self):
        lasts = list(self.last_eng.values()) + list(self.last_dma.values())
        for eng in ("pe", "act", "dve", "pool", "sp"):
            i = len(self.ops)
            deps = [d for d in lasts if not (not self.ops[d].dma and self.ops[d].eng == eng)]
            self.ops.append(Op(eng, None, deps, False, None))
        for r in self.res.values():
            r.writers = []
            r.readers = []
        self.last_dma = {}

    def lower(self):
        nc = self.nc
        n = len(self.ops)
        needed = [False] * n
        for op in self.ops:
            for d in op.deps:
                needed[d] = True
        sems = {}
        cnt = {}

        def getsem(key):
            s = sems.get(key)
            if s is None:
                s = nc.alloc_semaphore("s%d" % len(sems))
                sems[key] = s
                cnt[key] = 0
            return s

        sig = [None] * n
        waited = {}
        nwait = 0
        for i, op in enumerate(self.ops):
            e = self.engobj[op.eng]
            reqs = {}
            for d in op.deps:
                k, v = sig[d]
                if reqs.get(k, 0) < v:
                    reqs[k] = v
            for k, v in reqs.items():
                if waited.get((op.eng, k), 0) >= v:
                    continue
                e.wait_ge(sems[k], v)
                nwait += 1
                waited[(op.eng, k)] = v
            if op.fn is None:
                continue
            ins = None
            for (nm, kw) in op.fn:
                ins = getattr(e, nm)(**kw)
            if needed[i] or op.dma:
                if op.dma:
                    key = ("dma", op.semkey)
                    s = getsem(key)
                    cnt[key] += 16
                    ins.then_inc(s, 16)
                else:
                    key = ("eng", op.eng)
                    s = getsem(key)
                    cnt[key] += 1
                    ins.then_inc(s, 1)
                sig[i] = (key, cnt[key])
        self.stats = dict(nops=n, nwait=nwait, nsem=len(sems))


class Arena:
    def __init__(self, nc, nbytes):
        self.t = nc.alloc_sbuf_tensor("arena", [128, nbytes // 4], F32)
        self.cap = nbytes
        self.off = 0
        self.top = nbytes

    def alloc(self, nelem, dtype, top=False):
        nb = nelem * (4 if dtype == F32 else 2)
        nb = (nb + 63) // 64 * 64
        if top:
            self.top -= nb
            o = self.top
        else:
            o = self.off
            self.off += nb
        assert self.off <= self.top, ("SBUF arena overflow", self.off, self.top)
        ap = self.t[:, o // 4:(o + nb) // 4]
        if dtype != F32:
            ap = ap.bitcast(dtype)
        return ap[:, :nelem]


def _bf16_split(a, n):
    outs = []
    r = a.astype(np.float64)
    for _ in range(n):
        h = r.astype(np.float32).astype(ml_dtypes.bfloat16)
        outs.append(h)
        r = r - h.astype(np.float64)
    return outs


def make_consts(S):
    NT = S // 128
    c = {}
    k = np.arange(128)
    cb = np.zeros((128, 384), np.float32)
    cb[:, 0:128] = 1.0
    cb[:, 128:256] = np.eye(128, dtype=np.float32)
    cb[:, 256:384] = np.where(k[:, None] <= k[None, :], 0.0, NEG)
    c["cbf"] = cb.astype(ml_dtypes.bfloat16)
    slopes = 2.0 ** (-8.0 * np.arange(1, 9, dtype=np.float64) / 8)
    cf = np.zeros((128, 384 + 8 * NT), np.float32)
    cf[:, 0:128] = (k[:, None] <= k[None, :]).astype(np.float32)
    cf[:, 128:256] = 1.0
    cf[:, 256:384] = np.eye(128, dtype=np.float32)
    pos = (np.arange(NT)[None, :] * 128 + k[:, None]).astype(np.float64)
    for h in range(8):
        cf[:, 384 + h * NT:384 + (h + 1) * NT] = (slopes[h] * pos).astype(np.float32)
    c["cf"] = cf
    inv = (10000.0 ** (-np.arange(0, 32, 2, dtype=np.float32) / 32)).astype(np.float32)
    ang = (np.arange(S, dtype=np.float32)[:, None] * inv[None, :]).astype(np.float32)
    cos = np.cos(ang).astype(np.float32).T
    sin = np.sin(ang).astype(np.float32).T
    cs = np.concatenate([cos, cos], 0)
    ss = np.concatenate([-sin, sin], 0)
    rope = np.zeros((2, 128, S), np.float32)
    rope[0] = np.tile(cs, (4, 1))
    rope[1] = np.tile(ss, (4, 1))
    c["rope"] = rope
    t = np.arange(S, dtype=np.float64)
    aq = np.zeros((8, 2, S), ml_dtypes.bfloat16)
    for h in range(8):
        hi, lo = _bf16_split(-slopes[h] * t, 2)
        aq[h, 0] = hi
        aq[h, 1] = lo
    c["aq"] = aq
    ak = np.zeros((8, 2, S), ml_dtypes.bfloat16)
    for h in range(8):
        hi, lo = _bf16_split(slopes[h] * t, 2)
        ak[h, 0] = hi
        ak[h, 1] = lo
    c["ak"] = ak
    c["onesrow"] = np.ones((3, S), ml_dtypes.bfloat16)
    return c


V_NMP, V_NMPOST, V_NFP, V_NFPOST = 0, 32, 64, 96
V_QN = 128
V_KVN = 134
V_SUBLN = 138
V_CONVW = 140
V_CONVB = 404
V_BF = 492
V_LAM = 556
NVEC = 1068


def pack_vecs(inp):
    v = np.zeros((128, NVEC), np.float32)

    def pl(a):
        return np.asarray(a, np.float32).reshape(8, 128).T

    for l in range(DEPTH):
        v[:, V_NMP + 8 * l:V_NMP + 8 * l + 8] = pl(inp["norm_mix_pre"][l])
        v[:, V_NMPOST + 8 * l:V_NMPOST + 8 * l + 8] = pl(inp["norm_mix_post"][l])
        v[:, V_NFP + 8 * l:V_NFP + 8 * l + 8] = pl(inp["norm_ffn_pre"][l])
        v[:, V_NFPOST + 8 * l:V_NFPOST + 8 * l + 8] = pl(inp["norm_ffn_post"][l])
        cw = np.asarray(inp["ffn_conv_w"][l], np.float32)
        for tap in range(3):
            v[:, V_CONVW + (l * 3 + tap) * NJ:V_CONVW + (l * 3 + tap + 1) * NJ] = cw[tap].reshape(NJ, 128).T
        v[:, V_CONVB + l * NJ:V_CONVB + (l + 1) * NJ] = np.asarray(inp["ffn_conv_b"][l], np.float32).reshape(NJ, 128).T
    for i in range(2):
        v[:, V_QN + 3 * i:V_QN + 3 * i + 3] = np.asarray(inp["even_q_norm"][i], np.float32).reshape(3, 128).T
        v[:, V_KVN + 2 * i:V_KVN + 2 * i + 2] = np.asarray(inp["even_kv_norm"][i], np.float32).reshape(2, 128).T
        v[:, V_SUBLN + i] = np.asarray(inp["odd_subln"][i], np.float32)
        v[:, V_BF + 32 * i:V_BF + 32 * i + 32] = np.tile(np.asarray(inp["even_b_forget"][i], np.float32), 4)[None, :]
        for j, nm in enumerate(["odd_lambda_q1", "odd_lambda_k1", "odd_lambda_q2", "odd_lambda_k2"]):
            o = V_LAM + (i * 4 + j) * 64
            v[:, o:o + 64] = np.asarray(inp[nm][i], np.float32)[None, :]
    return v


def ACT(out, in_, func, **kw):
    return ("activation", dict(out=out, in_=in_, func=func, **kw))


def MM(out, lhsT, rhs, start=True, stop=True):
    return ("matmul", dict(out=out, lhsT=lhsT, rhs=rhs, start=start, stop=stop))


def TT(out, in0, in1, op):
    return ("tensor_tensor", dict(out=out, in0=in0, in1=in1, op=op))


def TS(out, in0, s1, s2, op0, op1=None):
    kw = dict(out=out, in0=in0, scalar1=s1, scalar2=s2, op0=op0)
    if op1 is not None:
        kw["op1"] = op1
    return ("tensor_scalar", kw)


def STT(out, in0, scalar, in1, op0, op1):
    return ("scalar_tensor_tensor", dict(out=out, in0=in0, scalar=scalar, in1=in1, op0=op0, op1=op1))


def CP(out, in_):
    return ("tensor_copy", dict(out=out, in_=in_))


def MSET(ap, v):
    return ("memset", dict(ap=ap, constant=v))


def RCP(out, in_):
    return ("reciprocal", dict(out=out, in_=in_))


class Builder:
    def __init__(self, S, depth=DEPTH, dbg=()):
        self.S = S
        self.NT = S // 128
        self.NCH = S // TC
        self.depth = depth
        nc = self.nc = bass.Bass("TRN2", target_bir_lowering=False)
        self.P = Prog(nc)
        self.dbg = set(dbg)
        NT = self.NT

        def din(name, shape, dt):
            return nc.dram_tensor(name, list(shape), dt, kind="ExternalInput").ap()

        def dscr(name, shape, dt):
            kind = "ExternalOutput" if name in self.dbg else "Internal"
            return nc.dram_tensor(name, list(shape), dt, kind=kind).ap()

        self.xT_in = din("xT", [8, 128, S], F32)
        self.vecs_in = din("vecs", [128, NVEC], F32)
        self.cbf_in = din("cbf", [128, 384], BF16)
        self.cf_in = din("cf", [128, 384 + 8 * NT], F32)
        self.rope_in = din("rope", [2, 128, S], F32)
        self.aq_in = din("aq", [8, 2, S], BF16)
        self.ak_in = din("ak", [8, 2, S], BF16)
        self.onesrow_in = din("onesrow", [3, S], BF16)
        self.w = {}
        self.w["even_w_in"] = din("even_w_in", [2, 1024, EVEN_W], F32)
        self.w["even_w_uq"] = din("even_w_uq", [2, 384, 768], F32)
        self.w["even_w_ukv"] = din("even_w_ukv", [2, 256, 1024], F32)
        self.w["even_w_out"] = din("even_w_out", [2, 1024, 1024], F32)
        self.w["odd_w_in"] = din("odd_w_in", [2, 1024, 3072], F32)
        self.w["odd_w_out"] = din("odd_w_out", [2, 1024, 1024], F32)
        self.w["ffn_w_up"] = din("ffn_w_up", [4, 1024, 2 * DFF], F32)
        self.w["ffn_w_down"] = din("ffn_w_down", [4, DFF, 1024], F32)
        self.out = nc.dram_tensor("outT", [8, 128, S], F32, kind="ExternalOutput").ap()
        self.XT = dscr("XT", [8, 128, S], F32)
        self.AT = dscr("AT", [8, 128, S], BF16)
        self.ACTT = dscr("ACTT", [NJ, 128, S], BF16)
        self.Qf = dscr("Qf", [512, S], BF16)
        self.Kf = dscr("Kf", [512, S], BF16)
        self.Vf = dscr("Vf", [S, 512], BF16)
        self.CqR = dscr("CqR", [8, 3, S], BF16)
        self.CkR = dscr("CkR", [8, 3, S], BF16)
        self.Qm = dscr("Qm", [8, 96, S], BF16)
        self.Kmn = dscr("Kmn", [512, S], BF16)
        self.Kr = dscr("Kr", [32, S], BF16)
        self.Vm = dscr("Vm", [S, 512], BF16)
        self.Qd = dscr("Qd", [1024, S], BF16)
        self.Kd = dscr("Kd", [1024, S], BF16)
        self.Vd = dscr("Vd", [S, 1024], BF16)

        self.arena = Arena(nc, ARENA_BYTES)
        A = self.arena
        self.vecs = A.alloc(NVEC, F32)
        self.cbf = A.alloc(384, BF16)
        self.cf = A.alloc(384 + 8 * NT, F32)
        self.cpos = A.alloc(NT * 8, F32)
        self.lamt = A.alloc(16, F32)
        self.epsb = A.alloc(2, F32)[:, 0:1]
        self.ones_bf = self.cbf[:, 0:128]
        self.ident_bf = self.cbf[:, 128:256]
        self.maskT = self.cbf[:, 256:384]
        self.U_f = self.cf[:, 0:128]
        self.ones_f = self.cf[:, 128:256]
        self.ident_f = self.cf[:, 256:384]
        self.phase_base = A.off
        self.ps = [nc.alloc_psum_tensor("ps%d" % i, [128, 512], F32) for i in range(8)]
        self.psr = [self.P.R("ps", i) for i in range(8)]
        self.ps_rrs = {}

    def phase_reset(self):
        self.P.barrier()
        self.arena.off = self.phase_base

    def next_ps(self, pool):
        key = tuple(pool)
        n = self.ps_rrs.get(key, 0)
        self.ps_rrs[key] = n + 1
        return pool[n % len(pool)]

    def load_weight(self, dst, src, K, ncols, res, c0=0, dst_c0=0, after=()):
        for kc in range(K // 128):
            for s0 in range(0, ncols, 2048):
                s1 = min(ncols, s0 + 2048)
                self.P.dma("pool", dst[:, kc, dst_c0 + s0:dst_c0 + s1], src[kc * 128:(kc + 1) * 128, c0 + s0:c0 + s1],
                           list(after) if (kc == 0 and s0 == 0) else [], [res], semkey=res.name)

    def prefetch_T1(self, l, want_wout, after=()):
        pf = getattr(self, "pf", None)
        if pf is None or pf["l"] != l:
            pf = self.pf = {"l": l, "wout": None, "wup": None}
        A, P = self.arena, self.P
        i = l // 2
        if pf["wup"] is None:
            pf["wup"] = A.alloc(8 * 2 * DFF, BF16, top=True).rearrange("p (k c) -> p k c", k=8)
            self.load_weight(pf["wup"], self.w["ffn_w_up"][l], 1024, 2 * DFF, P.R("wup", l), after=after)
        if want_wout and pf["wout"] is None:
            pf["wout"] = A.alloc(8 * 1024, BF16, top=True).rearrange("p (k c) -> p k c", k=8)
            self.load_weight(pf["wout"], self.w["even_w_out" if l % 2 == 0 else "odd_w_out"][i], 1024, 1024, P.R("wout", l))
        return pf

    def init_globals(self):
        P = self.P
        rg = self.rg = P.R("globals")
        P.dma("sp", self.vecs, self.vecs_in, [], [rg], semkey="g")
        P.dma("sp", self.cbf, self.cbf_in, [], [rg], semkey="g")
        P.dma("sp", self.cf, self.cf_in, [], [rg], semkey="g")
        P.add("dve", MSET(self.epsb, EPS), [], [rg])

    @staticmethod
    def merge_pairs(steps):
        out = []
        for q in range(0, len(steps), 2):
            grp = steps[q:q + 2]
            out.append(lambda grp=grp: [g() for g in grp])
        return out

    @staticmethod
    def spaced(steps, gaps):
        out = []
        for q, stp in enumerate(steps):
            out += [(lambda: None)] * gaps.get(q, 0)
            out.append(stp)
        return out

    def tick(self):
        if self.pending:
            self.pending.pop(0)()

    def flush(self):
        while self.pending:
            self.pending.pop(0)()

    def alloc_tok_common(self, nx=1, nhb=1):
        A = self.arena
        P = self.P
        self.pending = []
        self.xss = [(A.alloc(8 * 512, F32).rearrange("p (k c) -> p k c", k=8), P.R("xs", q)) for q in range(nx)]
        self.hbs = [(A.alloc(8 * 512, BF16).rearrange("p (k c) -> p k c", k=8), P.R("hb", q)) for q in range(nhb)]
        self.sq = A.alloc(8 * 512, BF16).rearrange("p (k c) -> p k c", k=8)
        self.rsq = P.R("sq")
        self.rstd = A.alloc(512, F32)
        self.rrstd = P.R("rstd")

    def stats_steps(self, src, nk, n, r_src, ps_i=4):
        P = self.P
        ps, psr = self.ps[ps_i], self.psr[ps_i]
        sq, rsq, rstd, rr = self.sq, self.rsq, self.rstd, self.rrstd
        steps = []
        h = (nk + 1) // 2
        for (k0, k1) in ((0, h), (h, nk)):
            if k1 > k0:
                steps.append(lambda k0=k0, k1=k1: P.add("act", ACT(sq[:, k0:k1, :], src[:, k0:k1, :], AF.Square), [r_src], [rsq]))

        def fin():
            P.add("pe", [MM(ps[:], self.ones_bf, sq[:, kc, :], kc == 0, kc == nk - 1) for kc in range(nk)], [rsq, self.rg], [psr])
            P.add("act", ACT(rstd, ps[:], AF.Ln, scale=1.0 / n, bias=self.epsb), [psr, self.rg], [rr])
            P.add("act", ACT(rstd, rstd, AF.Exp, scale=-0.5), [rr], [rr])
        steps.append(fin)
        return steps

    def stats_rstd(self, src, nk, n, r_src, ps_i=4):
        for st in self.stats_steps(src, nk, n, r_src, ps_i):
            st()

    def prenorm_steps(self, gcol0, xs, rx, hb, rhb):
        steps = self.stats_steps(xs, 8, 1024.0, rx)
        for half in range(2):
            def dv(half=half):
                for kc in range(half * 4, half * 4 + 4):
                    g = self.vecs[:, gcol0 + kc:gcol0 + kc + 1]
                    self.P.add("dve", STT(hb[:, kc, :], xs[:, kc, :], g, self.rstd, ALU.mult, ALU.mult),
                               [rx, self.rrstd, self.rg], [rhb])
            steps.append(dv)
        return steps

    def residual_steps(self, m, rm, gcol0, xs, rx, add_eng="pool"):
        P = self.P
        steps = self.stats_steps(m, 8, 1024.0, rm)
        for half in range(2):
            def dv(half=half):
                for kc in range(half * 4, half * 4 + 4):
                    g = self.vecs[:, gcol0 + kc:gcol0 + kc + 1]
                    P.add("dve", STT(m[:, kc, :], m[:, kc, :], g, self.rstd, ALU.mult, ALU.mult), [rm, self.rrstd, self.rg], [rm])
                sl = slice(half * 4, half * 4 + 4)
                P.add(add_eng, TT(xs[:, sl, :], xs[:, sl, :], m[:, sl, :], ALU.add), [rx, rm], [rx])
            steps.append(dv)
        return steps

    def load_x(self, c, src, rsrc, xs, rx, key="xs"):
        cs = slice(c * TC, (c + 1) * TC)
        self.P.dma("sp", xs, src[:, :, cs].rearrange("k p s -> p k s"), [rsrc], [rx], semkey=(key, rx.name))

    def store_x(self, c, dst, rdst, xs, rx):
        cs = slice(c * TC, (c + 1) * TC)
        self.P.dma("sp", dst[:, :, cs].rearrange("k p s -> p k s"), xs, [rx], [rdst], semkey=("xs_st", rx.name))

    def alloc_inproj(self, l, fbuf=None, rf=None):
        A = self.arena
        P = self.P
        st = {}
        if l % 2 == 0:
            st["win"] = A.alloc(8 * EVEN_WT, BF16).rearrange("p (k c) -> p k c", k=8)
            st["wuq"] = A.alloc(3 * 1024, BF16).rearrange("p (k c) -> p k c", k=3)
            st["wukv"] = A.alloc(2 * 1024, BF16).rearrange("p (k c) -> p k c", k=2)
            aliased = fbuf is not None
            if fbuf is None:
                fbuf = A.alloc(8 * 512, F32).rearrange("p (k c) -> p k c", k=8)
            st["cqs"] = fbuf[:, 0:3, :]
            st["ckvs"] = fbuf[:, 3:5, :]
            st["ropetmp"] = [fbuf[:, 5, :], fbuf[:, 6, :]]
            st["rf"] = rf if aliased else None
            st["r_cqs"], st["r_ckvs"], st["r_tmp"] = P.R("cqs", l), P.R("ckvs", l), P.R("ropetmp", l)
            st["cqn"] = A.alloc(3 * 512, BF16).rearrange("p (k c) -> p k c", k=3)
            st["ckvn"] = A.alloc(2 * 512, BF16).rearrange("p (k c) -> p k c", k=2)
            st["rope"] = A.alloc(2 * 512, F32).rearrange("p (k c) -> p k c", k=2)
            st["fg"] = A.alloc(32, F32)
            st["lf"] = A.alloc(32, F32)
            st["run"] = A.alloc(8, F32)
            st["crow"] = A.alloc(512, F32)
            st["crow2"] = A.alloc(512, F32)
            st["crs"] = [A.alloc(512, BF16) for _ in range(3)]
            st["crsp"] = [A.alloc(512, BF16) for _ in range(3)]
        else:
            st["win"] = A.alloc(8 * 3072, BF16).rearrange("p (k c) -> p k c", k=8)
        st["stage"] = [A.alloc(512, BF16) for _ in range(4)]
        st["stage_i"] = 0
        st["rw"] = P.R("w_inproj", l)
        st["id"] = l
        return st

    def load_inproj_weights(self, l, st):
        i = l // 2
        P = self.P
        rw = st["rw"]
        key = rw.name
        if l % 2 == 0:
            w = self.w["even_w_in"][i]
            self.load_weight(st["win"], w, 1024, EVEN_W, rw)
            for kc in range(8):
                P.dma("pool", st["win"][:, kc, C_KRS:C_KRS + 16], w[kc * 128:(kc + 1) * 128, C_KR + 16:C_KR + 32], [], [rw], semkey=key)
                P.dma("pool", st["win"][:, kc, C_KRS + 16:C_KRS + 32], w[kc * 128:(kc + 1) * 128, C_KR:C_KR + 16], [], [rw], semkey=key)
            wq = self.w["even_w_uq"][i].rearrange("r (h d) -> r h d", h=8)
            for kc in range(3):
                src = wq[kc * 128:(kc + 1) * 128]
                dq = st["wuq"][:, kc, :]
                P.dma("pool", dq[:, 0:512].rearrange("p (h d) -> p h d", h=8), src[:, :, 0:64], [], [rw], semkey=key)
                P.dma("pool", dq[:, 512:768].rearrange("p (h d) -> p h d", h=8), src[:, :, 64:96], [], [rw], semkey=key)
                d2 = dq[:, 768:1024].rearrange("p (h d) -> p h d", h=8)
                P.dma("pool", d2[:, :, 0:16], src[:, :, 80:96], [], [rw], semkey=key)
                P.dma("pool", d2[:, :, 16:32], src[:, :, 64:80], [], [rw], semkey=key)
            wkv = self.w["even_w_ukv"][i].rearrange("r (h d) -> r h d", h=8)
            for kc in range(2):
                src = wkv[kc * 128:(kc + 1) * 128]
                dk = st["wukv"][:, kc, :]
                P.dma("pool", dk[:, 0:512].rearrange("p (h d) -> p h d", h=8), src[:, :, 0:64], [], [rw], semkey=key)
                P.dma("pool", dk[:, 512:1024].rearrange("p (h d) -> p h d", h=8), src[:, :, 64:128], [], [rw], semkey=key)
        else:
            self.load_weight(st["win"], self.w["odd_w_in"][i], 1024, 3072, rw)

    def evac_store(self, st, ps_i, rows, dst, scale=1.0, eng="act", src=None, rsrc=None):
        P = self.P
        k = st["stage_i"] % len(st["stage"])
        st["stage_i"] += 1
        stg = st["stage"][k]
        rs = P.R("stage", st["id"], k)
        if src is None:
            src, rsrc = self.ps[ps_i], self.psr[ps_i]
        if eng == "act":
            P.add("act", ACT(stg[0:rows, :], src[0:rows, :], AF.Copy, scale=scale), [rsrc], [rs])
        else:
            P.add("dve", TS(stg[0:rows, :], src[0:rows, :], scale, None, ALU.mult), [rsrc], [rs])
        for (r0, r1, dap, dres) in dst:
            P.dma("pool", dap, stg[r0:r1, :], [rs], [dres], semkey=("stage", st["id"], k))
        self.tick()

    def inproj_chunk(self, l, c, st, h, rh):
        P = self.P
        cs = slice(c * TC, (c + 1) * TC)
        rw = st["rw"]
        win = st["win"]
        PSP = [0, 1, 2, 3]
        ev = [0]

        def eng():
            ev[0] += 1
            return "act" if ev[0] % 2 else "dve"

        def proj_fm(col0, m, ps_i, w=win, nk=8, src=h, rsrc=rh):
            P.add("pe", [MM(self.ps[ps_i][0:m, :], w[:, kc, col0:col0 + m], src[:, kc, :], kc == 0, kc == nk - 1) for kc in range(nk)],
                  [rw, rsrc], [self.psr[ps_i]])

        def proj_tm(tt, col0, ncol, ps_i, w=win, nk=8, src=h, rsrc=rh):
            P.add("pe", [MM(self.ps[ps_i][:, 0:ncol], src[:, kc, tt * 128:(tt + 1) * 128], w[:, kc, col0:col0 + ncol], kc == 0, kc == nk - 1) for kc in range(nk)],
                  [rw, rsrc], [self.psr[ps_i]])

        t0 = c * TC
        if l % 2 == 1:
            rq = P.R("Qd", l, c)
            for j in range(8):
                pi = self.next_ps(PSP)
                proj_fm(j * 128, 128, pi)
                self.evac_store(st, pi, 128, [(0, 128, self.Qd[j * 128:(j + 1) * 128, cs], rq)], scale=0.125, eng=eng())
            for j in range(8):
                pi = self.next_ps(PSP)
                proj_fm(1024 + j * 128, 128, pi)
                self.evac_store(st, pi, 128, [(0, 128, self.Kd[j * 128:(j + 1) * 128, cs], rq)], eng=eng())
            for tt in range(4):
                for hf in range(2):
                    pi = self.next_ps(PSP)
                    proj_tm(tt, 2048 + hf * 512, 512, pi)
                    self.evac_store(st, pi, 128, [(0, 128, self.Vd[t0 + tt * 128:t0 + (tt + 1) * 128, hf * 512:(hf + 1) * 512], rq)], eng=eng())
            return
        i = l // 2
        rq = P.R("Qe", l, c)
        rfg, rlf, rrun, rcpos = P.R("fg", l), P.R("lf", l), P.R("run", l), P.R("cpos")
        fg, lf, run = st["fg"], st["lf"], st["run"]
        cqs, cqn, ckvs, ckvn = st["cqs"], st["cqn"], st["ckvs"], st["ckvn"]
        rcqs, rckvs, rtmp = st["r_cqs"], st["r_ckvs"], st["r_tmp"]
        al = [st["rf"]] if st["rf"] is not None else []
        rcqn, rckvn = P.R("cqn", l), P.R("ckvn", l)
        rrope = P.R("rope", l)
        rope = st["rope"]
        wuq, wukv = st["wuq"], st["wukv"]
        qscale = 96.0 ** -0.5
        cpos3 = self.cpos.rearrange("p (t h) -> p t h", h=8)
        crow, crow2, crs, crsp = st["crow"], st["crow2"], st["crs"], st["crsp"]
        rcr = P.R("crow", l)
        rc = [P.R("crs", l, k) for k in range(3)]
        rcp = [P.R("crsp", l, k) for k in range(3)]
        pcs = {}

        def b_fq(j):
            pi = self.next_ps(PSP)
            proj_fm(C_FQ + j * 128, 128, pi)
            self.evac_store(st, pi, 128, [(0, 128, self.Qf[j * 128:(j + 1) * 128, cs], rq)], scale=0.125, eng=eng())

        def b_fk(j):
            pi = self.next_ps(PSP)
            proj_fm(C_FK + j * 128, 128, pi)
            self.evac_store(st, pi, 128, [(0, 128, self.Kf[j * 128:(j + 1) * 128, cs], rq)], eng=eng())

        def b_fv(tt):
            pi = self.next_ps(PSP)
            proj_tm(tt, C_FV, 512, pi)
            self.evac_store(st, pi, 128, [(0, 128, self.Vf[t0 + tt * 128:t0 + (tt + 1) * 128, :], rq)], eng=eng())

        def b_fg():
            pi = self.next_ps(PSP)
            P.add("pe", [MM(self.ps[pi][:, tt * 8:(tt + 1) * 8], h[:, kc, tt * 128:(tt + 1) * 128], win[:, kc, C_FG:C_FG + 8], kc == 0, kc == 7)
                         for tt in range(4) for kc in range(8)], [rw, rh], [self.psr[pi]])
            bfv = self.vecs[:, V_BF + 32 * i:V_BF + 32 * i + 32]
            P.add("dve", TT(fg, self.ps[pi][:, 0:32], bfv, ALU.add), [self.psr[pi], self.rg], [rfg])
            P.add("act", ACT(fg, fg, AF.Exp, scale=-1.0), [rfg], [rfg])
            P.add("act", ACT(lf, fg, AF.Ln, bias=1.0), [rfg], [rlf])
            if c == 0:
                P.add("dve", MSET(run, 0.0), [], [rrun])
            pcs["pc"] = self.next_ps([5, 6])

        def b_cum(tt):
            pc = pcs["pc"]
            P.add("pe", [MM(self.ps[pc][:, tt * 8:(tt + 1) * 8], self.U_f, lf[:, tt * 8:(tt + 1) * 8], True, False),
                         MM(self.ps[pc][:, tt * 8:(tt + 1) * 8], self.ones_f, run, False, True)], [rlf, rrun, self.rg], [self.psr[pc]])
            P.add("dve", TT(run, run, lf[:, tt * 8:(tt + 1) * 8], ALU.add), [rrun, rlf], [rrun])

        def b_cpos():
            pc = pcs["pc"]
            P.add("act", ACT(self.cpos[:, c * 32:(c + 1) * 32], self.ps[pc][:, 0:32], AF.Copy), [self.psr[pc]], [rcpos])

        def b_crow():
            pr = self.next_ps([5, 6])
            P.add("pe", [MM(self.ps[pr][0:8, tt * 128:(tt + 1) * 128], cpos3[:, c * 4 + tt, :], self.ident_f, True, True) for tt in range(4)],
                  [rcpos, self.rg], [self.psr[pr]])
            P.add("act", ACT(crow[0:8, :], self.ps[pr][0:8, :], AF.Copy, scale=-1.0), [self.psr[pr]], [rcr])
            P.add("dve", CP(crs[0][0:8, :], crow[0:8, :]), [rcr], [rc[0]])
            P.add("dve", TT(crow2[0:8, :], crow[0:8, :], crs[0][0:8, :], ALU.subtract), [rcr, rc[0]], [rcr])
            P.add("dve", CP(crs[1][0:8, :], crow2[0:8, :]), [rcr], [rc[1]])
            P.add("dve", TT(crow[0:8, :], crow2[0:8, :], crs[1][0:8, :], ALU.subtract), [rcr, rc[1]], [rcr])
            P.add("dve", CP(crs[2][0:8, :], crow[0:8, :]), [rcr], [rc[2]])
            for k3 in range(3):
                P.add("dve", TS(crsp[k3][0:8, :], crs[k3][0:8, :], -1.0, None, ALU.mult), [rc[k3]], [rcp[k3]])
                P.dma("pool", self.CqR[:, k3, cs], crs[k3][0:8, :], [rc[k3]], [rq], semkey=("crs", l, k3))
                P.dma("pool", self.CkR[:, k3, cs], crsp[k3][0:8, :], [rcp[k3]], [rq], semkey=("crsp", l, k3))

        def b_ropeload():
            P.dma("sp", rope[:, 0, :], self.rope_in[0, :, cs], [], [rrope], semkey=("rope", l))
            P.dma("sp", rope[:, 1, :], self.rope_in[1, :, cs], [], [rrope], semkey=("rope", l))

        def rope_combine(pa, pb, rows, dst, scale):
            ta, tb = st["ropetmp"]
            P.add("dve", TT(ta[0:rows, :], self.ps[pa][0:rows, :], rope[0:rows, 0, :], ALU.mult), [self.psr[pa], rrope], [rtmp] + al)
            P.add("dve", TT(tb[0:rows, :], self.ps[pb][0:rows, :], rope[0:rows, 1, :], ALU.mult), [self.psr[pb], rrope], [rtmp] + al)
            P.add("dve", TT(ta[0:rows, :], ta[0:rows, :], tb[0:rows, :], ALU.add), [rtmp], [rtmp])
            self.evac_store(st, None, rows, dst, scale=scale, eng="act", src=ta, rsrc=rtmp)

        def b_kr():
            pa = self.next_ps(PSP)
            proj_fm(C_KR, 32, pa)
            pb = self.next_ps(PSP)
            proj_fm(C_KRS, 32, pb)
            rope_combine(pa, pb, 32, [(0, 32, self.Kr[:, cs], rq)], 1.0)

        def b_cq(kc):
            pi = self.next_ps(PSP)
            proj_fm(C_CQ + kc * 128, 128, pi)
            P.add("act", ACT(cqs[:, kc, :], self.ps[pi][:], AF.Copy), [self.psr[pi]], [rcqs] + al)

        def b_ckv(kc):
            pi = self.next_ps(PSP)
            proj_fm(C_CKV + kc * 128, 128, pi)
            P.add("act", ACT(ckvs[:, kc, :], self.ps[pi][:], AF.Copy), [self.psr[pi]], [rckvs] + al)

        st_cq = self.stats_steps(cqs, 3, 384.0, rcqs)
        st_ckv = self.stats_steps(ckvs, 2, 256.0, rckvs)

        def b_cqn():
            for kc in range(3):
                g = self.vecs[:, V_QN + 3 * i + kc:V_QN + 3 * i + kc + 1]
                P.add("dve", STT(cqn[:, kc, :], cqs[:, kc, :], g, self.rstd, ALU.mult, ALU.mult), [rcqs, self.rrstd, self.rg], [rcqn])

        def b_ckvn():
            for kc in range(2):
                g = self.vecs[:, V_KVN + 2 * i + kc:V_KVN + 2 * i + kc + 1]
                P.add("dve", STT(ckvn[:, kc, :], ckvs[:, kc, :], g, self.rstd, ALU.mult, ALU.mult), [rckvs, self.rrstd, self.rg], [rckvn])

        def b_qn(j):
            pi = self.next_ps(PSP)
            proj_fm(j * 128, 128, pi, w=wuq, nk=3, src=cqn, rsrc=rcqn)
            self.evac_store(st, pi, 128, [(0, 64, self.Qm[2 * j, 0:64, cs], rq), (64, 128, self.Qm[2 * j + 1, 0:64, cs], rq)], scale=qscale, eng=eng())

        def b_qr(j):
            pa = self.next_ps(PSP)
            proj_fm(512 + j * 128, 128, pa, w=wuq, nk=3, src=cqn, rsrc=rcqn)
            pb = self.next_ps(PSP)
            proj_fm(768 + j * 128, 128, pb, w=wuq, nk=3, src=cqn, rsrc=rcqn)
            rope_combine(pa, pb, 128, [(32 * q, 32 * q + 32, self.Qm[4 * j + q, 64:96, cs], rq) for q in range(4)], qscale)

        def b_kn(j):
            pi = self.next_ps(PSP)
            proj_fm(j * 128, 128, pi, w=wukv, nk=2, src=ckvn, rsrc=rckvn)
            self.evac_store(st, pi, 128, [(0, 128, self.Kmn[j * 128:(j + 1) * 128, cs], rq)], eng=eng())

        def b_v(tt):
            pi = self.next_ps(PSP)
            proj_tm(tt, 512, 512, pi, w=wukv, nk=2, src=ckvn, rsrc=rckvn)
            self.evac_store(st, pi, 128, [(0, 128, self.Vm[t0 + tt * 128:t0 + (tt + 1) * 128, :], rq)], eng=eng())

        b_ropeload()
        for kc in range(3):
            b_cq(kc)
        for kc in range(2):
            b_ckv(kc)
        b_fg()
        st_cq[0](); st_cq[1]()
        b_fq(0); b_fq(1)
        st_cq[2]()
        b_fq(2); b_fq(3)
        b_cqn()
        b_fk(0)
        st_ckv[0](); st_ckv[1]()
        b_cum(0)
        b_fk(1)
        st_ckv[2]()
        b_cum(1)
        b_fk(2)
        b_ckvn()
        b_cum(2)
        b_fk(3)
        b_cum(3)
        b_qn(0)
        b_cpos()
        b_qn(1); b_qn(2)
        b_crow()
        b_qn(3)
        b_fv(0); b_fv(1)
        b_qr(0)
        b_fv(2); b_fv(3)
        b_qr(1)
        for j in range(4):
            b_kn(j)
        for tt in range(4):
            b_v(tt)
        b_kr()

    def phase_T0(self):
        P = self.P
        self.alloc_tok_common()
        xs, rx = self.xss[0]
        hb, rhb = self.hbs[0]
        st = self.alloc_inproj(0)
        self.load_inproj_weights(0, st)
        rxin = P.R("xT_in")
        for c in range(self.NCH):
            self.load_x(c, self.xT_in, rxin, xs, rx)
            for stp in self.prenorm_steps(V_NMP + 0, xs, rx, hb, rhb):
                stp()
            self.inproj_chunk(0, c, st, hb, rhb)

    def phase_T1(self, l):
        P = self.P
        A = self.arena
        i = l // 2
        self.alloc_tok_common(nx=1, nhb=2)
        xs, rx = self.xss[0]
        pf = self.prefetch_T1(l, want_wout=True)
        wout, wup = pf["wout"], pf["wup"]
        at = A.alloc(8 * 512, BF16).rearrange("p (k c) -> p k c", k=8)
        m = A.alloc(8 * 512, F32).rearrange("p (k c) -> p k c", k=8)
        G = [A.alloc(514, F32) for _ in range(3)]
        T = [A.alloc(512, F32) for _ in range(3)]
        GE = [A.alloc(512, F32) for _ in range(2)]
        AC = [A.alloc(512, BF16) for _ in range(3)]
        halo = A.alloc(NJ * 2, F32)
        rwo, rwu = P.R("wout", l), P.R("wup", l)
        rat, rm, rhalo = P.R("at"), P.R("m"), P.R("halo")
        P.add("dve", MSET(halo, 0.0), [], [rhalo])
        rAT, rXT = P.R("AT"), P.R("XT")
        PSP = [0, 1, 2, 3]
        src_x = self.xT_in if l == 0 else self.XT

        def stage1(c):
            hb, rhb = self.hbs[c % 2]
            cs = slice(c * TC, (c + 1) * TC)
            steps = []

            def s_load():
                self.load_x(c, src_x, rXT, xs, rx)
                P.dma("sp", at, self.AT[:, :, cs].rearrange("k p s -> p k s"), [rAT], [rat], semkey="at")
            steps.append(s_load)
            for j in range(8):
                def s_proj(j=j):
                    pi = self.next_ps(PSP)
                    P.add("pe", [MM(self.ps[pi][:], wout[:, kc, j * 128:(j + 1) * 128], at[:, kc, :], kc == 0, kc == 7) for kc in range(8)],
                          [rwo, rat], [self.psr[pi]])
                    P.add("act", ACT(m[:, j, :], self.ps[pi][:], AF.Copy), [self.psr[pi]], [rm])
                steps.append(s_proj)
            steps = [steps[0]] + self.merge_pairs(steps[1:])
            chain = self.residual_steps(m, rm, V_NMPOST + 8 * l, xs, rx, add_eng="dve")
            chain.append(lambda: self.store_x(c, self.XT, P.R("XTc", l, c), xs, rx))
            chain += self.prenorm_steps(V_NFP + 8 * l, xs, rx, hb, rhb)
            steps += self.spaced(chain, {2: 2, 5: 1, 8: 2})
            return steps

        for stp in stage1(0):
            stp()
        for c in range(self.NCH):
            cs = slice(c * TC, (c + 1) * TC)
            hb, rhb = self.hbs[c % 2]
            self.pending = stage1(c + 1) if c + 1 < self.NCH else []
            racts = P.R("ACTT", l, c)

            def stageA(j):
                pg = self.next_ps(PSP)
                pv = self.next_ps([5, 6, 7])
                P.add("pe", [MM(self.ps[pg][:], wup[:, kc, j * 128:(j + 1) * 128], hb[:, kc, :], kc == 0, kc == 7) for kc in range(8)],
                      [rwu, rhb], [self.psr[pg]])
                P.add("pe", [MM(self.ps[pv][:], wup[:, kc, DFF + j * 128:DFF + (j + 1) * 128], hb[:, kc, :], kc == 0, kc == 7) for kc in range(8)],
                      [rwu, rhb], [self.psr[pv]])
                g, t = G[j % 3], T[j % 3]
                rg_, rt_ = P.R("G", j % 3), P.R("T", j % 3)
                P.add("act", ACT(g[:, 2:514], self.ps[pg][:], AF.Copy), [self.psr[pg]], [rg_])
                P.add("pool", CP(g[:, 0:2], halo[:, 2 * j:2 * j + 2]), [rhalo], [rg_])
                cw = [self.vecs[:, V_CONVW + (l * 3 + tap) * NJ + j:V_CONVW + (l * 3 + tap) * NJ + j + 1] for tap in range(3)]
                cb = self.vecs[:, V_CONVB + l * NJ + j:V_CONVB + l * NJ + j + 1]
                P.add("pool", TS(t, g[:, 2:514], cw[2], cb, ALU.mult, ALU.add), [rg_, self.rg], [rt_])
                P.add("dve", STT(t, g[:, 1:513], cw[1], t, ALU.mult, ALU.add), [rg_, rt_, self.rg], [rt_])
                P.add("dve", STT(t, g[:, 0:512], cw[0], t, ALU.mult, ALU.add), [rg_, rt_, self.rg], [rt_])
                P.add("pool", CP(halo[:, 2 * j:2 * j + 2], g[:, 512:514]), [rg_], [rhalo])
                return pv

            def stageB(j, pv):
                t, ge, ac = T[j % 3], GE[j % 2], AC[j % 3]
                rt_, rge_, rac_ = P.R("T", j % 3), P.R("GE", j % 2), P.R("AC", j % 3)
                P.add("act", ACT(ge, t, AF.Gelu_apprx_tanh), [rt_], [rge_])
                P.add("dve", TT(ac, self.ps[pv][:], ge, ALU.mult), [self.psr[pv], rge_], [rac_])
                P.dma("pool", self.ACTT[j, :, cs], ac, [rac_], [racts], semkey=("AC", j % 3))

            pvs = {}
            pvs[0] = stageA(0)
            for j in range(1, NJ):
                pvs[j] = stageA(j)
                stageB(j - 1, pvs[j - 1])
                self.tick()
            stageB(NJ - 1, pvs[NJ - 1])
            self.flush()
        self.pf = None
        self.arena.top = self.arena.cap

    def phase_T2(self, l):
        P = self.P
        A = self.arena
        last = (l == self.depth - 1)
        nxt_even = (not last) and ((l + 1) % 2 == 0)
        zip_ = not nxt_even
        nbuf = 2 if last else 1
        self.alloc_tok_common(nx=nbuf, nhb=(2 if (zip_ and not last) else 1))
        wdn = A.alloc(NJ * 1024, BF16).rearrange("p (k c) -> p k c", k=NJ)
        acts = [(A.alloc(NJ * 512, BF16).rearrange("p (k c) -> p k c", k=NJ), P.R("act", q)) for q in range(nbuf)]
        fs = [(A.alloc(8 * 512, F32).rearrange("p (k c) -> p k c", k=8), P.R("f", q)) for q in range(nbuf)]
        rwd = P.R("wdn", l)
        self.load_weight(wdn, self.w["ffn_w_down"][l], DFF, 1024, rwd)
        st = None
        if not last:
            if nxt_even:
                st = self.alloc_inproj(l + 1, fbuf=fs[0][0], rf=fs[0][1])
            else:
                st = self.alloc_inproj(l + 1)
            self.load_inproj_weights(l + 1, st)
        rXT, rACTT = P.R("XT2"), P.R("ACTT2")
        PSP = [0, 1, 2, 3]
        alias_res = [st["r_cqs"], st["r_ckvs"], st["r_tmp"]] if nxt_even else []

        def stage_a(c):
            xs, rx = self.xss[c % nbuf]
            act, ract = acts[c % nbuf]
            f, rf = fs[c % nbuf]
            cs = slice(c * TC, (c + 1) * TC)
            steps = []

            def s_load():
                self.load_x(c, self.XT, rXT, xs, rx)
                P.dma("sp", act, self.ACTT[:, :, cs].rearrange("k p s -> p k s"), [rACTT], [ract], semkey=("act", c % nbuf))
            steps.append(s_load)
            for j in range(8):
                def s_proj(j=j):
                    pi = self.next_ps(PSP)
                    P.add("pe", [MM(self.ps[pi][:], wdn[:, kc, j * 128:(j + 1) * 128], act[:, kc, :], kc == 0, kc == NJ - 1) for kc in range(NJ)],
                          [rwd, ract], [self.psr[pi]])
                    P.add("act", ACT(f[:, j, :], self.ps[pi][:], AF.Copy), [self.psr[pi]], [rf] + alias_res)
                steps.append(s_proj)
            return steps

        def stage_b(c):
            xs, rx = self.xss[c % nbuf]
            f, rf = fs[c % nbuf]
            steps = self.residual_steps(f, rf, V_NFPOST + 8 * l, xs, rx, add_eng="dve")
            if last:
                steps.append(lambda: self.store_x(c, self.out, P.R("OUT"), xs, rx))
            else:
                hb, rhb = self.hbs[c % len(self.hbs)]
                steps.append(lambda: self.store_x(c, self.XT, P.R("XTc2", l, c), xs, rx))
                steps += self.prenorm_steps(V_NMP + 8 * (l + 1), xs, rx, hb, rhb)
            return steps

        def run(steps):
            for stp in steps:
                stp()

        if last and not getattr(self, 'dbg_nolastpipe', False):
            run(stage_a(0))
            for c in range(self.NCH):
                sa = stage_a(c + 1) if c + 1 < self.NCH else []
                sb = stage_b(c)
                while sa or sb:
                    if sa:
                        sa.pop(0)()
                    if sb:
                        sb.pop(0)()
        elif last or not zip_:
            for c in range(self.NCH):
                run(stage_a(c))
                run(stage_b(c))
                if not last:
                    hb, rhb = self.hbs[0]
                    self.inproj_chunk(l + 1, c, st, hb, rhb)
        else:
            run(stage_a(0))
            run(stage_b(0))
            for c in range(self.NCH):
                hb, rhb = self.hbs[c % len(self.hbs)]
                if c + 1 < self.NCH:
                    sa = stage_a(c + 1)
                    sb = stage_b(c + 1)
                    self.pending = [sa[0]] + self.merge_pairs(sa[1:]) + self.spaced(sb, {2: 4, 3: 1, 4: 1, 5: 3, 8: 4, 9: 1})
                else:
                    self.pending = []
                self.inproj_chunk(l + 1, c, st, hb, rhb)
                self.flush()

    def phase_ATT(self, l):
        P = self.P
        A = self.arena
        S, NT, NCH = self.S, self.NT, self.NCH
        even = (l % 2 == 0)
        i = l // 2
        nsm = 1 if even else 2
        NSLOT = 2
        slots = []
        for s in range(NSLOT):
            sl = {}
            sl["QA"] = [A.alloc(S, BF16) for _ in range(nsm)]
            sl["KA"] = [A.alloc(S, BF16) for _ in range(nsm)]
            sl["V"] = A.alloc(NT * 128, BF16).rearrange("p (t d) -> p t d", d=128)
            sl["r"] = P.R("slot", l, s)
            sl["key"] = ("slot", l, s)
            slots.append(sl)
        NPT = 6
        PT = [A.alloc(512, BF16) for _ in range(NPT)]
        rPT = [P.R("PT", k) for k in range(NPT)]
        zs = A.alloc(512, F32)
        ab = [A.alloc(512, BF16) for _ in range(2)]
        rzs, rab = P.R("zs"), [P.R("ab", 0), P.R("ab", 1)]
        if even:
            for sl in slots:
                P.add("pool", MSET(sl["V"][:, :, 64:128], 1.0), [], [sl["r"]])
            P.barrier()
        else:
            o1, o2 = A.alloc(512, F32), A.alloc(512, F32)
            r1, r2 = A.alloc(512, F32), A.alloc(512, F32)
            sqo = A.alloc(512, BF16)
            rstd = A.alloc(512, F32)
            ro, ro2, rr1, rr2, rsqo, rrs = [P.R("otail", q) for q in range(6)]
            lt = self.lamt
            lam_init = 0.8 - 0.6 * math.exp(-0.3 * l)
            prod = A.alloc(128, F32)
            rl = P.R("lam", l)
            b = V_LAM + i * 256
            vv = self.vecs
            P.add("dve", TT(prod[:, 0:64], vv[:, b:b + 64], vv[:, b + 64:b + 128], ALU.mult), [self.rg], [rl])
            P.add("dve", TT(prod[:, 64:128], vv[:, b + 128:b + 192], vv[:, b + 192:b + 256], ALU.mult), [self.rg, rl], [rl])
            P.add("dve", ("tensor_reduce", dict(out=lt[:, 0:1], in_=prod[:, 0:64], axis=mybir.AxisListType.X, op=ALU.add)), [rl], [rl])
            P.add("dve", ("tensor_reduce", dict(out=lt[:, 1:2], in_=prod[:, 64:128], axis=mybir.AxisListType.X, op=ALU.add)), [rl], [rl])
            P.add("act", ACT(lt[:, 2:4], lt[:, 0:2], AF.Exp), [rl], [rl])
            P.add("dve", TT(lt[:, 4:5], lt[:, 3:4], lt[:, 2:3], ALU.subtract), [rl], [rl])
            P.add("dve", TS(lt[:, 5:6], lt[:, 4:5], -lam_init, None, ALU.add), [rl], [rl])
            P.add("dve", TS(lt[:, 6:7], vv[:, V_SUBLN + i:V_SUBLN + i + 1], 1.0 - lam_init, None, ALU.mult), [self.rg, rl], [rl])
            P.barrier()
            neglam = lt[:, 5:6]
            subs = lt[:, 6:7]
        rAT = P.R("AT_w", l)
        rsrc = P.R("attsrc", l)
        cpos3 = self.cpos.rearrange("p (t h) -> p t h", h=8)
        alibi = self.cf[:, 384:384 + 8 * NT].rearrange("p (h t) -> p h t", h=8)
        units = list(range(16)) if even else list(range(8))
        SPS = [0, 1, 2]
        VSPL = 8

        def load_v(sl, dst_cols, src2d):
            v3 = src2d.rearrange("(t p) d -> p t d", p=128)
            for t0 in range(0, NT, VSPL):
                P.dma("sp", sl["V"][:, t0:t0 + VSPL, dst_cols], v3[:, t0:t0 + VSPL, :], [rsrc], [sl["r"]], semkey=sl["key"])

        def load_unit(u, sl):
            r, key = sl["r"], sl["key"]
            if even:
                if u < 8:
                    h = u
                    P.dma("sp", sl["QA"][0][0:64, :], self.Qf[h * 64:(h + 1) * 64, :], [rsrc], [r], semkey=key)
                    P.dma("sp", sl["QA"][0][64:67, :], self.CqR[h], [rsrc], [r], semkey=key)
                    P.dma("sp", sl["KA"][0][0:64, :], self.Kf[h * 64:(h + 1) * 64, :], [rsrc], [r], semkey=key)
                    P.dma("sp", sl["KA"][0][64:67, :], self.onesrow_in, [rsrc], [r], semkey=key)
                    P.dma("sp", sl["QA"][0][67:70, :], self.onesrow_in, [rsrc], [r], semkey=key)
                    P.dma("sp", sl["KA"][0][67:70, :], self.CkR[h], [rsrc], [r], semkey=key)
                    load_v(sl, slice(0, 64), self.Vf[:, h * 64:(h + 1) * 64])
                else:
                    h = u - 8
                    P.dma("sp", sl["QA"][0][0:96, :], self.Qm[h], [rsrc], [r], semkey=key)
                    P.dma("sp", sl["KA"][0][0:64, :], self.Kmn[h * 64:(h + 1) * 64, :], [rsrc], [r], semkey=key)
                    P.dma("sp", sl["KA"][0][64:96, :], self.Kr, [rsrc], [r], semkey=key)
                    load_v(sl, slice(0, 64), self.Vm[:, h * 64:(h + 1) * 64])
            else:
                h = u
                for s in range(2):
                    P.dma("sp", sl["QA"][s][0:64, :], self.Qd[h * 128 + s * 64:h * 128 + (s + 1) * 64, :], [rsrc], [r], semkey=key)
                    P.dma("sp", sl["QA"][s][64:66, :], self.aq_in[h], [rsrc], [r], semkey=key)
                    P.dma("sp", sl["KA"][s][0:64, :], self.Kd[h * 128 + s * 64:h * 128 + (s + 1) * 64, :], [rsrc], [r], semkey=key)
                    P.dma("sp", sl["KA"][s][64:66, :], self.onesrow_in[0:2, :], [rsrc], [r], semkey=key)
                    P.dma("sp", sl["QA"][s][66:68, :], self.onesrow_in[0:2, :], [rsrc], [r], semkey=key)
                    P.dma("sp", sl["KA"][s][66:68, :], self.ak_in[h], [rsrc], [r], semkey=key)
                load_v(sl, slice(0, 128), self.Vd[:, h * 128:(h + 1) * 128])

        load_unit(units[0], slots[0])
        deferred = []
        zhold = [None, None]
        zq = []
        psz_i = 0
        PSZ = [A.alloc(512, BF16) for _ in range(3)]
        rPSZ = [P.R("PSZ", q) for q in range(3)]
        pt_i = 0
        sp_i = 0
        tail_i = 0
        for ui, u in enumerate(units):
            sl = slots[ui % NSLOT]
            if ui + 1 < len(units):
                load_unit(units[ui + 1], slots[(ui + 1) % NSLOT])
            if ui == 0:
                self.prefetch_T1(l, want_wout=even, after=[slots[0]["r"], slots[1]["r"]])
            kdim = (70 if u < 8 else 96) if even else 68
            QA, KA, V, rs_ = sl["QA"], sl["KA"], sl["V"], sl["r"]
            for c in range(NCH):
                cs = slice(c * TC, (c + 1) * TC)
                nkt = 4 * c + 4
                acc = [3 + (tail_i % 2)] if even else [3, 4, 5, 6]
                items = [(kt, s) for s in range(nsm) for kt in range(nkt)]
                LOOK = 2
                pend = []
                n = len(items)
                for t in range(n + LOOK):
                    if t < n:
                        kt, s = items[t]
                        spi = SPS[sp_i % 3]
                        sp_i += 1
                        j = kt - 4 * c
                        q0 = 0 if j <= 0 else 128 * j
                        ps = self.ps[spi]
                        ins = [MM(ps[:, q0:512], KA[s][0:kdim, kt * 128:(kt + 1) * 128], QA[s][0:kdim, c * TC + q0:(c + 1) * TC], True, j < 0)]
                        if j >= 0:
                            ins.append(MM(ps[:, q0:q0 + 128], self.ident_bf, self.maskT, False, True))
                        P.add("pe", ins, [rs_, self.rg], [self.psr[spi]])
                        pend.append((kt, s, spi, q0))
                    if t >= LOOK:
                        kt, s, spi, q0 = pend.pop(0)
                        pti = pt_i % NPT
                        pt_i += 1
                        ps = self.ps[spi]
                        rd = [self.psr[spi], self.rg]
                        P.add("act", ACT(PT[pti][:, q0:512], ps[:, q0:512], AF.Exp, bias=0.0), rd, [rPT[pti]])
                        first, lastk = (kt == 0), (kt == nkt - 1)
                        if even:
                            P.add("pe", MM(self.ps[acc[0]][:, q0:512], V[:, kt, :], PT[pti][:, q0:512], first, lastk),
                                  [rs_, rPT[pti]], [self.psr[acc[0]]])
                        else:
                            P.add("pe", MM(self.ps[acc[s]][:, q0:512], V[:, kt, :], PT[pti][:, q0:512], first, lastk),
                                  [rs_, rPT[pti]], [self.psr[acc[s]]])
                            while zq and zq[0][0] < t:
                                zq.pop(0)[1]()
                            if kt < 4 * c and kt % 2 == 0:
                                zhold[s] = pti
                            elif kt < 4 * c:
                                pa = zhold[s]
                                zi = psz_i % len(PSZ)
                                psz_i += 1
                                P.add("dve", TT(PSZ[zi], PT[pa], PT[pti], ALU.add), [rPT[pa], rPT[pti]], [rPSZ[zi]])
                                zq.append((t, lambda zi=zi, s=s, kt=kt: P.add(
                                    "pe", MM(self.ps[acc[2 + s]][:], self.ones_bf, PSZ[zi], kt == 1, False),
                                    [rPSZ[zi], self.rg], [self.psr[acc[2 + s]]])))
                            else:
                                P.add("pe", MM(self.ps[acc[2 + s]][:, q0:512], self.ones_bf, PT[pti][:, q0:512], first, lastk),
                                      [rPT[pti], self.rg], [self.psr[acc[2 + s]]])
                        if (not even) and lastk:
                            if s == 0:
                                while deferred:
                                    deferred.pop(0)()
                                P.add("dve", CP(o1, self.ps[3][:]), [self.psr[3]], [ro])
                                P.add("act", ACT(r1, self.ps[5][:], AF.Ln), [self.psr[5]], [rr1])
                            else:
                                P.add("dve", CP(o2, self.ps[4][:]), [self.psr[4]], [ro2])
                                P.add("act", ACT(r2, self.ps[6][:], AF.Ln), [self.psr[6]], [rr2])
                        if deferred and t >= LOOK + 1:
                            deferred.pop(0)()
                if even:
                    while deferred:
                        deferred.pop(0)()
                k = tail_i % 2
                tail_i += 1
                steps = []
                if even:
                    a = acc[0]
                    steps.append(lambda a=a: P.add("dve", RCP(zs[64:128, :], self.ps[a][64:128, :]), [self.psr[a]], [rzs]))
                    kc = (u // 2) if u < 8 else 4 + (u - 8) // 2
                    po = (u % 2) * 64

                    def fin(a=a, k=k, kc=kc, po=po, cs=cs):
                        P.add("dve", TT(ab[k][0:64, :], self.ps[a][0:64, :], zs[64:128, :], ALU.mult), [self.psr[a], rzs], [rab[k]])
                        P.dma("pool", self.AT[kc, po:po + 64, cs], ab[k][0:64, :], [rab[k]], [rAT], semkey=("ab", k))
                    steps.append(fin)
                else:
                    steps.append(lambda: P.add("act", ACT(r1, r1, AF.Exp, scale=-1.0), [rr1], [rr1]))
                    steps.append(lambda: P.add("act", ACT(r2, r2, AF.Exp, scale=-1.0), [rr2], [rr2]))
                    steps.append(lambda: P.add("dve", TT(o1, o1, r1, ALU.mult), [ro, rr1], [ro]))
                    steps.append(lambda: P.add("dve", TT(o2, o2, r2, ALU.mult), [ro2, rr2], [ro2]))
                    steps.append(lambda: P.add("dve", STT(o1, o2, neglam, o1, ALU.mult, ALU.add), [ro, ro2], [ro]))
                    steps.append(lambda: P.add("act", ACT(sqo, o1, AF.Square), [ro], [rsqo]))
                    steps.append(lambda: P.add("pe", MM(self.ps[7][:], self.ones_bf, sqo, True, True), [rsqo, self.rg], [self.psr[7]]))
                    steps.append(lambda: P.add("act", ACT(rstd, self.ps[7][:], AF.Ln, scale=1.0 / 128, bias=self.epsb), [self.psr[7], self.rg], [rrs]))
                    steps.append(lambda: P.add("act", ACT(rstd, rstd, AF.Exp, scale=-0.5), [rrs], [rrs]))

                    def fin(k=k, u=u, cs=cs):
                        P.add("dve", STT(ab[k], o1, subs, rstd, ALU.mult, ALU.mult), [ro, rrs], [rab[k]])
                        P.dma("pool", self.AT[u, :, cs], ab[k], [rab[k]], [rAT], semkey=("ab", k))
                    steps.append(fin)
                deferred.extend(steps)
        while deferred:
            deferred.pop(0)()

    def build(self, stop_after=None, only=None):
        P = self.P
        self.init_globals()
        P.barrier()
        seq = [("T0", 0)]
        for l in range(self.depth):
            seq += [("ATT", l), ("T1", l), ("T2", l)]
        for (ph, l) in seq:
            if only is not None and (ph, l) not in only:
                continue
            if ph == "T0":
                self.phase_T0()
            elif ph == "ATT":
                self.phase_ATT(l)
            elif ph == "T1":
                self.phase_T1(l)
            else:
                self.phase_T2(l)
            self.phase_reset()
            if stop_after == (ph, l):
                break
        P.lower()
        return self.nc


_CACHE = {}
SHARED_KEYS = ["even_w_in", "even_w_uq", "even_w_ukv", "even_w_out", "odd_w_in", "odd_w_out", "ffn_w_up", "ffn_w_down"]


def make_in_maps(inputs, S, B):
    consts = _CACHE.get(("consts", S))
    if consts is None:
        consts = make_consts(S)
        _CACHE[("consts", S)] = consts
    shared = {"vecs": pack_vecs(inputs)}
    for k in SHARED_KEYS:
        shared[k] = np.ascontiguousarray(np.asarray(inputs[k], np.float32))
    for k in ["cbf", "cf", "rope", "aq", "ak", "onesrow"]:
        shared[k] = consts[k]
    in_maps = []
    for b in range(B):
        m = dict(shared)
        x = np.asarray(inputs["x"][b], np.float32)
        m["xT"] = np.ascontiguousarray(x.T).reshape(8, 128, S)
        in_maps.append(m)
    return in_maps


def kernel(**inputs):
    x = np.asarray(inputs["x"])
    B, S, _ = x.shape
    bld = Builder(S)
    nc = bld.build()
    in_maps = make_in_maps(inputs, S, B)
    res = run_bass_kernel_spmd(nc, in_maps, core_ids=list(range(B)))
    out = np.empty((B, S, D), np.float32)
    for b in range(B):
        out[b] = np.asarray(res.results[b]["outT"]).reshape(D, S).T
    return out
```

```python
import math
import numpy as np
import ml_dtypes
import concourse.bass as bass
import concourse.mybir as mybir
from concourse.bass_utils import run_bass_kernel_spmd

F32 = mybir.dt.float32
BF16 = mybir.dt.bfloat16
AF = mybir.ActivationFunctionType
ALU = mybir.AluOpType

D = 1024
DEPTH = 4
DFF = 2816
NJ = DFF // 128
EPS = 1e-6
TC = 512
ARENA_BYTES = 206 * 1024
NEG = -30000.0
EVEN_W = 2216
C_FQ, C_FK, C_FV, C_FG, C_CQ, C_CKV, C_KR, C_KRS = 0, 512, 1024, 1536, 1544, 1928, 2184, 2216
EVEN_WT = 2248


class Res:
    __slots__ = ("name", "writers", "readers")

    def __init__(self, name):
        self.name = name
        self.writers = []
        self.readers = []


class Op:
    __slots__ = ("eng", "fn", "deps", "dma", "semkey")

    def __init__(self, eng, fn, deps, dma, semkey):
        self.eng = eng
        self.fn = fn
        self.deps = deps
        self.dma = dma
        self.semkey = semkey


class Prog:
    def __init__(self, nc):
        self.nc = nc
        self.ops = []
        self.res = {}
        self.last_eng = {}
        self.last_dma = {}
        self.engobj = {"pe": nc.tensor, "act": nc.scalar, "dve": nc.vector, "pool": nc.gpsimd, "sp": nc.sync}

    def R(self, *key):
        r = self.res.get(key)
        if r is None:
            r = Res(key)
            self.res[key] = r
        return r

    def add(self, eng, insts, reads=(), writes=(), dma=False, semkey=None):
        if isinstance(insts, tuple):
            insts = [insts]
        i = len(self.ops)
        deps = {}
        for r in reads:
            for w in r.writers:
                deps[w] = "raw"
        for r in writes:
            for x in r.readers:
                deps.setdefault(x, "war")
        fdeps = []
        for d, kind in deps.items():
            od = self.ops[d]
            if not dma and not od.dma and od.eng == eng:
                if eng == "pe" or kind != "raw":
                    continue
            fdeps.append(d)

        def push(lst):
            if lst and not dma:
                o = self.ops[lst[-1]]
                if not o.dma and o.eng == eng:
                    lst[-1] = i
                    return
            lst.append(i)

        for r in writes:
            if r.readers or (r in reads):
                r.writers = [i]
                r.readers = []
            else:
                push(r.writers)
        for r in reads:
            if r not in writes:
                push(r.readers)
        self.ops.append(Op(eng, insts, fdeps, dma, semkey))
        if dma:
            self.last_dma[semkey] = i
        else:
            self.last_eng[eng] = i
        return i

    def dma(self, queue, out, in_, reads, writes, semkey):
        return self.add(queue, [("dma_start", dict(out=out, in_=in_))], reads, writes, dma=True, semkey=semkey)

    def barrier(self):
        lasts = list(self.last_eng.values()) + list(self.last_dma.values())
        for eng in ("pe", "act", "dve", "pool", "sp"):
            i = len(self.ops)
            deps = [d for d in lasts if not (not self.ops[d].dma and self.ops[d].eng == eng)]
            self.ops.append(Op(eng, None, deps, False, None))
        for r in self.res.values():
            r.writers = []
            r.readers = []
        self.last_dma = {}

    def lower(self):
        nc = self.nc
        n = len(self.ops)
        needed = [False] * n
        for op in self.ops:
            for d in op.deps:
                needed[d] = True
        sems = {}
        cnt = {}

        def getsem(key):
            s = sems.get(key)
            if s is None:
                s = nc.alloc_semaphore("s%d" % len(sems))
                sems[key] = s
                cnt[key] = 0
            return s

        sig = [None] * n
        waited = {}
        nwait = 0
        for i, op in enumerate(self.ops):
            e = self.engobj[op.eng]
            reqs = {}
            for d in op.deps:
                k, v = sig[d]
                if reqs.get(k, 0) < v:
                    reqs[k] = v
            for k, v in reqs.items():
                if waited.get((op.eng, k), 0) >= v:
                    continue
                e.wait_ge(sems[k], v)
                nwait += 1
                waited[(op.eng, k)] = v
            if op.fn is None:
                continue
            ins = None
            for (nm, kw) in op.fn:
                ins = getattr(e, nm)(**kw)
            if needed[i] or op.dma:
                if op.dma:
                    key = ("dma", op.semkey)
                    s = getsem(key)
                    cnt[key] += 16
                    ins.then_inc(s, 16)
                else:
                    key = ("eng", op.eng)
                    s = getsem(key)
                    cnt[key] += 1
                    ins.then_inc(s, 1)
                sig[i] = (key, cnt[key])
        self.stats = dict(nops=n, nwait=nwait, nsem=len(sems))


class Arena:
    def __init__(self, nc, nbytes):
        self.t = nc.alloc_sbuf_tensor("arena", [128, nbytes // 4], F32)
        self.cap = nbytes
        self.off = 0
        self.top = nbytes

    def alloc(self, nelem, dtype, top=False):
        nb = nelem * (4 if dtype == F32 else 2)
        nb = (nb + 63) // 64 * 64
        if top:
            self.top -= nb
            o = self.top
        else:
            o = self.off
            self.off += nb
        assert self.off <= self.top, ("SBUF arena overflow", self.off, self.top)
        ap = self.t[:, o // 4:(o + nb) // 4]
        if dtype != F32:
            ap = ap.bitcast(dtype)
        return ap[:, :nelem]


def _bf16_split(a, n):
    outs = []
    r = a.astype(np.float64)
    for _ in range(n):
        h = r.astype(np.float32).astype(ml_dtypes.bfloat16)
        outs.append(h)
        r = r - h.astype(np.float64)
    return outs


def make_consts(S):
    NT = S // 128
    c = {}
    k = np.arange(128)
    cb = np.zeros((128, 384), np.float32)
    cb[:, 0:128] = 1.0
    cb[:, 128:256] = np.eye(128, dtype=np.float32)
    cb[:, 256:384] = np.where(k[:, None] <= k[None, :], 0.0, NEG)
    c["cbf"] = cb.astype(ml_dtypes.bfloat16)
    slopes = 2.0 ** (-8.0 * np.arange(1, 9, dtype=np.float64) / 8)
    cf = np.zeros((128, 384 + 8 * NT), np.float32)
    cf[:, 0:128] = (k[:, None] <= k[None, :]).astype(np.float32)
    cf[:, 128:256] = 1.0
    cf[:, 256:384] = np.eye(128, dtype=np.float32)
    pos = (np.arange(NT)[None, :] * 128 + k[:, None]).astype(np.float64)
    for h in range(8):
        cf[:, 384 + h * NT:384 + (h + 1) * NT] = (slopes[h] * pos).astype(np.float32)
    c["cf"] = cf
    inv = (10000.0 ** (-np.arange(0, 32, 2, dtype=np.float32) / 32)).astype(np.float32)
    ang = (np.arange(S, dtype=np.float32)[:, None] * inv[None, :]).astype(np.float32)
    cos = np.cos(ang).astype(np.float32).T
    sin = np.sin(ang).astype(np.float32).T
    cs = np.concatenate([cos, cos], 0)
    ss = np.concatenate([-sin, sin], 0)
    rope = np.zeros((2, 128, S), np.float32)
    rope[0] = np.tile(cs, (4, 1))
    rope[1] = np.tile(ss, (4, 1))
    c["rope"] = rope
    t = np.arange(S, dtype=np.float64)
    aq = np.zeros((8, 2, S), ml_dtypes.bfloat16)
    for h in range(8):
        hi, lo = _bf16_split(-slopes[h] * t, 2)
        aq[h, 0] = hi
        aq[h, 1] = lo
    c["aq"] = aq
    ak = np.zeros((8, 2, S), ml_dtypes.bfloat16)
    for h in range(8):
        hi, lo = _bf16_split(slopes[h] * t, 2)
        ak[h, 0] = hi
        ak[h, 1] = lo
    c["ak"] = ak
    c["onesrow"] = np.ones((3, S), ml_dtypes.bfloat16)
    return c


V_NMP, V_NMPOST, V_NFP, V_NFPOST = 0, 32, 64, 96
V_QN = 128
V_KVN = 134
V_SUBLN = 138
V_CONVW = 140
V_CONVB = 404
V_BF = 492
V_LAM = 556
NVEC = 1068


def pack_vecs(inp):
    v = np.zeros((128, NVEC), np.float32)

    def pl(a):
        return np.asarray(a, np.float32).reshape(8, 128).T

    for l in range(DEPTH):
        v[:, V_NMP + 8 * l:V_NMP + 8 * l + 8] = pl(inp["norm_mix_pre"][l])
        v[:, V_NMPOST + 8 * l:V_NMPOST + 8 * l + 8] = pl(inp["norm_mix_post"][l])
        v[:, V_NFP + 8 * l:V_NFP + 8 * l + 8] = pl(inp["norm_ffn_pre"][l])
        v[:, V_NFPOST + 8 * l:V_NFPOST + 8 * l + 8] = pl(inp["norm_ffn_post"][l])
        cw = np.asarray(inp["ffn_conv_w"][l], np.float32)
        for tap in range(3):
            v[:, V_CONVW + (l * 3 + tap) * NJ:V_CONVW + (l * 3 + tap + 1) * NJ] = cw[tap].reshape(NJ, 128).T
        v[:, V_CONVB + l * NJ:V_CONVB + (l + 1) * NJ] = np.asarray(inp["ffn_conv_b"][l], np.float32).reshape(NJ, 128).T
    for i in range(2):
        v[:, V_QN + 3 * i:V_QN + 3 * i + 3] = np.asarray(inp["even_q_norm"][i], np.float32).reshape(3, 128).T
        v[:, V_KVN + 2 * i:V_KVN + 2 * i + 2] = np.asarray(inp["even_kv_norm"][i], np.float32).reshape(2, 128).T
        v[:, V_SUBLN + i] = np.asarray(inp["odd_subln"][i], np.float32)
        v[:, V_BF + 32 * i:V_BF + 32 * i + 32] = np.tile(np.asarray(inp["even_b_forget"][i], np.float32), 4)[None, :]
        for j, nm in enumerate(["odd_lambda_q1", "odd_lambda_k1", "odd_lambda_q2", "odd_lambda_k2"]):
            o = V_LAM + (i * 4 + j) * 64
            v[:, o:o + 64] = np.asarray(inp[nm][i], np.float32)[None, :]
    return v


def ACT(out, in_, func, **kw):
    return ("activation", dict(out=out, in_=in_, func=func, **kw))


def MM(out, lhsT, rhs, start=True, stop=True):
    return ("matmul", dict(out=out, lhsT=lhsT, rhs=rhs, start=start, stop=stop))


def TT(out, in0, in1, op):
    return ("tensor_tensor", dict(out=out, in0=in0, in1=in1, op=op))


def TS(out, in0, s1, s2, op0, op1=None):
    kw = dict(out=out, in0=in0, scalar1=s1, scalar2=s2, op0=op0)
    if op1 is not None:
        kw["op1"] = op1
    return ("tensor_scalar", kw)


def STT(out, in0, scalar, in1, op0, op1):
    return ("scalar_tensor_tensor", dict(out=out, in0=in0, scalar=scalar, in1=in1, op0=op0, op1=op1))


def CP(out, in_):
    return ("tensor_copy", dict(out=out, in_=in_))


def MSET(ap, v):
    return ("memset", dict(ap=ap, constant=v))


def RCP(out, in_):
    return ("reciprocal", dict(out=out, in_=in_))


class Builder:
    def __init__(self, S, depth=DEPTH, dbg=()):
        self.S = S
        self.NT = S // 128
        self.NCH = S // TC
        self.depth = depth
        nc = self.nc = bass.Bass("TRN2", target_bir_lowering=False)
        self.P = Prog(nc)
        self.dbg = set(dbg)
        NT = self.NT

        def din(name, shape, dt):
            return nc.dram_tensor(name, list(shape), dt, kind="ExternalInput").ap()

        def dscr(name, shape, dt):
            kind = "ExternalOutput" if name in self.dbg else "Internal"
            return nc.dram_tensor(name, list(shape), dt, kind=kind).ap()

        self.xT_in = din("xT", [8, 128, S], F32)
        self.vecs_in = din("vecs", [128, NVEC], F32)
        self.cbf_in = din("cbf", [128, 384], BF16)
        self.cf_in = din("cf", [128, 384 + 8 * NT], F32)
        self.rope_in = din("rope", [2, 128, S], F32)
        self.aq_in = din("aq", [8, 2, S], BF16)
        self.ak_in = din("ak", [8, 2, S], BF16)
        self.onesrow_in = din("onesrow", [3, S], BF16)
        self.w = {}
        self.w["even_w_in"] = din("even_w_in", [2, 1024, EVEN_W], F32)
        self.w["even_w_uq"] = din("even_w_uq", [2, 384, 768], F32)
        self.w["even_w_ukv"] = din("even_w_ukv", [2, 256, 1024], F32)
        self.w["even_w_out"] = din("even_w_out", [2, 1024, 1024], F32)
        self.w["odd_w_in"] = din("odd_w_in", [2, 1024, 3072], F32)
        self.w["odd_w_out"] = din("odd_w_out", [2, 1024, 1024], F32)
        self.w["ffn_w_up"] = din("ffn_w_up", [4, 1024, 2 * DFF], F32)
        self.w["ffn_w_down"] = din("ffn_w_down", [4, DFF, 1024], F32)
        self.out = nc.dram_tensor("outT", [8, 128, S], F32, kind="ExternalOutput").ap()
        self.XT = dscr("XT", [8, 128, S], F32)
        self.AT = dscr("AT", [8, 128, S], BF16)
        self.ACTT = dscr("ACTT", [NJ, 128, S], BF16)
        self.Qf = dscr("Qf", [512, S], BF16)
        self.Kf = dscr("Kf", [512, S], BF16)
        self.Vf = dscr("Vf", [S, 512], BF16)
        self.CqR = dscr("CqR", [8, 3, S], BF16)
        self.CkR = dscr("CkR", [8, 3, S], BF16)
        self.Qm = dscr("Qm", [8, 96, S], BF16)
        self.Kmn = dscr("Kmn", [512, S], BF16)
        self.Kr = dscr("Kr", [32, S], BF16)
        self.Vm = dscr("Vm", [S, 512], BF16)
        self.Qd = dscr("Qd", [1024, S], BF16)
        self.Kd = dscr("Kd", [1024, S], BF16)
        self.Vd = dscr("Vd", [S, 1024], BF16)

        self.arena = Arena(nc, ARENA_BYTES)
        A = self.arena
        self.vecs = A.alloc(NVEC, F32)
        self.cbf = A.alloc(384, BF16)
        self.cf = A.alloc(384 + 8 * NT, F32)
        self.cpos = A.alloc(NT * 8, F32)
        self.lamt = A.alloc(16, F32)
        self.epsb = A.alloc(2, F32)[:, 0:1]
        self.ones_bf = self.cbf[:, 0:128]
        self.ident_bf = self.cbf[:, 128:256]
        self.maskT = self.cbf[:, 256:384]
        self.U_f = self.cf[:, 0:128]
        self.ones_f = self.cf[:, 128:256]
        self.ident_f = self.cf[:, 256:384]
        self.phase_base = A.off
        self.ps = [nc.alloc_psum_tensor("ps%d" % i, [128, 512], F32) for i in range(8)]
        self.psr = [self.P.R("ps", i) for i in range(8)]
        self.ps_rrs = {}

    def phase_reset(self):
        self.P.barrier()
        self.arena.off = self.phase_base

    def next_ps(self, pool):
        key = tuple(pool)
        n = self.ps_rrs.get(key, 0)
        self.ps_rrs[key] = n + 1
        return pool[n % len(pool)]

    def load_weight(self, dst, src, K, ncols, res, c0=0, dst_c0=0, after=()):
        for kc in range(K // 128):
            for s0 in range(0, ncols, 2048):
                s1 = min(ncols, s0 + 2048)
                self.P.dma("pool", dst[:, kc, dst_c0 + s0:dst_c0 + s1], src[kc * 128:(kc + 1) * 128, c0 + s0:c0 + s1],
                           list(after) if (kc == 0 and s0 == 0) else [], [res], semkey=res.name)

    def prefetch_T1(self, l, want_wout, after=()):
        pf = getattr(self, "pf", None)
        if pf is None or pf["l"] != l:
            pf = self.pf = {"l": l, "wout": None, "wup": None}
        A, P = self.arena, self.P
        i = l // 2
        if pf["wup"] is None:
            pf["wup"] = A.alloc(8 * 2 * DFF, BF16, top=True).rearrange("p (k c) -> p k c", k=8)
            self.load_weight(pf["wup"], self.w["ffn_w_up"][l], 1024, 2 * DFF, P.R("wup", l), after=after)
        if want_wout and pf["wout"] is None:
            pf["wout"] = A.alloc(8 * 1024, BF16, top=True).rearrange("p (k c) -> p k c", k=8)
            self.load_weight(pf["wout"], self.w["even_w_out" if l % 2 == 0 else "odd_w_out"][i], 1024, 1024, P.R("wout", l))
        return pf

    def init_globals(self):
        P = self.P
        rg = self.rg = P.R("globals")
        P.dma("sp", self.vecs, self.vecs_in, [], [rg], semkey="g")
        P.dma("sp", self.cbf, self.cbf_in, [], [rg], semkey="g")
        P.dma("sp", self.cf, self.cf_in, [], [rg], semkey="g")
        P.add("dve", MSET(self.epsb, EPS), [], [rg])

    @staticmethod
    def merge_pairs(steps):
        out = []
        for q in range(0, len(steps), 2):
            grp = steps[q:q + 2]
            out.append(lambda grp=grp: [g() for g in grp])
        return out

    @staticmethod
    def spaced(steps, gaps):
        out = []
        for q, stp in enumerate(steps):
            out += [(lambda: None)] * gaps.get(q, 0)
            out.append(stp)
        return out

    def tick(self):
        if self.pending:
            self.pending.pop(0)()

    def flush(self):
        while self.pending:
            self.pending.pop(0)()

    def alloc_tok_common(self, nx=1, nhb=1):
        A = self.arena
        P = self.P
        self.pending = []
        self.xss = [(A.alloc(8 * 512, F32).rearrange("p (k c) -> p k c", k=8), P.R("xs", q)) for q in range(nx)]
        self.hbs = [(A.alloc(8 * 512, BF16).rearrange("p (k c) -> p k c", k=8), P.R("hb", q)) for q in range(nhb)]
        self.sq = A.alloc(8 * 512, BF16).rearrange("p (k c) -> p k c", k=8)
        self.rsq = P.R("sq")
        self.rstd = A.alloc(512, F32)
        self.rrstd = P.R("rstd")

    def stats_steps(self, src, nk, n, r_src, ps_i=4):
        P = self.P
        ps, psr = self.ps[ps_i], self.psr[ps_i]
        sq, rsq, rstd, rr = self.sq, self.rsq, self.rstd, self.rrstd
        steps = []
        h = (nk + 1) // 2
        for (k0, k1) in ((0, h), (h, nk)):
            if k1 > k0:
                steps.append(lambda k0=k0, k1=k1: P.add("act", ACT(sq[:, k0:k1, :], src[:, k0:k1, :], AF.Square), [r_src], [rsq]))

        def fin():
            P.add("pe", [MM(ps[:], self.ones_bf, sq[:, kc, :], kc == 0, kc == nk - 1) for kc in range(nk)], [rsq, self.rg], [psr])
            P.add("act", ACT(rstd, ps[:], AF.Ln, scale=1.0 / n, bias=self.epsb), [psr, self.rg], [rr])
            P.add("act", ACT(rstd, rstd, AF.Exp, scale=-0.5), [rr], [rr])
        steps.append(fin)
        return steps

    def stats_rstd(self, src, nk, n, r_src, ps_i=4):
        for st in self.stats_steps(src, nk, n, r_src, ps_i):
            st()

    def prenorm_steps(self, gcol0, xs, rx, hb, rhb):
        steps = self.stats_steps(xs, 8, 1024.0, rx)
        for half in range(2):
            def dv(half=half):
                for kc in range(half * 4, half * 4 + 4):
                    g = self.vecs[:, gcol0 + kc:gcol0 + kc + 1]
                    self.P.add("dve", STT(hb[:, kc, :], xs[:, kc, :], g, self.rstd, ALU.mult, ALU.mult),
                               [rx, self.rrstd, self.rg], [rhb])
            steps.append(dv)
        return steps

    def residual_steps(self, m, rm, gcol0, xs, rx, add_eng="pool"):
        P = self.P
        steps = self.stats_steps(m, 8, 1024.0, rm)
        for half in range(2):
            def dv(half=half):
                for kc in range(half * 4, half * 4 + 4):
                    g = self.vecs[:, gcol0 + kc:gcol0 + kc + 1]
                    P.add("dve", STT(m[:, kc, :], m[:, kc, :], g, self.rstd, ALU.mult, ALU.mult), [rm, self.rrstd, self.rg], [rm])
                sl = slice(half * 4, half * 4 + 4)
                P.add(add_eng, TT(xs[:, sl, :], xs[:, sl, :], m[:, sl, :], ALU.add), [rx, rm], [rx])
            steps.append(dv)
        return steps

    def load_x(self, c, src, rsrc, xs, rx, key="xs"):
        cs = slice(c * TC, (c + 1) * TC)
        self.P.dma("sp", xs, src[:, :, cs].rearrange("k p s -> p k s"), [rsrc], [rx], semkey=(key, rx.name))

    def store_x(self, c, dst, rdst, xs, rx):
        cs = slice(c * TC, (c + 1) * TC)
        self.P.dma("sp", dst[:, :, cs].rearrange("k p s -> p k s"), xs, [rx], [rdst], semkey=("xs_st", rx.name))

    def alloc_inproj(self, l, fbuf=None, rf=None):
        A = self.arena
        P = self.P
        st = {}
        if l % 2 == 0:
            st["win"] = A.alloc(8 * EVEN_WT, BF16).rearrange("p (k c) -> p k c", k=8)
            st["wuq"] = A.alloc(3 * 1024, BF16).rearrange("p (k c) -> p k c", k=3)
            st["wukv"] = A.alloc(2 * 1024, BF16).rearrange("p (k c) -> p k c", k=2)
            aliased = fbuf is not None
            if fbuf is None:
                fbuf = A.alloc(8 * 512, F32).rearrange("p (k c) -> p k c", k=8)
            st["cqs"] = fbuf[:, 0:3, :]
            st["ckvs"] = fbuf[:, 3:5, :]
            st["ropetmp"] = [fbuf[:, 5, :], fbuf[:, 6, :]]
            st["rf"] = rf if aliased else None
            st["r_cqs"], st["r_ckvs"], st["r_tmp"] = P.R("cqs", l), P.R("ckvs", l), P.R("ropetmp", l)
            st["cqn"] = A.alloc(3 * 512, BF16).rearrange("p (k c) -> p k c", k=3)
            st["ckvn"] = A.alloc(2 * 512, BF16).rearrange("p (k c) -> p k c", k=2)
            st["rope"] = A.alloc(2 * 512, F32).rearrange("p (k c) -> p k c", k=2)
            st["fg"] = A.alloc(32, F32)
            st["lf"] = A.alloc(32, F32)
            st["run"] = A.alloc(8, F32)
            st["crow"] = A.alloc(512, F32)
            st["crow2"] = A.alloc(512, F32)
            st["crs"] = [A.alloc(512, BF16) for _ in range(3)]
            st["crsp"] = [A.alloc(512, BF16) for _ in range(3)]
        else:
            st["win"] = A.alloc(8 * 3072, BF16).rearrange("p (k c) -> p k c", k=8)
        st["stage"] = [A.alloc(512, BF16) for _ in range(4)]
        st["stage_i"] = 0
        st["rw"] = P.R("w_inproj", l)
        st["id"] = l
        return st

    def load_inproj_weights(self, l, st):
        i = l // 2
        P = self.P
        rw = st["rw"]
        key = rw.name
        if l % 2 == 0:
            w = self.w["even_w_in"][i]
            self.load_weight(st["win"], w, 1024, EVEN_W, rw)
            for kc in range(8):
                P.dma("pool", st["win"][:, kc, C_KRS:C_KRS + 16], w[kc * 128:(kc + 1) * 128, C_KR + 16:C_KR + 32], [], [rw], semkey=key)
                P.dma("pool", st["win"][:, kc, C_KRS + 16:C_KRS + 32], w[kc * 128:(kc + 1) * 128, C_KR:C_KR + 16], [], [rw], semkey=key)
            wq = self.w["even_w_uq"][i].rearrange("r (h d) -> r h d", h=8)
            for kc in range(3):
                src = wq[kc * 128:(kc + 1) * 128]
                dq = st["wuq"][:, kc, :]
                P.dma("pool", dq[:, 0:512].rearrange("p (h d) -> p h d", h=8), src[:, :, 0:64], [], [rw], semkey=key)
                P.dma("pool", dq[:, 512:768].rearrange("p (h d) -> p h d", h=8), src[:, :, 64:96], [], [rw], semkey=key)
                d2 = dq[:, 768:1024].rearrange("p (h d) -> p h d", h=8)
                P.dma("pool", d2[:, :, 0:16], src[:, :, 80:96], [], [rw], semkey=key)
                P.dma("pool", d2[:, :, 16:32], src[:, :, 64:80], [], [rw], semkey=key)
            wkv = self.w["even_w_ukv"][i].rearrange("r (h d) -> r h d", h=8)
            for kc in range(2):
                src = wkv[kc * 128:(kc + 1) * 128]
                dk = st["wukv"][:, kc, :]
                P.dma("pool", dk[:, 0:512].rearrange("p (h d) -> p h d", h=8), src[:, :, 0:64], [], [rw], semkey=key)
                P.dma("pool", dk[:, 512:1024].rearrange("p (h d) -> p h d", h=8), src[:, :, 64:128], [], [rw], semkey=key)
        else:
            self.load_weight(st["win"], self.w["odd_w_in"][i], 1024, 3072, rw)

    def evac_store(self, st, ps_i, rows, dst, scale=1.0, eng="act", src=None, rsrc=None):
        P = self.P
        k = st["stage_i"] % len(st["stage"])
        st["stage_i"] += 1
        stg = st["stage"][k]
        rs = P.R("stage", st["id"], k)
        if src is None:
            src, rsrc = self.ps[ps_i], self.psr[ps_i]
        if eng == "act":
            P.add("act", ACT(stg[0:rows, :], src[0:rows, :], AF.Copy, scale=scale), [rsrc], [rs])
        else:
            P.add("dve", TS(stg[0:rows, :], src[0:rows, :], scale, None, ALU.mult), [rsrc], [rs])
        for (r0, r1, dap, dres) in dst:
            P.dma("pool", dap, stg[r0:r1, :], [rs], [dres], semkey=("stage", st["id"], k))
        self.tick()

    def inproj_chunk(self, l, c, st, h, rh):
        P = self.P
        cs = slice(c * TC, (c + 1) * TC)
        rw = st["rw"]
        win = st["win"]
        PSP = [0, 1, 2, 3]
        ev = [0]

        def eng():
            ev[0] += 1
            return "act" if ev[0] % 2 else "dve"

        def proj_fm(col0, m, ps_i, w=win, nk=8, src=h, rsrc=rh):
            P.add("pe", [MM(self.ps[ps_i][0:m, :], w[:, kc, col0:col0 + m], src[:, kc, :], kc == 0, kc == nk - 1) for kc in range(nk)],
                  [rw, rsrc], [self.psr[ps_i]])

        def proj_tm(tt, col0, ncol, ps_i, w=win, nk=8, src=h, rsrc=rh):
            P.add("pe", [MM(self.ps[ps_i][:, 0:ncol], src[:, kc, tt * 128:(tt + 1) * 128], w[:, kc, col0:col0 + ncol], kc == 0, kc == nk - 1) for kc in range(nk)],
                  [rw, rsrc], [self.psr[ps_i]])

        t0 = c * TC
        if l % 2 == 1:
            rq = P.R("Qd", l, c)
            for j in range(8):
                pi = self.next_ps(PSP)
                proj_fm(j * 128, 128, pi)
                self.evac_store(st, pi, 128, [(0, 128, self.Qd[j * 128:(j + 1) * 128, cs], rq)], scale=0.125, eng=eng())
            for j in range(8):
                pi = self.next_ps(PSP)
                proj_fm(1024 + j * 128, 128, pi)
                self.evac_store(st, pi, 128, [(0, 128, self.Kd[j * 128:(j + 1) * 128, cs], rq)], eng=eng())
            for tt in range(4):
                for hf in range(2):
                    pi = self.next_ps(PSP)
                    proj_tm(tt, 2048 + hf * 512, 512, pi)
                    self.evac_store(st, pi, 128, [(0, 128, self.Vd[t0 + tt * 128:t0 + (tt + 1) * 128, hf * 512:(hf + 1) * 512], rq)], eng=eng())
            return
        i = l // 2
        rq = P.R("Qe", l, c)
        rfg, rlf, rrun, rcpos = P.R("fg", l), P.R("lf", l), P.R("run", l), P.R("cpos")
        fg, lf, run = st["fg"], st["lf"], st["run"]
        cqs, cqn, ckvs, ckvn = st["cqs"], st["cqn"], st["ckvs"], st["ckvn"]
        rcqs, rckvs, rtmp = st["r_cqs"], st["r_ckvs"], st["r_tmp"]
        al = [st["rf"]] if st["rf"] is not None else []
        rcqn, rckvn = P.R("cqn", l), P.R("ckvn", l)
        rrope = P.R("rope", l)
        rope = st["rope"]
        wuq, wukv = st["wuq"], st["wukv"]
        qscale = 96.0 ** -0.5
        cpos3 = self.cpos.rearrange("p (t h) -> p t h", h=8)
        crow, crow2, crs, crsp = st["crow"], st["crow2"], st["crs"], st["crsp"]
        rcr = P.R("crow", l)
        rc = [P.R("crs", l, k) for k in range(3)]
        rcp = [P.R("crsp", l, k) for k in range(3)]
        pcs = {}

        def b_fq(j):
            pi = self.next_ps(PSP)
            proj_fm(C_FQ + j * 128, 128, pi)
            self.evac_store(st, pi, 128, [(0, 128, self.Qf[j * 128:(j + 1) * 128, cs], rq)], scale=0.125, eng=eng())

        def b_fk(j):
            pi = self.next_ps(PSP)
            proj_fm(C_FK + j * 128, 128, pi)
            self.evac_store(st, pi, 128, [(0, 128, self.Kf[j * 128:(j + 1) * 128, cs], rq)], eng=eng())

        def b_fv(tt):
            pi = self.next_ps(PSP)
            proj_tm(tt, C_FV, 512, pi)
            self.evac_store(st, pi, 128, [(0, 128, self.Vf[t0 + tt * 128:t0 + (tt + 1) * 128, :], rq)], eng=eng())

        def b_fg():
            pi = self.next_ps(PSP)
            P.add("pe", [MM(self.ps[pi][:, tt * 8:(tt + 1) * 8], h[:, kc, tt * 128:(tt + 1) * 128], win[:, kc, C_FG:C_FG + 8], kc == 0, kc == 7)
                         for tt in range(4) for kc in range(8)], [rw, rh], [self.psr[pi]])
            bfv = self.vecs[:, V_BF + 32 * i:V_BF + 32 * i + 32]
            P.add("dve", TT(fg, self.ps[pi][:, 0:32], bfv, ALU.add), [self.psr[pi], self.rg], [rfg])
            P.add("act", ACT(fg, fg, AF.Exp, scale=-1.0), [rfg], [rfg])
            P.add("act", ACT(lf, fg, AF.Ln, bias=1.0), [rfg], [rlf])
            if c == 0:
                P.add("dve", MSET(run, 0.0), [], [rrun])
            pcs["pc"] = self.next_ps([5, 6])

        def b_cum(tt):
            pc = pcs["pc"]
            P.add("pe", [MM(self.ps[pc][:, tt * 8:(tt + 1) * 8], self.U_f, lf[:, tt * 8:(tt + 1) * 8], True, False),
                         MM(self.ps[pc][:, tt * 8:(tt + 1) * 8], self.ones_f, run, False, True)], [rlf, rrun, self.rg], [self.psr[pc]])
            P.add("dve", TT(run, run, lf[:, tt * 8:(tt + 1) * 8], ALU.add), [rrun, rlf], [rrun])

        def b_cpos():
            pc = pcs["pc"]
            P.add("act", ACT(self.cpos[:, c * 32:(c + 1) * 32], self.ps[pc][:, 0:32], AF.Copy), [self.psr[pc]], [rcpos])

        def b_crow():
            pr = self.next_ps([5, 6])
            P.add("pe", [MM(self.ps[pr][0:8, tt * 128:(tt + 1) * 128], cpos3[:, c * 4 + tt, :], self.ident_f, True, True) for tt in range(4)],
                  [rcpos, self.rg], [self.psr[pr]])
            P.add("act", ACT(crow[0:8, :], self.ps[pr][0:8, :], AF.Copy, scale=-1.0), [self.psr[pr]], [rcr])
            P.add("dve", CP(crs[0][0:8, :], crow[0:8, :]), [rcr], [rc[0]])
            P.add("dve", TT(crow2[0:8, :], crow[0:8, :], crs[0][0:8, :], ALU.subtract), [rcr, rc[0]], [rcr])
            P.add("dve", CP(crs[1][0:8, :], crow2[0:8, :]), [rcr], [rc[1]])
            P.add("dve", TT(crow[0:8, :], crow2[0:8, :], crs[1][0:8, :], ALU.subtract), [rcr, rc[1]], [rcr])
            P.add("dve", CP(crs[2][0:8, :], crow[0:8, :]), [rcr], [rc[2]])
            for k3 in range(3):
                P.add("dve", TS(crsp[k3][0:8, :], crs[k3][0:8, :], -1.0, None, ALU.mult), [rc[k3]], [rcp[k3]])
                P.dma("pool", self.CqR[:, k3, cs], crs[k3][0:8, :], [rc[k3]], [rq], semkey=("crs", l, k3))
                P.dma("pool", self.CkR[:, k3, cs], crsp[k3][0:8, :], [rcp[k3]], [rq], semkey=("crsp", l, k3))

        def b_ropeload():
            P.dma("sp", rope[:, 0, :], self.rope_in[0, :, cs], [], [rrope], semkey=("rope", l))
            P.dma("sp", rope[:, 1, :], self.rope_in[1, :, cs], [], [rrope], semkey=("rope", l))

        def rope_combine(pa, pb, rows, dst, scale):
            ta, tb = st["ropetmp"]
            P.add("dve", TT(ta[0:rows, :], self.ps[pa][0:rows, :], rope[0:rows, 0, :], ALU.mult), [self.psr[pa], rrope], [rtmp] + al)
            P.add("dve", TT(tb[0:rows, :], self.ps[pb][0:rows, :], rope[0:rows, 1, :], ALU.mult), [self.psr[pb], rrope], [rtmp] + al)
            P.add("dve", TT(ta[0:rows, :], ta[0:rows, :], tb[0:rows, :], ALU.add), [rtmp], [rtmp])
            self.evac_store(st, None, rows, dst, scale=scale, eng="act", src=ta, rsrc=rtmp)

        def b_kr():
            pa = self.next_ps(PSP)
            proj_fm(C_KR, 32, pa)
            pb = self.next_ps(PSP)
            proj_fm(C_KRS, 32, pb)
            rope_combine(pa, pb, 32, [(0, 32, self.Kr[:, cs], rq)], 1.0)

        def b_cq(kc):
            pi = self.next_ps(PSP)
            proj_fm(C_CQ + kc * 128, 128, pi)
            P.add("act", ACT(cqs[:, kc, :], self.ps[pi][:], AF.Copy), [self.psr[pi]], [rcqs] + al)

        def b_ckv(kc):
            pi = self.next_ps(PSP)
            proj_fm(C_CKV + kc * 128, 128, pi)
            P.add("act", ACT(ckvs[:, kc, :], self.ps[pi][:], AF.Copy), [self.psr[pi]], [rckvs] + al)

        st_cq = self.stats_steps(cqs, 3, 384.0, rcqs)
        st_ckv = self.stats_steps(ckvs, 2, 256.0, rckvs)

        def b_cqn():
            for kc in range(3):
                g = self.vecs[:, V_QN + 3 * i + kc:V_QN + 3 * i + kc + 1]
                P.add("dve", STT(cqn[:, kc, :], cqs[:, kc, :], g, self.rstd, ALU.mult, ALU.mult), [rcqs, self.rrstd, self.rg], [rcqn])

        def b_ckvn():
            for kc in range(2):
                g = self.vecs[:, V_KVN + 2 * i + kc:V_KVN + 2 * i + kc + 1]
                P.add("dve", STT(ckvn[:, kc, :], ckvs[:, kc, :], g, self.rstd, ALU.mult, ALU.mult), [rckvs, self.rrstd, self.rg], [rckvn])

        def b_qn(j):
            pi = self.next_ps(PSP)
            proj_fm(j * 128, 128, pi, w=wuq, nk=3, src=cqn, rsrc=rcqn)
            self.evac_store(st, pi, 128, [(0, 64, self.Qm[2 * j, 0:64, cs], rq), (64, 128, self.Qm[2 * j + 1, 0:64, cs], rq)], scale=qscale, eng=eng())

        def b_qr(j):
            pa = self.next_ps(PSP)
            proj_fm(512 + j * 128, 128, pa, w=wuq, nk=3, src=cqn, rsrc=rcqn)
            pb = self.next_ps(PSP)
            proj_fm(768 + j * 128, 128, pb, w=wuq, nk=3, src=cqn, rsrc=rcqn)
            rope_combine(pa, pb, 128, [(32 * q, 32 * q + 32, self.Qm[4 * j + q, 64:96, cs], rq) for q in range(4)], qscale)

        def b_kn(j):
            pi = self.next_ps(PSP)
            proj_fm(j * 128, 128, pi, w=wukv, nk=2, src=ckvn, rsrc=rckvn)
            self.evac_store(st, pi, 128, [(0, 128, self.Kmn[j * 128:(j + 1) * 128, cs], rq)], eng=eng())

        def b_v(tt):
            pi = self.next_ps(PSP)
            proj_tm(tt, 512, 512, pi, w=wukv, nk=2, src=ckvn, rsrc=rckvn)
            self.evac_store(st, pi, 128, [(0, 128, self.Vm[t0 + tt * 128:t0 + (tt + 1) * 128, :], rq)], eng=eng())

        b_ropeload()
        for kc in range(3):
            b_cq(kc)
        for kc in range(2):
            b_ckv(kc)
        b_fg()
        st_cq[0](); st_cq[1]()
        b_fq(0); b_fq(1)
        st_cq[2]()
        b_fq(2); b_fq(3)
        b_cqn()
        b_fk(0)
        st_ckv[0](); st_ckv[1]()
        b_cum(0)
        b_fk(1)
        st_ckv[2]()
        b_cum(1)
        b_fk(2)
        b_ckvn()
        b_cum(2)
        b_fk(3)
        b_cum(3)
        b_qn(0)
        b_cpos()
        b_qn(1); b_qn(2)
        b_crow()
        b_qn(3)
        b_fv(0); b_fv(1)
        b_qr(0)
        b_fv(2); b_fv(3)
        b_qr(1)
        for j in range(4):
            b_kn(j)
        for tt in range(4):
            b_v(tt)
        b_kr()

    def phase_T0(self):
        P = self.P
        self.alloc_tok_common()
        xs, rx = self.xss[0]
        hb, rhb = self.hbs[0]
        st = self.alloc_inproj(0)
        self.load_inproj_weights(0, st)
        rxin = P.R("xT_in")
        for c in range(self.NCH):
            self.load_x(c, self.xT_in, rxin, xs, rx)
            for stp in self.prenorm_steps(V_NMP + 0, xs, rx, hb, rhb):
                stp()
            self.inproj_chunk(0, c, st, hb, rhb)

    def phase_T1(self, l):
        P = self.P
        A = self.arena
        i = l // 2
        self.alloc_tok_common(nx=1, nhb=2)
        xs, rx = self.xss[0]
        pf = self.prefetch_T1(l, want_wout=True)
        wout, wup = pf["wout"], pf["wup"]
        at = A.alloc(8 * 512, BF16).rearrange("p (k c) -> p k c", k=8)
        m = A.alloc(8 * 512, F32).rearrange("p (k c) -> p k c", k=8)
        G = [A.alloc(514, F32) for _ in range(3)]
        T = [A.alloc(512, F32) for _ in range(3)]
        GE = [A.alloc(512, F32) for _ in range(2)]
        AC = [A.alloc(512, BF16) for _ in range(3)]
        halo = A.alloc(NJ * 2, F32)
        rwo, rwu = P.R("wout", l), P.R("wup", l)
        rat, rm, rhalo = P.R("at"), P.R("m"), P.R("halo")
        P.add("dve", MSET(halo, 0.0), [], [rhalo])
        rAT, rXT = P.R("AT"), P.R("XT")
        PSP = [0, 1, 2, 3]
        src_x = self.xT_in if l == 0 else self.XT

        def stage1(c):
            hb, rhb = self.hbs[c % 2]
            cs = slice(c * TC, (c + 1) * TC)
            steps = []

            def s_load():
                self.load_x(c, src_x, rXT, xs, rx)
                P.dma("sp", at, self.AT[:, :, cs].rearrange("k p s -> p k s"), [rAT], [rat], semkey="at")
            steps.append(s_load)
            for j in range(8):
                def s_proj(j=j):
                    pi = self.next_ps(PSP)
                    P.add("pe", [MM(self.ps[pi][:], wout[:, kc, j * 128:(j + 1) * 128], at[:, kc, :], kc == 0, kc == 7) for kc in range(8)],
                          [rwo, rat], [self.psr[pi]])
                    P.add("act", ACT(m[:, j, :], self.ps[pi][:], AF.Copy), [self.psr[pi]], [rm])
                steps.append(s_proj)
            steps = self.merge_pairs(self.merge_pairs(steps[0:1] + steps[1:]))
            chain = self.residual_steps(m, rm, V_NMPOST + 8 * l, xs, rx, add_eng="dve")
            chain.append(lambda: self.store_x(c, self.XT, P.R("XTc", l, c), xs, rx))
            chain += self.prenorm_steps(V_NFP + 8 * l, xs, rx, hb, rhb)
            steps += self.spaced(chain, {2: 1, 8: 1})
            return steps

        for stp in stage1(0):
            stp()
        for c in range(self.NCH):
            cs = slice(c * TC, (c + 1) * TC)
            hb, rhb = self.hbs[c % 2]
            self.pending = stage1(c + 1) if c + 1 < self.NCH else []
            racts = P.R("ACTT", l, c)

            def stageA(j):
                pg = self.next_ps(PSP)
                pv = self.next_ps([5, 6, 7])
                P.add("pe", [MM(self.ps[pg][:], wup[:, kc, j * 128:(j + 1) * 128], hb[:, kc, :], kc == 0, kc == 7) for kc in range(8)],
                      [rwu, rhb], [self.psr[pg]])
                P.add("pe", [MM(self.ps[pv][:], wup[:, kc, DFF + j * 128:DFF + (j + 1) * 128], hb[:, kc, :], kc == 0, kc == 7) for kc in range(8)],
                      [rwu, rhb], [self.psr[pv]])
                g, t = G[j % 3], T[j % 3]
                rg_, rt_ = P.R("G", j % 3), P.R("T", j % 3)
                P.add("act", ACT(g[:, 2:514], self.ps[pg][:], AF.Copy), [self.psr[pg]], [rg_])
                P.add("pool", CP(g[:, 0:2], halo[:, 2 * j:2 * j + 2]), [rhalo], [rg_])
                cw = [self.vecs[:, V_CONVW + (l * 3 + tap) * NJ + j:V_CONVW + (l * 3 + tap) * NJ + j + 1] for tap in range(3)]
                cb = self.vecs[:, V_CONVB + l * NJ + j:V_CONVB + l * NJ + j + 1]
                P.add("pool", TS(t, g[:, 2:514], cw[2], cb, ALU.mult, ALU.add), [rg_, self.rg], [rt_])
                P.add("dve", STT(t, g[:, 1:513], cw[1], t, ALU.mult, ALU.add), [rg_, rt_, self.rg], [rt_])
                P.add("dve", STT(t, g[:, 0:512], cw[0], t, ALU.mult, ALU.add), [rg_, rt_, self.rg], [rt_])
                P.add("pool", CP(halo[:, 2 * j:2 * j + 2], g[:, 512:514]), [rg_], [rhalo])
                return pv

            def stageB(j, pv):
                t, ge, ac = T[j % 3], GE[j % 2], AC[j % 3]
                rt_, rge_, rac_ = P.R("T", j % 3), P.R("GE", j % 2), P.R("AC", j % 3)
                P.add("act", ACT(ge, t, AF.Gelu_apprx_tanh), [rt_], [rge_])
                P.add("dve", TT(ac, self.ps[pv][:], ge, ALU.mult), [self.psr[pv], rge_], [rac_])
                P.dma("pool", self.ACTT[j, :, cs], ac, [rac_], [racts], semkey=("AC", j % 3))

            pvs = {}
            pvs[0] = stageA(0)
            for j in range(1, NJ):
                pvs[j] = stageA(j)
                stageB(j - 1, pvs[j - 1])
                self.tick()
            stageB(NJ - 1, pvs[NJ - 1])
            self.flush()
        self.pf = None
        self.arena.top = self.arena.cap

    def phase_T2(self, l):
        P = self.P
        A = self.arena
        last = (l == self.depth - 1)
        nxt_even = (not last) and ((l + 1) % 2 == 0)
        zip_ = not nxt_even
        nbuf = 2 if last else 1
        self.alloc_tok_common(nx=nbuf, nhb=(2 if (zip_ and not last) else 1))
        wdn = A.alloc(NJ * 1024, BF16).rearrange("p (k c) -> p k c", k=NJ)
        acts = [(A.alloc(NJ * 512, BF16).rearrange("p (k c) -> p k c", k=NJ), P.R("act", q)) for q in range(nbuf)]
        fs = [(A.alloc(8 * 512, F32).rearrange("p (k c) -> p k c", k=8), P.R("f", q)) for q in range(nbuf)]
        rwd = P.R("wdn", l)
        self.load_weight(wdn, self.w["ffn_w_down"][l], DFF, 1024, rwd)
        st = None
        if not last:
            if nxt_even:
                st = self.alloc_inproj(l + 1, fbuf=fs[0][0], rf=fs[0][1])
            else:
                st = self.alloc_inproj(l + 1)
            self.load_inproj_weights(l + 1, st)
        rXT, rACTT = P.R("XT2"), P.R("ACTT2")
        PSP = [0, 1, 2, 3]
        alias_res = [st["r_cqs"], st["r_ckvs"], st["r_tmp"]] if nxt_even else []

        def stage_a(c):
            xs, rx = self.xss[c % nbuf]
            act, ract = acts[c % nbuf]
            f, rf = fs[c % nbuf]
            cs = slice(c * TC, (c + 1) * TC)
            steps = []

            def s_load():
                self.load_x(c, self.XT, rXT, xs, rx)
                P.dma("sp", act, self.ACTT[:, :, cs].rearrange("k p s -> p k s"), [rACTT], [ract], semkey=("act", c % nbuf))
            steps.append(s_load)
            for j in range(8):
                def s_proj(j=j):
                    pi = self.next_ps(PSP)
                    P.add("pe", [MM(self.ps[pi][:], wdn[:, kc, j * 128:(j + 1) * 128], act[:, kc, :], kc == 0, kc == NJ - 1) for kc in range(NJ)],
                          [rwd, ract], [self.psr[pi]])
                    P.add("act", ACT(f[:, j, :], self.ps[pi][:], AF.Copy), [self.psr[pi]], [rf] + alias_res)
                steps.append(s_proj)
            return steps

        def stage_b(c):
            xs, rx = self.xss[c % nbuf]
            f, rf = fs[c % nbuf]
            steps = self.residual_steps(f, rf, V_NFPOST + 8 * l, xs, rx, add_eng="dve")
            if last:
                steps.append(lambda: self.store_x(c, self.out, P.R("OUT"), xs, rx))
            else:
                hb, rhb = self.hbs[c % len(self.hbs)]
                steps.append(lambda: self.store_x(c, self.XT, P.R("XTc2", l, c), xs, rx))
                steps += self.prenorm_steps(V_NMP + 8 * (l + 1), xs, rx, hb, rhb)
            return steps

        def run(steps):
            for stp in steps:
                stp()

        if last and not getattr(self, 'dbg_nolastpipe', False):
            run(stage_a(0))
            for c in range(self.NCH):
                sa = stage_a(c + 1) if c + 1 < self.NCH else []
                sb = stage_b(c)
                while sa or sb:
                    if sa:
                        sa.pop(0)()
                    if sb:
                        sb.pop(0)()
        elif last or not zip_:
            for c in range(self.NCH):
                run(stage_a(c))
                run(stage_b(c))
                if not last:
                    hb, rhb = self.hbs[0]
                    self.inproj_chunk(l + 1, c, st, hb, rhb)
        else:
            run(stage_a(0))
            run(stage_b(0))
            for c in range(self.NCH):
                hb, rhb = self.hbs[c % len(self.hbs)]
                if c + 1 < self.NCH:
                    sa = stage_a(c + 1)
                    sb = stage_b(c + 1)
                    self.pending = self.merge_pairs(sa[0:1] + sa[1:]) + self.spaced(sb, {2: 2, 3: 1, 5: 1, 8: 2, 9: 1})
                else:
                    self.pending = []
                self.inproj_chunk(l + 1, c, st, hb, rhb)
                self.flush()

    def phase_ATT(self, l):
        P = self.P
        A = self.arena
        S, NT, NCH = self.S, self.NT, self.NCH
        even = (l % 2 == 0)
        i = l // 2
        nsm = 1 if even else 2
        NSLOT = 2
        slots = []
        for s in range(NSLOT):
            sl = {}
            sl["QA"] = [A.alloc(S, BF16) for _ in range(nsm)]
            sl["KA"] = [A.alloc(S, BF16) for _ in range(nsm)]
            sl["V"] = A.alloc(NT * 128, BF16).rearrange("p (t d) -> p t d", d=128)
            sl["r"] = P.R("slot", l, s)
            sl["key"] = ("slot", l, s)
            slots.append(sl)
        NPT = 6
        PT = [A.alloc(512, BF16) for _ in range(NPT)]
        rPT = [P.R("PT", k) for k in range(NPT)]
        zs = A.alloc(512, F32)
        ab = [A.alloc(512, BF16) for _ in range(2)]
        rzs, rab = P.R("zs"), [P.R("ab", 0), P.R("ab", 1)]
        if even:
            for sl in slots:
                P.add("pool", MSET(sl["V"][:, :, 64:128], 1.0), [], [sl["r"]])
            P.barrier()
        else:
            o1, o2 = A.alloc(512, F32), A.alloc(512, F32)
            r1, r2 = A.alloc(512, F32), A.alloc(512, F32)
            sqo = A.alloc(512, BF16)
            rstd = A.alloc(512, F32)
            ro, ro2, rr1, rr2, rsqo, rrs = [P.R("otail", q) for q in range(6)]
            lt = self.lamt
            lam_init = 0.8 - 0.6 * math.exp(-0.3 * l)
            prod = A.alloc(128, F32)
            rl = P.R("lam", l)
            b = V_LAM + i * 256
            vv = self.vecs
            P.add("dve", TT(prod[:, 0:64], vv[:, b:b + 64], vv[:, b + 64:b + 128], ALU.mult), [self.rg], [rl])
            P.add("dve", TT(prod[:, 64:128], vv[:, b + 128:b + 192], vv[:, b + 192:b + 256], ALU.mult), [self.rg, rl], [rl])
            P.add("dve", ("tensor_reduce", dict(out=lt[:, 0:1], in_=prod[:, 0:64], axis=mybir.AxisListType.X, op=ALU.add)), [rl], [rl])
            P.add("dve", ("tensor_reduce", dict(out=lt[:, 1:2], in_=prod[:, 64:128], axis=mybir.AxisListType.X, op=ALU.add)), [rl], [rl])
            P.add("act", ACT(lt[:, 2:4], lt[:, 0:2], AF.Exp), [rl], [rl])
            P.add("dve", TT(lt[:, 4:5], lt[:, 3:4], lt[:, 2:3], ALU.subtract), [rl], [rl])
            P.add("dve", TS(lt[:, 5:6], lt[:, 4:5], -lam_init, None, ALU.add), [rl], [rl])
            P.add("dve", TS(lt[:, 6:7], vv[:, V_SUBLN + i:V_SUBLN + i + 1], 1.0 - lam_init, None, ALU.mult), [self.rg, rl], [rl])
            P.barrier()
            neglam = lt[:, 5:6]
            subs = lt[:, 6:7]
        rAT = P.R("AT_w", l)
        rsrc = P.R("attsrc", l)
        cpos3 = self.cpos.rearrange("p (t h) -> p t h", h=8)
        alibi = self.cf[:, 384:384 + 8 * NT].rearrange("p (h t) -> p h t", h=8)
        units = list(range(16)) if even else list(range(8))
        SPS = [0, 1, 2]
        VSPL = 8

        def load_v(sl, dst_cols, src2d):
            v3 = src2d.rearrange("(t p) d -> p t d", p=128)
            for t0 in range(0, NT, VSPL):
                P.dma("sp", sl["V"][:, t0:t0 + VSPL, dst_cols], v3[:, t0:t0 + VSPL, :], [rsrc], [sl["r"]], semkey=sl["key"])

        def load_unit(u, sl):
            r, key = sl["r"], sl["key"]
            if even:
                if u < 8:
                    h = u
                    P.dma("sp", sl["QA"][0][0:64, :], self.Qf[h * 64:(h + 1) * 64, :], [rsrc], [r], semkey=key)
                    P.dma("sp", sl["QA"][0][64:67, :], self.CqR[h], [rsrc], [r], semkey=key)
                    P.dma("sp", sl["KA"][0][0:64, :], self.Kf[h * 64:(h + 1) * 64, :], [rsrc], [r], semkey=key)
                    P.dma("sp", sl["KA"][0][64:67, :], self.onesrow_in, [rsrc], [r], semkey=key)
                    P.dma("sp", sl["QA"][0][67:70, :], self.onesrow_in, [rsrc], [r], semkey=key)
                    P.dma("sp", sl["KA"][0][67:70, :], self.CkR[h], [rsrc], [r], semkey=key)
                    load_v(sl, slice(0, 64), self.Vf[:, h * 64:(h + 1) * 64])
                else:
                    h = u - 8
                    P.dma("sp", sl["QA"][0][0:96, :], self.Qm[h], [rsrc], [r], semkey=key)
                    P.dma("sp", sl["KA"][0][0:64, :], self.Kmn[h * 64:(h + 1) * 64, :], [rsrc], [r], semkey=key)
                    P.dma("sp", sl["KA"][0][64:96, :], self.Kr, [rsrc], [r], semkey=key)
                    load_v(sl, slice(0, 64), self.Vm[:, h * 64:(h + 1) * 64])
            else:
                h = u
                for s in range(2):
                    P.dma("sp", sl["QA"][s][0:64, :], self.Qd[h * 128 + s * 64:h * 128 + (s + 1) * 64, :], [rsrc], [r], semkey=key)
                    P.dma("sp", sl["QA"][s][64:66, :], self.aq_in[h], [rsrc], [r], semkey=key)
                    P.dma("sp", sl["KA"][s][0:64, :], self.Kd[h * 128 + s * 64:h * 128 + (s + 1) * 64, :], [rsrc], [r], semkey=key)
                    P.dma("sp", sl["KA"][s][64:66, :], self.onesrow_in[0:2, :], [rsrc], [r], semkey=key)
                    P.dma("sp", sl["QA"][s][66:68, :], self.onesrow_in[0:2, :], [rsrc], [r], semkey=key)
                    P.dma("sp", sl["KA"][s][66:68, :], self.ak_in[h], [rsrc], [r], semkey=key)
                load_v(sl, slice(0, 128), self.Vd[:, h * 128:(h + 1) * 128])

        load_unit(units[0], slots[0])
        deferred = []
        zhold = [None, None]
        zq = []
        psz_i = 0
        PSZ = [A.alloc(512, BF16) for _ in range(3)]
        rPSZ = [P.R("PSZ", q) for q in range(3)]
        pt_i = 0
        sp_i = 0
        tail_i = 0
        for ui, u in enumerate(units):
            sl = slots[ui % NSLOT]
            if ui + 1 < len(units):
                load_unit(units[ui + 1], slots[(ui + 1) % NSLOT])
            if ui == 0:
                self.prefetch_T1(l, want_wout=even, after=[slots[0]["r"], slots[1]["r"]])
            kdim = (70 if u < 8 else 96) if even else 68
            QA, KA, V, rs_ = sl["QA"], sl["KA"], sl["V"], sl["r"]
            for c in range(NCH):
                cs = slice(c * TC, (c + 1) * TC)
                nkt = 4 * c + 4
                acc = [3 + (tail_i % 2)] if even else [3, 4, 5, 6]
                items = [(kt, s) for s in range(nsm) for kt in range(nkt)]
                LOOK = 2
                pend = []
                n = len(items)
                for t in range(n + LOOK):
                    if t < n:
                        kt, s = items[t]
                        spi = SPS[sp_i % 3]
                        sp_i += 1
                        j = kt - 4 * c
                        q0 = 0 if j <= 0 else 128 * j
                        ps = self.ps[spi]
                        ins = [MM(ps[:, q0:512], KA[s][0:kdim, kt * 128:(kt + 1) * 128], QA[s][0:kdim, c * TC + q0:(c + 1) * TC], True, j < 0)]
                        if j >= 0:
                            ins.append(MM(ps[:, q0:q0 + 128], self.ident_bf, self.maskT, False, True))
                        P.add("pe", ins, [rs_, self.rg], [self.psr[spi]])
                        pend.append((kt, s, spi, q0))
                    if t >= LOOK:
                        kt, s, spi, q0 = pend.pop(0)
                        pti = pt_i % NPT
                        pt_i += 1
                        ps = self.ps[spi]
                        rd = [self.psr[spi], self.rg]
                        P.add("act", ACT(PT[pti][:, q0:512], ps[:, q0:512], AF.Exp, bias=0.0), rd, [rPT[pti]])
                        first, lastk = (kt == 0), (kt == nkt - 1)
                        if even:
                            P.add("pe", MM(self.ps[acc[0]][:, q0:512], V[:, kt, :], PT[pti][:, q0:512], first, lastk),
                                  [rs_, rPT[pti]], [self.psr[acc[0]]])
                        else:
                            P.add("pe", MM(self.ps[acc[s]][:, q0:512], V[:, kt, :], PT[pti][:, q0:512], first, lastk),
                                  [rs_, rPT[pti]], [self.psr[acc[s]]])
                            while zq and zq[0][0] < t:
                                zq.pop(0)[1]()
                            if kt < 4 * c and kt % 2 == 0:
                                zhold[s] = pti
                            elif kt < 4 * c:
                                pa = zhold[s]
                                zi = psz_i % len(PSZ)
                                psz_i += 1
                                P.add("dve", TT(PSZ[zi], PT[pa], PT[pti], ALU.add), [rPT[pa], rPT[pti]], [rPSZ[zi]])
                                zq.append((t, lambda zi=zi, s=s, kt=kt: P.add(
                                    "pe", MM(self.ps[acc[2 + s]][:], self.ones_bf, PSZ[zi], kt == 1, False),
                                    [rPSZ[zi], self.rg], [self.psr[acc[2 + s]]])))
                            else:
                                P.add("pe", MM(self.ps[acc[2 + s]][:, q0:512], self.ones_bf, PT[pti][:, q0:512], first, lastk),
                                      [rPT[pti], self.rg], [self.psr[acc[2 + s]]])
                        if (not even) and lastk:
                            if s == 0:
                                while deferred:
                                    deferred.pop(0)()
                                P.add("dve", CP(o1, self.ps[3][:]), [self.psr[3]], [ro])
                                P.add("act", ACT(r1, self.ps[5][:], AF.Ln), [self.psr[5]], [rr1])
                            else:
                                P.add("dve", CP(o2, self.ps[4][:]), [self.psr[4]], [ro2])
                                P.add("act", ACT(r2, self.ps[6][:], AF.Ln), [self.psr[6]], [rr2])
                        if deferred and t >= LOOK + 1:
                            deferred.pop(0)()
                if even:
                    while deferred:
                        deferred.pop(0)()
                k = tail_i % 2
                tail_i += 1
                steps = []
                if even:
                    a = acc[0]
                    steps.append(lambda a=a: P.add("dve", RCP(zs[64:128, :], self.ps[a][64:128, :]), [self.psr[a]], [rzs]))
                    kc = (u // 2) if u < 8 else 4 + (u - 8) // 2
                    po = (u % 2) * 64

                    def fin(a=a, k=k, kc=kc, po=po, cs=cs):
                        P.add("dve", TT(ab[k][0:64, :], self.ps[a][0:64, :], zs[64:128, :], ALU.mult), [self.psr[a], rzs], [rab[k]])
                        P.dma("pool", self.AT[kc, po:po + 64, cs], ab[k][0:64, :], [rab[k]], [rAT], semkey=("ab", k))
                    steps.append(fin)
                else:
                    steps.append(lambda: P.add("act", ACT(r1, r1, AF.Exp, scale=-1.0), [rr1], [rr1]))
                    steps.append(lambda: P.add("act", ACT(r2, r2, AF.Exp, scale=-1.0), [rr2], [rr2]))
                    steps.append(lambda: P.add("dve", TT(o1, o1, r1, ALU.mult), [ro, rr1], [ro]))
                    steps.append(lambda: P.add("dve", TT(o2, o2, r2, ALU.mult), [ro2, rr2], [ro2]))
                    steps.append(lambda: P.add("dve", STT(o1, o2, neglam, o1, ALU.mult, ALU.add), [ro, ro2], [ro]))
                    steps.append(lambda: P.add("act", ACT(sqo, o1, AF.Square), [ro], [rsqo]))
                    steps.append(lambda: P.add("pe", MM(self.ps[7][:], self.ones_bf, sqo, True, True), [rsqo, self.rg], [self.psr[7]]))
                    steps.append(lambda: P.add("act", ACT(rstd, self.ps[7][:], AF.Ln, scale=1.0 / 128, bias=self.epsb), [self.psr[7], self.rg], [rrs]))
                    steps.append(lambda: P.add("act", ACT(rstd, rstd, AF.Exp, scale=-0.5), [rrs], [rrs]))

                    def fin(k=k, u=u, cs=cs):
                        P.add("dve", STT(ab[k], o1, subs, rstd, ALU.mult, ALU.mult), [ro, rrs], [rab[k]])
                        P.dma("pool", self.AT[u, :, cs], ab[k], [rab[k]], [rAT], semkey=("ab", k))
                    steps.append(fin)
                deferred.extend(steps)
        while deferred:
            deferred.pop(0)()

    def build(self, stop_after=None, only=None):
        P = self.P
        self.init_globals()
        P.barrier()
        seq = [("T0", 0)]
        for l in range(self.depth):
            seq += [("ATT", l), ("T1", l), ("T2", l)]
        for (ph, l) in seq:
            if only is not None and (ph, l) not in only:
                continue
            if ph == "T0":
                self.phase_T0()
            elif ph == "ATT":
                self.phase_ATT(l)
            elif ph == "T1":
                self.phase_T1(l)
            else:
                self.phase_T2(l)
            self.phase_reset()
            if stop_after == (ph, l):
                break
        P.lower()
        return self.nc


_CACHE = {}
SHARED_KEYS = ["even_w_in", "even_w_uq", "even_w_ukv", "even_w_out", "odd_w_in", "odd_w_out", "ffn_w_up", "ffn_w_down"]


def make_in_maps(inputs, S, B):
    consts = _CACHE.get(("consts", S))
    if consts is None:
        consts = make_consts(S)
        _CACHE[("consts", S)] = consts
    shared = {"vecs": pack_vecs(inputs)}
    for k in SHARED_KEYS:
        shared[k] = np.ascontiguousarray(np.asarray(inputs[k], np.float32))
    for k in ["cbf", "cf", "rope", "aq", "ak", "onesrow"]:
        shared[k] = consts[k]
    in_maps = []
    for b in range(B):
        m = dict(shared)
        x = np.asarray(inputs["x"][b], np.float32)
        m["xT"] = np.ascontiguousarray(x.T).reshape(8, 128, S)
        in_maps.append(m)
    return in_maps


def kernel(**inputs):
    x = np.asarray(inputs["x"])
    B, S, _ = x.shape
    bld = Builder(S)
    nc = bld.build()
    in_maps = make_in_maps(inputs, S, B)
    res = run_bass_kernel_spmd(nc, in_maps, core_ids=list(range(B)))
    out = np.empty((B, S, D), np.float32)
    for b in range(B):
        out[b] = np.asarray(res.results[b]["outT"]).reshape(D, S).T
    return out
```

```python
import math
import numpy as np
import ml_dtypes
import concourse.bass as bass
import concourse.mybir as mybir
from concourse.bass_utils import run_bass_kernel_spmd

F32 = mybir.dt.float32
BF16 = mybir.dt.bfloat16
AF = mybir.ActivationFunctionType
ALU = mybir.AluOpType

D = 1024
DEPTH = 4
DFF = 2816
NJ = DFF // 128
EPS = 1e-6
TC = 512
ARENA_BYTES = 206 * 1024
NEG = -30000.0
EVEN_W = 2216
C_FQ, C_FK, C_FV, C_FG, C_CQ, C_CKV, C_KR, C_KRS = 0, 512, 1024, 1536, 1544, 1928, 2184, 2216
EVEN_WT = 2248


class Res:
    __slots__ = ("name", "writers", "readers")

    def __init__(self, name):
        self.name = name
        self.writers = []
        self.readers = []


class Op:
    __slots__ = ("eng", "fn", "deps", "dma", "semkey")

    def __init__(self, eng, fn, deps, dma, semkey):
        self.eng = eng
        self.fn = fn
        self.deps = deps
        self.dma = dma
        self.semkey = semkey


class Prog:
    def __init__(self, nc):
        self.nc = nc
        self.ops = []
        self.res = {}
        self.last_eng = {}
        self.last_dma = {}
        self.engobj = {"pe": nc.tensor, "act": nc.scalar, "dve": nc.vector, "pool": nc.gpsimd, "sp": nc.sync}

    def R(self, *key):
        r = self.res.get(key)
        if r is None:
            r = Res(key)
            self.res[key] = r
        return r

    def add(self, eng, insts, reads=(), writes=(), dma=False, semkey=None):
        if isinstance(insts, tuple):
            insts = [insts]
        i = len(self.ops)
        deps = {}
        for r in reads:
            for w in r.writers:
                deps[w] = "raw"
        for r in writes:
            for x in r.readers:
                deps.setdefault(x, "war")
        fdeps = []
        for d, kind in deps.items():
            od = self.ops[d]
            if not dma and not od.dma and od.eng == eng:
                if eng == "pe" or kind != "raw":
                    continue
            fdeps.append(d)

        def push(lst):
            if lst and not dma:
                o = self.ops[lst[-1]]
                if not o.dma and o.eng == eng:
                    lst[-1] = i
                    return
            lst.append(i)

        for r in writes:
            if r.readers or (r in reads):
                r.writers = [i]
                r.readers = []
            else:
                push(r.writers)
        for r in reads:
            if r not in writes:
                push(r.readers)
        self.ops.append(Op(eng, insts, fdeps, dma, semkey))
        if dma:
            self.last_dma[semkey] = i
        else:
            self.last_eng[eng] = i
        return i

    def dma(self, queue, out, in_, reads, writes, semkey):
        return self.add(queue, [("dma_start", dict(out=out, in_=in_))], reads, writes, dma=True, semkey=semkey)

    def barrier(self):
        lasts = list(self.last_eng.values()) + list(self.last_dma.values())
        for eng in ("pe", "act", "dve", "pool", "sp"):
            i = len(self.ops)
            deps = [d for d in lasts if not (not self.ops[d].dma and self.ops[d].eng == eng)]
            self.ops.append(Op(eng, None, deps, False, None))
        for r in self.res.values():
            r.writers = []
            r.readers = []
        self.last_dma = {}

    def lower(self):
        nc = self.nc
        n = len(self.ops)
        needed = [False] * n
        for op in self.ops:
            for d in op.deps:
                needed[d] = True
        sems = {}
        cnt = {}

        def getsem(key):
            s = sems.get(key)
            if s is None:
                s = nc.alloc_semaphore("s%d" % len(sems))
                sems[key] = s
                cnt[key] = 0
            return s

        sig = [None] * n
        waited = {}
        nwait = 0
        for i, op in enumerate(self.ops):
            e = self.engobj[op.eng]
            reqs = {}
            for d in op.deps:
                k, v = sig[d]
                if reqs.get(k, 0) < v:
                    reqs[k] = v
            for k, v in reqs.items():
                if waited.get((op.eng, k), 0) >= v:
                    continue
                e.wait_ge(sems[k], v)
                nwait += 1
                waited[(op.eng, k)] = v
            if op.fn is None:
                continue
            ins = None
            for (nm, kw) in op.fn:
                ins = getattr(e, nm)(**kw)
            if needed[i] or op.dma:
                if op.dma:
                    key = ("dma", op.semkey)
                    s = getsem(key)
                    cnt[key] += 16
                    ins.then_inc(s, 16)
                else:
                    key = ("eng", op.eng)
                    s = getsem(key)
                    cnt[key] += 1
                    ins.then_inc(s, 1)
                sig[i] = (key, cnt[key])
        self.stats = dict(nops=n, nwait=nwait, nsem=len(sems))


class Arena:
    def __init__(self, nc, nbytes):
        self.t = nc.alloc_sbuf_tensor("arena", [128, nbytes // 4], F32)
        self.cap = nbytes
        self.off = 0
        self.top = nbytes

    def alloc(self, nelem, dtype, top=False):
        nb = nelem * (4 if dtype == F32 else 2)
        nb = (nb + 63) // 64 * 64
        if top:
            self.top -= nb
            o = self.top
        else:
            o = self.off
            self.off += nb
        assert self.off <= self.top, ("SBUF arena overflow", self.off, self.top)
        ap = self.t[:, o // 4:(o + nb) // 4]
        if dtype != F32:
            ap = ap.bitcast(dtype)
        return ap[:, :nelem]


def _bf16_split(a, n):
    outs = []
    r = a.astype(np.float64)
    for _ in range(n):
        h = r.astype(np.float32).astype(ml_dtypes.bfloat16)
        outs.append(h)
        r = r - h.astype(np.float64)
    return outs


def make_consts(S):
    NT = S // 128
    c = {}
    k = np.arange(128)
    cb = np.zeros((128, 384), np.float32)
    cb[:, 0:128] = 1.0
    cb[:, 128:256] = np.eye(128, dtype=np.float32)
    cb[:, 256:384] = np.where(k[:, None] <= k[None, :], 0.0, NEG)
    c["cbf"] = cb.astype(ml_dtypes.bfloat16)
    slopes = 2.0 ** (-8.0 * np.arange(1, 9, dtype=np.float64) / 8)
    cf = np.zeros((128, 384 + 8 * NT), np.float32)
    cf[:, 0:128] = (k[:, None] <= k[None, :]).astype(np.float32)
    cf[:, 128:256] = 1.0
    cf[:, 256:384] = np.eye(128, dtype=np.float32)
    pos = (np.arange(NT)[None, :] * 128 + k[:, None]).astype(np.float64)
    for h in range(8):
        cf[:, 384 + h * NT:384 + (h + 1) * NT] = (slopes[h] * pos).astype(np.float32)
    c["cf"] = cf
    inv = (10000.0 ** (-np.arange(0, 32, 2, dtype=np.float32) / 32)).astype(np.float32)
    ang = (np.arange(S, dtype=np.float32)[:, None] * inv[None, :]).astype(np.float32)
    cos = np.cos(ang).astype(np.float32).T
    sin = np.sin(ang).astype(np.float32).T
    cs = np.concatenate([cos, cos], 0)
    ss = np.concatenate([-sin, sin], 0)
    rope = np.zeros((2, 128, S), np.float32)
    rope[0] = np.tile(cs, (4, 1))
    rope[1] = np.tile(ss, (4, 1))
    c["rope"] = rope
    t = np.arange(S, dtype=np.float64)
    aq = np.zeros((8, 2, S), ml_dtypes.bfloat16)
    for h in range(8):
        hi, lo = _bf16_split(-slopes[h] * t, 2)
        aq[h, 0] = hi
        aq[h, 1] = lo
    c["aq"] = aq
    ak = np.zeros((8, 2, S), ml_dtypes.bfloat16)
    for h in range(8):
        hi, lo = _bf16_split(slopes[h] * t, 2)
        ak[h, 0] = hi
        ak[h, 1] = lo
    c["ak"] = ak
    c["onesrow"] = np.ones((3, S), ml_dtypes.bfloat16)
    return c


V_NMP, V_NMPOST, V_NFP, V_NFPOST = 0, 32, 64, 96
V_QN = 128
V_KVN = 134
V_SUBLN = 138
V_CONVW = 140
V_CONVB = 404
V_BF = 492
V_LAM = 556
NVEC = 1068


def pack_vecs(inp):
    v = np.zeros((128, NVEC), np.float32)

    def pl(a):
        return np.asarray(a, np.float32).reshape(8, 128).T

    for l in range(DEPTH):
        v[:, V_NMP + 8 * l:V_NMP + 8 * l + 8] = pl(inp["norm_mix_pre"][l])
        v[:, V_NMPOST + 8 * l:V_NMPOST + 8 * l + 8] = pl(inp["norm_mix_post"][l])
        v[:, V_NFP + 8 * l:V_NFP + 8 * l + 8] = pl(inp["norm_ffn_pre"][l])
        v[:, V_NFPOST + 8 * l:V_NFPOST + 8 * l + 8] = pl(inp["norm_ffn_post"][l])
        cw = np.asarray(inp["ffn_conv_w"][l], np.float32)
        for tap in range(3):
            v[:, V_CONVW + (l * 3 + tap) * NJ:V_CONVW + (l * 3 + tap + 1) * NJ] = cw[tap].reshape(NJ, 128).T
        v[:, V_CONVB + l * NJ:V_CONVB + (l + 1) * NJ] = np.asarray(inp["ffn_conv_b"][l], np.float32).reshape(NJ, 128).T
    for i in range(2):
        v[:, V_QN + 3 * i:V_QN + 3 * i + 3] = np.asarray(inp["even_q_norm"][i], np.float32).reshape(3, 128).T
        v[:, V_KVN + 2 * i:V_KVN + 2 * i + 2] = np.asarray(inp["even_kv_norm"][i], np.float32).reshape(2, 128).T
        v[:, V_SUBLN + i] = np.asarray(inp["odd_subln"][i], np.float32)
        v[:, V_BF + 32 * i:V_BF + 32 * i + 32] = np.tile(np.asarray(inp["even_b_forget"][i], np.float32), 4)[None, :]
        for j, nm in enumerate(["odd_lambda_q1", "odd_lambda_k1", "odd_lambda_q2", "odd_lambda_k2"]):
            o = V_LAM + (i * 4 + j) * 64
            v[:, o:o + 64] = np.asarray(inp[nm][i], np.float32)[None, :]
    return v


def ACT(out, in_, func, **kw):
    return ("activation", dict(out=out, in_=in_, func=func, **kw))


def MM(out, lhsT, rhs, start=True, stop=True):
    return ("matmul", dict(out=out, lhsT=lhsT, rhs=rhs, start=start, stop=stop))


def TT(out, in0, in1, op):
    return ("tensor_tensor", dict(out=out, in0=in0, in1=in1, op=op))


def TS(out, in0, s1, s2, op0, op1=None):
    kw = dict(out=out, in0=in0, scalar1=s1, scalar2=s2, op0=op0)
    if op1 is not None:
        kw["op1"] = op1
    return ("tensor_scalar", kw)


def STT(out, in0, scalar, in1, op0, op1):
    return ("scalar_tensor_tensor", dict(out=out, in0=in0, scalar=scalar, in1=in1, op0=op0, op1=op1))


def CP(out, in_):
    return ("tensor_copy", dict(out=out, in_=in_))


def MSET(ap, v):
    return ("memset", dict(ap=ap, constant=v))


def RCP(out, in_):
    return ("reciprocal", dict(out=out, in_=in_))


class Builder:
    def __init__(self, S, depth=DEPTH, dbg=()):
        self.S = S
        self.NT = S // 128
        self.NCH = S // TC
        self.depth = depth
        nc = self.nc = bass.Bass("TRN2", target_bir_lowering=False)
        self.P = Prog(nc)
        self.dbg = set(dbg)
        NT = self.NT

        def din(name, shape, dt):
            return nc.dram_tensor(name, list(shape), dt, kind="ExternalInput").ap()

        def dscr(name, shape, dt):
            kind = "ExternalOutput" if name in self.dbg else "Internal"
            return nc.dram_tensor(name, list(shape), dt, kind=kind).ap()

        self.xT_in = din("xT", [8, 128, S], F32)
        self.vecs_in = din("vecs", [128, NVEC], F32)
        self.cbf_in = din("cbf", [128, 384], BF16)
        self.cf_in = din("cf", [128, 384 + 8 * NT], F32)
        self.rope_in = din("rope", [2, 128, S], F32)
        self.aq_in = din("aq", [8, 2, S], BF16)
        self.ak_in = din("ak", [8, 2, S], BF16)
        self.onesrow_in = din("onesrow", [3, S], BF16)
        self.w = {}
        self.w["even_w_in"] = din("even_w_in", [2, 1024, EVEN_W], F32)
        self.w["even_w_uq"] = din("even_w_uq", [2, 384, 768], F32)
        self.w["even_w_ukv"] = din("even_w_ukv", [2, 256, 1024], F32)
        self.w["even_w_out"] = din("even_w_out", [2, 1024, 1024], F32)
        self.w["odd_w_in"] = din("odd_w_in", [2, 1024, 3072], F32)
        self.w["odd_w_out"] = din("odd_w_out", [2, 1024, 1024], F32)
        self.w["ffn_w_up"] = din("ffn_w_up", [4, 1024, 2 * DFF], F32)
        self.w["ffn_w_down"] = din("ffn_w_down", [4, DFF, 1024], F32)
        self.out = nc.dram_tensor("outT", [8, 128, S], F32, kind="ExternalOutput").ap()
        self.XT = dscr("XT", [8, 128, S], F32)
        self.AT = dscr("AT", [8, 128, S], BF16)
        self.ACTT = dscr("ACTT", [NJ, 128, S], BF16)
        self.Qf = dscr("Qf", [512, S], BF16)
        self.Kf = dscr("Kf", [512, S], BF16)
        self.Vf = dscr("Vf", [S, 512], BF16)
        self.CqR = dscr("CqR", [8, 3, S], BF16)
        self.CkR = dscr("CkR", [8, 3, S], BF16)
        self.Qm = dscr("Qm", [8, 96, S], BF16)
        self.Kmn = dscr("Kmn", [512, S], BF16)
        self.Kr = dscr("Kr", [32, S], BF16)
        self.Vm = dscr("Vm", [S, 512], BF16)
        self.Qd = dscr("Qd", [1024, S], BF16)
        self.Kd = dscr("Kd", [1024, S], BF16)
        self.Vd = dscr("Vd", [S, 1024], BF16)

        self.arena = Arena(nc, ARENA_BYTES)
        A = self.arena
        self.vecs = A.alloc(NVEC, F32)
        self.cbf = A.alloc(384, BF16)
        self.cf = A.alloc(384 + 8 * NT, F32)
        self.cpos = A.alloc(NT * 8, F32)
        self.lamt = A.alloc(16, F32)
        self.epsb = A.alloc(2, F32)[:, 0:1]
        self.ones_bf = self.cbf[:, 0:128]
        self.ident_bf = self.cbf[:, 128:256]
        self.maskT = self.cbf[:, 256:384]
        self.U_f = self.cf[:, 0:128]
        self.ones_f = self.cf[:, 128:256]
        self.ident_f = self.cf[:, 256:384]
        self.phase_base = A.off
        self.ps = [nc.alloc_psum_tensor("ps%d" % i, [128, 512], F32) for i in range(8)]
        self.psr = [self.P.R("ps", i) for i in range(8)]
        self.ps_rrs = {}

    def phase_reset(self):
        self.P.barrier()
        self.arena.off = self.phase_base

    def next_ps(self, pool):
        key = tuple(pool)
        n = self.ps_rrs.get(key, 0)
        self.ps_rrs[key] = n + 1
        return pool[n % len(pool)]

    def load_weight(self, dst, src, K, ncols, res, c0=0, dst_c0=0, after=()):
        for kc in range(K // 128):
            for s0 in range(0, ncols, 2048):
                s1 = min(ncols, s0 + 2048)
                self.P.dma("pool", dst[:, kc, dst_c0 + s0:dst_c0 + s1], src[kc * 128:(kc + 1) * 128, c0 + s0:c0 + s1],
                           list(after) if (kc == 0 and s0 == 0) else [], [res], semkey=res.name)

    def prefetch_T1(self, l, want_wout, after=()):
        pf = getattr(self, "pf", None)
        if pf is None or pf["l"] != l:
            pf = self.pf = {"l": l, "wout": None, "wup": None}
        A, P = self.arena, self.P
        i = l // 2
        if pf["wup"] is None:
            pf["wup"] = A.alloc(8 * 2 * DFF, BF16, top=True).rearrange("p (k c) -> p k c", k=8)
            self.load_weight(pf["wup"], self.w["ffn_w_up"][l], 1024, 2 * DFF, P.R("wup", l), after=after)
        if want_wout and pf["wout"] is None:
            pf["wout"] = A.alloc(8 * 1024, BF16, top=True).rearrange("p (k c) -> p k c", k=8)
            self.load_weight(pf["wout"], self.w["even_w_out" if l % 2 == 0 else "odd_w_out"][i], 1024, 1024, P.R("wout", l))
        return pf

    def init_globals(self):
        P = self.P
        rg = self.rg = P.R("globals")
        P.dma("sp", self.vecs, self.vecs_in, [], [rg], semkey="g")
        P.dma("sp", self.cbf, self.cbf_in, [], [rg], semkey="g")
        P.dma("sp", self.cf, self.cf_in, [], [rg], semkey="g")
        P.add("dve", MSET(self.epsb, EPS), [], [rg])

    @staticmethod
    def merge_pairs(steps):
        out = []
        for q in range(0, len(steps), 2):
            grp = steps[q:q + 2]
            out.append(lambda grp=grp: [g() for g in grp])
        return out

    @staticmethod
    def spaced(steps, gaps):
        out = []
        for q, stp in enumerate(steps):
            out += [(lambda: None)] * gaps.get(q, 0)
            out.append(stp)
        return out

    def tick(self):
        if self.pending:
            self.pending.pop(0)()

    def flush(self):
        while self.pending:
            self.pending.pop(0)()

    def alloc_tok_common(self, nx=1, nhb=1):
        A = self.arena
        P = self.P
        self.pending = []
        self.xss = [(A.alloc(8 * 512, F32).rearrange("p (k c) -> p k c", k=8), P.R("xs", q)) for q in range(nx)]
        self.hbs = [(A.alloc(8 * 512, BF16).rearrange("p (k c) -> p k c", k=8), P.R("hb", q)) for q in range(nhb)]
        self.sq = A.alloc(8 * 512, BF16).rearrange("p (k c) -> p k c", k=8)
        self.rsq = P.R("sq")
        self.rstd = A.alloc(512, F32)
        self.rrstd = P.R("rstd")

    def stats_steps(self, src, nk, n, r_src, ps_i=4):
        P = self.P
        ps, psr = self.ps[ps_i], self.psr[ps_i]
        sq, rsq, rstd, rr = self.sq, self.rsq, self.rstd, self.rrstd
        steps = []
        h = (nk + 1) // 2
        for (k0, k1) in ((0, h), (h, nk)):
            if k1 > k0:
                steps.append(lambda k0=k0, k1=k1: P.add("act", ACT(sq[:, k0:k1, :], src[:, k0:k1, :], AF.Square), [r_src], [rsq]))

        def fin():
            P.add("pe", [MM(ps[:], self.ones_bf, sq[:, kc, :], kc == 0, kc == nk - 1) for kc in range(nk)], [rsq, self.rg], [psr])
            P.add("act", ACT(rstd, ps[:], AF.Ln, scale=1.0 / n, bias=self.epsb), [psr, self.rg], [rr])
            P.add("act", ACT(rstd, rstd, AF.Exp, scale=-0.5), [rr], [rr])
        steps.append(fin)
        return steps

    def stats_rstd(self, src, nk, n, r_src, ps_i=4):
        for st in self.stats_steps(src, nk, n, r_src, ps_i):
            st()

    def prenorm_steps(self, gcol0, xs, rx, hb, rhb):
        steps = self.stats_steps(xs, 8, 1024.0, rx)
        for half in range(2):
            def dv(half=half):
                for kc in range(half * 4, half * 4 + 4):
                    g = self.vecs[:, gcol0 + kc:gcol0 + kc + 1]
                    self.P.add("dve", STT(hb[:, kc, :], xs[:, kc, :], g, self.rstd, ALU.mult, ALU.mult),
                               [rx, self.rrstd, self.rg], [rhb])
            steps.append(dv)
        return steps

    def residual_steps(self, m, rm, gcol0, xs, rx, add_eng="pool"):
        P = self.P
        steps = self.stats_steps(m, 8, 1024.0, rm)
        for half in range(2):
            def dv(half=half):
                for kc in range(half * 4, half * 4 + 4):
                    g = self.vecs[:, gcol0 + kc:gcol0 + kc + 1]
                    P.add("dve", STT(m[:, kc, :], m[:, kc, :], g, self.rstd, ALU.mult, ALU.mult), [rm, self.rrstd, self.rg], [rm])
                sl = slice(half * 4, half * 4 + 4)
                P.add(add_eng, TT(xs[:, sl, :], xs[:, sl, :], m[:, sl, :], ALU.add), [rx, rm], [rx])
            steps.append(dv)
        return steps

    def load_x(self, c, src, rsrc, xs, rx, key="xs"):
        cs = slice(c * TC, (c + 1) * TC)
        self.P.dma("sp", xs, src[:, :, cs].rearrange("k p s -> p k s"), [rsrc], [rx], semkey=(key, rx.name))

    def store_x(self, c, dst, rdst, xs, rx):
        cs = slice(c * TC, (c + 1) * TC)
        self.P.dma("sp", dst[:, :, cs].rearrange("k p s -> p k s"), xs, [rx], [rdst], semkey=("xs_st", rx.name))

    def alloc_inproj(self, l, fbuf=None, rf=None):
        A = self.arena
        P = self.P
        st = {}
        if l % 2 == 0:
            st["win"] = A.alloc(8 * EVEN_WT, BF16).rearrange("p (k c) -> p k c", k=8)
            st["wuq"] = A.alloc(3 * 1024, BF16).rearrange("p (k c) -> p k c", k=3)
            st["wukv"] = A.alloc(2 * 1024, BF16).rearrange("p (k c) -> p k c", k=2)
            aliased = fbuf is not None
            if fbuf is None:
                fbuf = A.alloc(8 * 512, F32).rearrange("p (k c) -> p k c", k=8)
            st["cqs"] = fbuf[:, 0:3, :]
            st["ckvs"] = fbuf[:, 3:5, :]
            st["ropetmp"] = [fbuf[:, 5, :], fbuf[:, 6, :]]
            st["rf"] = rf if aliased else None
            st["r_cqs"], st["r_ckvs"], st["r_tmp"] = P.R("cqs", l), P.R("ckvs", l), P.R("ropetmp", l)
            st["cqn"] = A.alloc(3 * 512, BF16).rearrange("p (k c) -> p k c", k=3)
            st["ckvn"] = A.alloc(2 * 512, BF16).rearrange("p (k c) -> p k c", k=2)
            st["rope"] = A.alloc(2 * 512, F32).rearrange("p (k c) -> p k c", k=2)
            st["fg"] = A.alloc(32, F32)
            st["lf"] = A.alloc(32, F32)
            st["run"] = A.alloc(8, F32)
            st["crow"] = A.alloc(512, F32)
            st["crow2"] = A.alloc(512, F32)
            st["crs"] = [A.alloc(512, BF16) for _ in range(3)]
            st["crsp"] = [A.alloc(512, BF16) for _ in range(3)]
        else:
            st["win"] = A.alloc(8 * 3072, BF16).rearrange("p (k c) -> p k c", k=8)
        st["stage"] = [A.alloc(512, BF16) for _ in range(4)]
        st["stage_i"] = 0
        st["rw"] = P.R("w_inproj", l)
        st["id"] = l
        return st

    def load_inproj_weights(self, l, st):
        i = l // 2
        P = self.P
        rw = st["rw"]
        key = rw.name
        if l % 2 == 0:
            w = self.w["even_w_in"][i]
            self.load_weight(st["win"], w, 1024, EVEN_W, rw)
            for kc in range(8):
                P.dma("pool", st["win"][:, kc, C_KRS:C_KRS + 16], w[kc * 128:(kc + 1) * 128, C_KR + 16:C_KR + 32], [], [rw], semkey=key)
                P.dma("pool", st["win"][:, kc, C_KRS + 16:C_KRS + 32], w[kc * 128:(kc + 1) * 128, C_KR:C_KR + 16], [], [rw], semkey=key)
            wq = self.w["even_w_uq"][i].rearrange("r (h d) -> r h d", h=8)
            for kc in range(3):
                src = wq[kc * 128:(kc + 1) * 128]
                dq = st["wuq"][:, kc, :]
                P.dma("pool", dq[:, 0:512].rearrange("p (h d) -> p h d", h=8), src[:, :, 0:64], [], [rw], semkey=key)
                P.dma("pool", dq[:, 512:768].rearrange("p (h d) -> p h d", h=8), src[:, :, 64:96], [], [rw], semkey=key)
                d2 = dq[:, 768:1024].rearrange("p (h d) -> p h d", h=8)
                P.dma("pool", d2[:, :, 0:16], src[:, :, 80:96], [], [rw], semkey=key)
                P.dma("pool", d2[:, :, 16:32], src[:, :, 64:80], [], [rw], semkey=key)
            wkv = self.w["even_w_ukv"][i].rearrange("r (h d) -> r h d", h=8)
            for kc in range(2):
                src = wkv[kc * 128:(kc + 1) * 128]
                dk = st["wukv"][:, kc, :]
                P.dma("pool", dk[:, 0:512].rearrange("p (h d) -> p h d", h=8), src[:, :, 0:64], [], [rw], semkey=key)
                P.dma("pool", dk[:, 512:1024].rearrange("p (h d) -> p h d", h=8), src[:, :, 64:128], [], [rw], semkey=key)
        else:
            self.load_weight(st["win"], self.w["odd_w_in"][i], 1024, 3072, rw)

    def evac_store(self, st, ps_i, rows, dst, scale=1.0, eng="act", src=None, rsrc=None):
        P = self.P
        k = st["stage_i"] % len(st["stage"])
        st["stage_i"] += 1
        stg = st["stage"][k]
        rs = P.R("stage", st["id"], k)
        if src is None:
            src, rsrc = self.ps[ps_i], self.psr[ps_i]
        if eng == "act":
            P.add("act", ACT(stg[0:rows, :], src[0:rows, :], AF.Copy, scale=scale), [rsrc], [rs])
        else:
            P.add("dve", TS(stg[0:rows, :], src[0:rows, :], scale, None, ALU.mult), [rsrc], [rs])
        for (r0, r1, dap, dres) in dst:
            P.dma("pool", dap, stg[r0:r1, :], [rs], [dres], semkey=("stage", st["id"], k))
        self.tick()

    def inproj_chunk(self, l, c, st, h, rh):
        P = self.P
        cs = slice(c * TC, (c + 1) * TC)
        rw = st["rw"]
        win = st["win"]
        PSP = [0, 1, 2, 3]
        ev = [0]

        def eng():
            ev[0] += 1
            return "act" if ev[0] % 2 else "dve"

        def proj_fm(col0, m, ps_i, w=win, nk=8, src=h, rsrc=rh):
            P.add("pe", [MM(self.ps[ps_i][0:m, :], w[:, kc, col0:col0 + m], src[:, kc, :], kc == 0, kc == nk - 1) for kc in range(nk)],
                  [rw, rsrc], [self.psr[ps_i]])

        def proj_tm(tt, col0, ncol, ps_i, w=win, nk=8, src=h, rsrc=rh):
            P.add("pe", [MM(self.ps[ps_i][:, 0:ncol], src[:, kc, tt * 128:(tt + 1) * 128], w[:, kc, col0:col0 + ncol], kc == 0, kc == nk - 1) for kc in range(nk)],
                  [rw, rsrc], [self.psr[ps_i]])

        t0 = c * TC
        if l % 2 == 1:
            rq = P.R("Qd", l, c)
            for j in range(8):
                pi = self.next_ps(PSP)
                proj_fm(j * 128, 128, pi)
                self.evac_store(st, pi, 128, [(0, 128, self.Qd[j * 128:(j + 1) * 128, cs], rq)], scale=0.125, eng=eng())
            for j in range(8):
                pi = self.next_ps(PSP)
                proj_fm(1024 + j * 128, 128, pi)
                self.evac_store(st, pi, 128, [(0, 128, self.Kd[j * 128:(j + 1) * 128, cs], rq)], eng=eng())
            for tt in range(4):
                for hf in range(2):
                    pi = self.next_ps(PSP)
                    proj_tm(tt, 2048 + hf * 512, 512, pi)
                    self.evac_store(st, pi, 128, [(0, 128, self.Vd[t0 + tt * 128:t0 + (tt + 1) * 128, hf * 512:(hf + 1) * 512], rq)], eng=eng())
            return
        i = l // 2
        rq = P.R("Qe", l, c)
        rfg, rlf, rrun, rcpos = P.R("fg", l), P.R("lf", l), P.R("run", l), P.R("cpos")
        fg, lf, run = st["fg"], st["lf"], st["run"]
        cqs, cqn, ckvs, ckvn = st["cqs"], st["cqn"], st["ckvs"], st["ckvn"]
        rcqs, rckvs, rtmp = st["r_cqs"], st["r_ckvs"], st["r_tmp"]
        al = [st["rf"]] if st["rf"] is not None else []
        rcqn, rckvn = P.R("cqn", l), P.R("ckvn", l)
        rrope = P.R("rope", l)
        rope = st["rope"]
        wuq, wukv = st["wuq"], st["wukv"]
        qscale = 96.0 ** -0.5
        cpos3 = self.cpos.rearrange("p (t h) -> p t h", h=8)
        crow, crow2, crs, crsp = st["crow"], st["crow2"], st["crs"], st["crsp"]
        rcr = P.R("crow", l)
        rc = [P.R("crs", l, k) for k in range(3)]
        rcp = [P.R("crsp", l, k) for k in range(3)]
        pcs = {}

        def b_fq(j):
            pi = self.next_ps(PSP)
            proj_fm(C_FQ + j * 128, 128, pi)
            self.evac_store(st, pi, 128, [(0, 128, self.Qf[j * 128:(j + 1) * 128, cs], rq)], scale=0.125, eng=eng())

        def b_fk(j):
            pi = self.next_ps(PSP)
            proj_fm(C_FK + j * 128, 128, pi)
            self.evac_store(st, pi, 128, [(0, 128, self.Kf[j * 128:(j + 1) * 128, cs], rq)], eng=eng())

        def b_fv(tt):
            pi = self.next_ps(PSP)
            proj_tm(tt, C_FV, 512, pi)
            self.evac_store(st, pi, 128, [(0, 128, self.Vf[t0 + tt * 128:t0 + (tt + 1) * 128, :], rq)], eng=eng())

        def b_fg():
            pi = self.next_ps(PSP)
            P.add("pe", [MM(self.ps[pi][:, tt * 8:(tt + 1) * 8], h[:, kc, tt * 128:(tt + 1) * 128], win[:, kc, C_FG:C_FG + 8], kc == 0, kc == 7)
                         for tt in range(4) for kc in range(8)], [rw, rh], [self.psr[pi]])
            bfv = self.vecs[:, V_BF + 32 * i:V_BF + 32 * i + 32]
            P.add("dve", TT(fg, self.ps[pi][:, 0:32], bfv, ALU.add), [self.psr[pi], self.rg], [rfg])
            P.add("act", ACT(fg, fg, AF.Exp, scale=-1.0), [rfg], [rfg])
            P.add("act", ACT(lf, fg, AF.Ln, bias=1.0), [rfg], [rlf])
            if c == 0:
                P.add("dve", MSET(run, 0.0), [], [rrun])
            pcs["pc"] = self.next_ps([5, 6])

        def b_cum(tt):
            pc = pcs["pc"]
            P.add("pe", [MM(self.ps[pc][:, tt * 8:(tt + 1) * 8], self.U_f, lf[:, tt * 8:(tt + 1) * 8], True, False),
                         MM(self.ps[pc][:, tt * 8:(tt + 1) * 8], self.ones_f, run, False, True)], [rlf, rrun, self.rg], [self.psr[pc]])
            P.add("dve", TT(run, run, lf[:, tt * 8:(tt + 1) * 8], ALU.add), [rrun, rlf], [rrun])

        def b_cpos():
            pc = pcs["pc"]
            P.add("act", ACT(self.cpos[:, c * 32:(c + 1) * 32], self.ps[pc][:, 0:32], AF.Copy), [self.psr[pc]], [rcpos])

        def b_crow():
            pr = self.next_ps([5, 6])
            P.add("pe", [MM(self.ps[pr][0:8, tt * 128:(tt + 1) * 128], cpos3[:, c * 4 + tt, :], self.ident_f, True, True) for tt in range(4)],
                  [rcpos, self.rg], [self.psr[pr]])
            P.add("act", ACT(crow[0:8, :], self.ps[pr][0:8, :], AF.Copy, scale=-1.0), [self.psr[pr]], [rcr])
            P.add("dve", CP(crs[0][0:8, :], crow[0:8, :]), [rcr], [rc[0]])
            P.add("dve", TT(crow2[0:8, :], crow[0:8, :], crs[0][0:8, :], ALU.subtract), [rcr, rc[0]], [rcr])
            P.add("dve", CP(crs[1][0:8, :], crow2[0:8, :]), [rcr], [rc[1]])
            P.add("dve", TT(crow[0:8, :], crow2[0:8, :], crs[1][0:8, :], ALU.subtract), [rcr, rc[1]], [rcr])
            P.add("dve", CP(crs[2][0:8, :], crow[0:8, :]), [rcr], [rc[2]])
            for k3 in range(3):
                P.add("dve", TS(crsp[k3][0:8, :], crs[k3][0:8, :], -1.0, None, ALU.mult), [rc[k3]], [rcp[k3]])
                P.dma("pool", self.CqR[:, k3, cs], crs[k3][0:8, :], [rc[k3]], [rq], semkey=("crs", l, k3))
                P.dma("pool", self.CkR[:, k3, cs], crsp[k3][0:8, :], [rcp[k3]], [rq], semkey=("crsp", l, k3))

        def b_ropeload():
            P.dma("sp", rope[:, 0, :], self.rope_in[0, :, cs], [], [rrope], semkey=("rope", l))
            P.dma("sp", rope[:, 1, :], self.rope_in[1, :, cs], [], [rrope], semkey=("rope", l))

        def rope_combine(pa, pb, rows, dst, scale):
            ta, tb = st["ropetmp"]
            P.add("dve", TT(ta[0:rows, :], self.ps[pa][0:rows, :], rope[0:rows, 0, :], ALU.mult), [self.psr[pa], rrope], [rtmp] + al)
            P.add("dve", TT(tb[0:rows, :], self.ps[pb][0:rows, :], rope[0:rows, 1, :], ALU.mult), [self.psr[pb], rrope], [rtmp] + al)
            P.add("dve", TT(ta[0:rows, :], ta[0:rows, :], tb[0:rows, :], ALU.add), [rtmp], [rtmp])
            self.evac_store(st, None, rows, dst, scale=scale, eng="act", src=ta, rsrc=rtmp)

        def b_kr():
            pa = self.next_ps(PSP)
            proj_fm(C_KR, 32, pa)
            pb = self.next_ps(PSP)
            proj_fm(C_KRS, 32, pb)
            rope_combine(pa, pb, 32, [(0, 32, self.Kr[:, cs], rq)], 1.0)

        def b_cq(kc):
            pi = self.next_ps(PSP)
            proj_fm(C_CQ + kc * 128, 128, pi)
            P.add("act", ACT(cqs[:, kc, :], self.ps[pi][:], AF.Copy), [self.psr[pi]], [rcqs] + al)

        def b_ckv(kc):
            pi = self.next_ps(PSP)
            proj_fm(C_CKV + kc * 128, 128, pi)
            P.add("act", ACT(ckvs[:, kc, :], self.ps[pi][:], AF.Copy), [self.psr[pi]], [rckvs] + al)

        st_cq = self.stats_steps(cqs, 3, 384.0, rcqs)
        st_ckv = self.stats_steps(ckvs, 2, 256.0, rckvs)

        def b_cqn():
            for kc in range(3):
                g = self.vecs[:, V_QN + 3 * i + kc:V_QN + 3 * i + kc + 1]
                P.add("dve", STT(cqn[:, kc, :], cqs[:, kc, :], g, self.rstd, ALU.mult, ALU.mult), [rcqs, self.rrstd, self.rg], [rcqn])

        def b_ckvn():
            for kc in range(2):
                g = self.vecs[:, V_KVN + 2 * i + kc:V_KVN + 2 * i + kc + 1]
                P.add("dve", STT(ckvn[:, kc, :], ckvs[:, kc, :], g, self.rstd, ALU.mult, ALU.mult), [rckvs, self.rrstd, self.rg], [rckvn])

        def b_qn(j):
            pi = self.next_ps(PSP)
            proj_fm(j * 128, 128, pi, w=wuq, nk=3, src=cqn, rsrc=rcqn)
            self.evac_store(st, pi, 128, [(0, 64, self.Qm[2 * j, 0:64, cs], rq), (64, 128, self.Qm[2 * j + 1, 0:64, cs], rq)], scale=qscale, eng=eng())

        def b_qr(j):
            pa = self.next_ps(PSP)
            proj_fm(512 + j * 128, 128, pa, w=wuq, nk=3, src=cqn, rsrc=rcqn)
            pb = self.next_ps(PSP)
            proj_fm(768 + j * 128, 128, pb, w=wuq, nk=3, src=cqn, rsrc=rcqn)
            rope_combine(pa, pb, 128, [(32 * q, 32 * q + 32, self.Qm[4 * j + q, 64:96, cs], rq) for q in range(4)], qscale)

        def b_kn(j):
            pi = self.next_ps(PSP)
            proj_fm(j * 128, 128, pi, w=wukv, nk=2, src=ckvn, rsrc=rckvn)
            self.evac_store(st, pi, 128, [(0, 128, self.Kmn[j * 128:(j + 1) * 128, cs], rq)], eng=eng())

        def b_v(tt):
            pi = self.next_ps(PSP)
            proj_tm(tt, 512, 512, pi, w=wukv, nk=2, src=ckvn, rsrc=rckvn)
            self.evac_store(st, pi, 128, [(0, 128, self.Vm[t0 + tt * 128:t0 + (tt + 1) * 128, :], rq)], eng=eng())

        b_ropeload()
        for kc in range(3):
            b_cq(kc)
        for kc in range(2):
            b_ckv(kc)
        b_fg()
        st_cq[0](); st_cq[1]()
        b_fq(0); b_fq(1)
        st_cq[2]()
        b_fq(2); b_fq(3)
        b_cqn()
        b_fk(0)
        st_ckv[0](); st_ckv[1]()
        b_cum(0)
        b_fk(1)
        st_ckv[2]()
        b_cum(1)
        b_fk(2)
        b_ckvn()
        b_cum(2)
        b_fk(3)
        b_cum(3)
        b_qn(0)
        b_cpos()
        b_qn(1); b_qn(2)
        b_crow()
        b_qn(3)
        b_fv(0); b_fv(1)
        b_qr(0)
        b_fv(2); b_fv(3)
        b_qr(1)
        for j in range(4):
            b_kn(j)
        for tt in range(4):
            b_v(tt)
        b_kr()

    def phase_T0(self):
        P = self.P
        self.alloc_tok_common()
        xs, rx = self.xss[0]
        hb, rhb = self.hbs[0]
        st = self.alloc_inproj(0)
        self.load_inproj_weights(0, st)
        rxin = P.R("xT_in")
        for c in range(self.NCH):
            self.load_x(c, self.xT_in, rxin, xs, rx)
            for stp in self.prenorm_steps(V_NMP + 0, xs, rx, hb, rhb):
                stp()
            self.inproj_chunk(0, c, st, hb, rhb)

    def phase_T1(self, l):
        P = self.P
        A = self.arena
        i = l // 2
        self.alloc_tok_common(nx=1, nhb=2)
        xs, rx = self.xss[0]
        pf = self.prefetch_T1(l, want_wout=True)
        wout, wup = pf["wout"], pf["wup"]
        at = A.alloc(8 * 512, BF16).rearrange("p (k c) -> p k c", k=8)
        m = A.alloc(8 * 512, F32).rearrange("p (k c) -> p k c", k=8)
        G = [A.alloc(514, F32) for _ in range(3)]
        T = [A.alloc(512, F32) for _ in range(3)]
        GE = [A.alloc(512, F32) for _ in range(2)]
        AC = [A.alloc(512, BF16) for _ in range(3)]
        halo = A.alloc(NJ * 2, F32)
        rwo, rwu = P.R("wout", l), P.R("wup", l)
        rat, rm, rhalo = P.R("at"), P.R("m"), P.R("halo")
        P.add("dve", MSET(halo, 0.0), [], [rhalo])
        rAT, rXT = P.R("AT"), P.R("XT")
        PSP = [0, 1, 2, 3]
        src_x = self.xT_in if l == 0 else self.XT

        def stage1(c):
            hb, rhb = self.hbs[c % 2]
            cs = slice(c * TC, (c + 1) * TC)
            steps = []

            def load_at(cc):
                cs_ = slice(cc * TC, (cc + 1) * TC)
                P.dma("sp", at, self.AT[:, :, cs_].rearrange("k p s -> p k s"), [rAT], [rat], semkey="at")

            def s_load():
                self.load_x(c, src_x, rXT, xs, rx)
                if c == 0:
                    load_at(0)
            steps.append(s_load)
            for j in range(8):
                def s_proj(j=j):
                    pi = self.next_ps(PSP)
                    P.add("pe", [MM(self.ps[pi][:], wout[:, kc, j * 128:(j + 1) * 128], at[:, kc, :], kc == 0, kc == 7) for kc in range(8)],
                          [rwo, rat], [self.psr[pi]])
                    P.add("act", ACT(m[:, j, :], self.ps[pi][:], AF.Copy), [self.psr[pi]], [rm])
                    if j == 7 and c + 1 < self.NCH:
                        load_at(c + 1)
                steps.append(s_proj)
            steps = self.merge_pairs(self.merge_pairs(steps[0:1] + steps[1:]))
            chain = self.residual_steps(m, rm, V_NMPOST + 8 * l, xs, rx, add_eng="dve")
            chain.append(lambda: self.store_x(c, self.XT, P.R("XTc", l, c), xs, rx))
            chain += self.prenorm_steps(V_NFP + 8 * l, xs, rx, hb, rhb)
            steps += self.spaced(chain, {2: 1, 8: 1})
            return steps

        for stp in stage1(0):
            stp()
        for c in range(self.NCH):
            cs = slice(c * TC, (c + 1) * TC)
            hb, rhb = self.hbs[c % 2]
            self.pending = stage1(c + 1) if c + 1 < self.NCH else []
            racts = P.R("ACTT", l, c)

            def stageA(j):
                pg = self.next_ps(PSP)
                pv = self.next_ps([5, 6, 7])
                P.add("pe", [MM(self.ps[pg][:], wup[:, kc, j * 128:(j + 1) * 128], hb[:, kc, :], kc == 0, kc == 7) for kc in range(8)],
                      [rwu, rhb], [self.psr[pg]])
                P.add("pe", [MM(self.ps[pv][:], wup[:, kc, DFF + j * 128:DFF + (j + 1) * 128], hb[:, kc, :], kc == 0, kc == 7) for kc in range(8)],
                      [rwu, rhb], [self.psr[pv]])
                g, t = G[j % 3], T[j % 3]
                rg_, rt_ = P.R("G", j % 3), P.R("T", j % 3)
                P.add("act", ACT(g[:, 2:514], self.ps[pg][:], AF.Copy), [self.psr[pg]], [rg_])
                P.add("pool", CP(g[:, 0:2], halo[:, 2 * j:2 * j + 2]), [rhalo], [rg_])
                cw = [self.vecs[:, V_CONVW + (l * 3 + tap) * NJ + j:V_CONVW + (l * 3 + tap) * NJ + j + 1] for tap in range(3)]
                cb = self.vecs[:, V_CONVB + l * NJ + j:V_CONVB + l * NJ + j + 1]
                P.add("pool", TS(t, g[:, 2:514], cw[2], cb, ALU.mult, ALU.add), [rg_, self.rg], [rt_])
                P.add("dve", STT(t, g[:, 1:513], cw[1], t, ALU.mult, ALU.add), [rg_, rt_, self.rg], [rt_])
                P.add("dve", STT(t, g[:, 0:512], cw[0], t, ALU.mult, ALU.add), [rg_, rt_, self.rg], [rt_])
                P.add("pool", CP(halo[:, 2 * j:2 * j + 2], g[:, 512:514]), [rg_], [rhalo])
                return pv

            def stageB(j, pv):
                t, ge, ac = T[j % 3], GE[j % 2], AC[j % 3]
                rt_, rge_, rac_ = P.R("T", j % 3), P.R("GE", j % 2), P.R("AC", j % 3)
                P.add("act", ACT(ge, t, AF.Gelu_apprx_tanh), [rt_], [rge_])
                P.add("dve", TT(ac, self.ps[pv][:], ge, ALU.mult), [self.psr[pv], rge_], [rac_])
                P.dma("pool", self.ACTT[j, :, cs], ac, [rac_], [racts], semkey=("AC", j % 3))

            pvs = {}
            pvs[0] = stageA(0)
            for j in range(1, NJ):
                pvs[j] = stageA(j)
                stageB(j - 1, pvs[j - 1])
                self.tick()
            stageB(NJ - 1, pvs[NJ - 1])
            self.flush()
        self.pf = None
        self.arena.top = self.arena.cap

    def phase_T2(self, l):
        P = self.P
        A = self.arena
        last = (l == self.depth - 1)
        nxt_even = (not last) and ((l + 1) % 2 == 0)
        zip_ = not nxt_even
        nbuf = 2 if last else 1
        self.alloc_tok_common(nx=nbuf, nhb=(2 if (zip_ and not last) else 1))
        wdn = A.alloc(NJ * 1024, BF16).rearrange("p (k c) -> p k c", k=NJ)
        acts = [(A.alloc(NJ * 512, BF16).rearrange("p (k c) -> p k c", k=NJ), P.R("act", q)) for q in range(nbuf)]
        fs = [(A.alloc(8 * 512, F32).rearrange("p (k c) -> p k c", k=8), P.R("f", q)) for q in range(nbuf)]
        rwd = P.R("wdn", l)
        self.load_weight(wdn, self.w["ffn_w_down"][l], DFF, 1024, rwd)
        st = None
        if not last:
            if nxt_even:
                st = self.alloc_inproj(l + 1, fbuf=fs[0][0], rf=fs[0][1])
            else:
                st = self.alloc_inproj(l + 1)
            self.load_inproj_weights(l + 1, st)
        rXT, rACTT = P.R("XT2"), P.R("ACTT2")
        PSP = [0, 1, 2, 3]
        alias_res = [st["r_cqs"], st["r_ckvs"], st["r_tmp"]] if nxt_even else []

        def stage_a(c):
            xs, rx = self.xss[c % nbuf]
            act, ract = acts[c % nbuf]
            f, rf = fs[c % nbuf]
            cs = slice(c * TC, (c + 1) * TC)
            steps = []

            def load_act(cc):
                a_, ra_ = acts[cc % nbuf]
                cs_ = slice(cc * TC, (cc + 1) * TC)
                P.dma("sp", a_, self.ACTT[:, :, cs_].rearrange("k p s -> p k s"), [rACTT], [ra_], semkey=("act", cc % nbuf))

            def s_load():
                self.load_x(c, self.XT, rXT, xs, rx)
                if c == 0 or nbuf > 1:
                    load_act(c)
            steps.append(s_load)
            for j in range(8):
                def s_proj(j=j):
                    pi = self.next_ps(PSP)
                    P.add("pe", [MM(self.ps[pi][:], wdn[:, kc, j * 128:(j + 1) * 128], act[:, kc, :], kc == 0, kc == NJ - 1) for kc in range(NJ)],
                          [rwd, ract], [self.psr[pi]])
                    P.add("act", ACT(f[:, j, :], self.ps[pi][:], AF.Copy), [self.psr[pi]], [rf] + alias_res)
                    if j == 7 and nbuf == 1 and c + 1 < self.NCH:
                        load_act(c + 1)
                steps.append(s_proj)
            return steps

        def stage_b(c):
            xs, rx = self.xss[c % nbuf]
            f, rf = fs[c % nbuf]
            steps = self.residual_steps(f, rf, V_NFPOST + 8 * l, xs, rx, add_eng="dve")
            if last:
                steps.append(lambda: self.store_x(c, self.out, P.R("OUT"), xs, rx))
            else:
                hb, rhb = self.hbs[c % len(self.hbs)]
                steps.append(lambda: self.store_x(c, self.XT, P.R("XTc2", l, c), xs, rx))
                steps += self.prenorm_steps(V_NMP + 8 * (l + 1), xs, rx, hb, rhb)
            return steps

        def run(steps):
            for stp in steps:
                stp()

        if last and not getattr(self, 'dbg_nolastpipe', False):
            run(stage_a(0))
            for c in range(self.NCH):
                sa = stage_a(c + 1) if c + 1 < self.NCH else []
                sb = stage_b(c)
                while sa or sb:
                    if sa:
                        sa.pop(0)()
                    if sb:
                        sb.pop(0)()
        elif last or not zip_:
            for c in range(self.NCH):
                run(stage_a(c))
                run(stage_b(c))
                if not last:
                    hb, rhb = self.hbs[0]
                    self.inproj_chunk(l + 1, c, st, hb, rhb)
        else:
            run(stage_a(0))
            run(stage_b(0))
            for c in range(self.NCH):
                hb, rhb = self.hbs[c % len(self.hbs)]
                if c + 1 < self.NCH:
                    sa = stage_a(c + 1)
                    sb = stage_b(c + 1)
                    self.pending = self.merge_pairs(sa[0:1] + sa[1:]) + self.spaced(sb, {2: 2, 3: 1, 5: 1, 8: 2, 9: 1})
                else:
                    self.pending = []
                self.inproj_chunk(l + 1, c, st, hb, rhb)
                self.flush()

    def phase_ATT(self, l):
        P = self.P
        A = self.arena
        S, NT, NCH = self.S, self.NT, self.NCH
        even = (l % 2 == 0)
        i = l // 2
        nsm = 1 if even else 2
        NSLOT = 2
        slots = []
        for s in range(NSLOT):
            sl = {}
            sl["QA"] = [A.alloc(S, BF16) for _ in range(nsm)]
            sl["KA"] = [A.alloc(S, BF16) for _ in range(nsm)]
            sl["V"] = A.alloc(NT * 128, BF16).rearrange("p (t d) -> p t d", d=128)
            sl["r"] = P.R("slot", l, s)
            sl["key"] = ("slot", l, s)
            slots.append(sl)
        NPT = 6
        PT = [A.alloc(512, BF16) for _ in range(NPT)]
        rPT = [P.R("PT", k) for k in range(NPT)]
        zs = A.alloc(512, F32)
        ab = [A.alloc(512, BF16) for _ in range(2)]
        rzs, rab = P.R("zs"), [P.R("ab", 0), P.R("ab", 1)]
        if even:
            for sl in slots:
                P.add("pool", MSET(sl["V"][:, :, 64:128], 1.0), [], [sl["r"]])
            P.barrier()
        else:
            o1, o2 = A.alloc(512, F32), A.alloc(512, F32)
            r1, r2 = A.alloc(512, F32), A.alloc(512, F32)
            sqo = A.alloc(512, BF16)
            rstd = A.alloc(512, F32)
            ro, ro2, rr1, rr2, rsqo, rrs = [P.R("otail", q) for q in range(6)]
            lt = self.lamt
            lam_init = 0.8 - 0.6 * math.exp(-0.3 * l)
            prod = A.alloc(128, F32)
            rl = P.R("lam", l)
            b = V_LAM + i * 256
            vv = self.vecs
            P.add("dve", TT(prod[:, 0:64], vv[:, b:b + 64], vv[:, b + 64:b + 128], ALU.mult), [self.rg], [rl])
            P.add("dve", TT(prod[:, 64:128], vv[:, b + 128:b + 192], vv[:, b + 192:b + 256], ALU.mult), [self.rg, rl], [rl])
            P.add("dve", ("tensor_reduce", dict(out=lt[:, 0:1], in_=prod[:, 0:64], axis=mybir.AxisListType.X, op=ALU.add)), [rl], [rl])
            P.add("dve", ("tensor_reduce", dict(out=lt[:, 1:2], in_=prod[:, 64:128], axis=mybir.AxisListType.X, op=ALU.add)), [rl], [rl])
            P.add("act", ACT(lt[:, 2:4], lt[:, 0:2], AF.Exp), [rl], [rl])
            P.add("dve", TT(lt[:, 4:5], lt[:, 3:4], lt[:, 2:3], ALU.subtract), [rl], [rl])
            P.add("dve", TS(lt[:, 5:6], lt[:, 4:5], -lam_init, None, ALU.add), [rl], [rl])
            P.add("dve", TS(lt[:, 6:7], vv[:, V_SUBLN + i:V_SUBLN + i + 1], 1.0 - lam_init, None, ALU.mult), [self.rg, rl], [rl])
            P.barrier()
            neglam = lt[:, 5:6]
            subs = lt[:, 6:7]
        rAT = P.R("AT_w", l)
        rsrc = P.R("attsrc", l)
        cpos3 = self.cpos.rearrange("p (t h) -> p t h", h=8)
        alibi = self.cf[:, 384:384 + 8 * NT].rearrange("p (h t) -> p h t", h=8)
        units = list(range(16)) if even else list(range(8))
        SPS = [0, 1, 2]
        VSPL = 8

        def load_v(sl, dst_cols, src2d):
            v3 = src2d.rearrange("(t p) d -> p t d", p=128)
            for t0 in range(0, NT, VSPL):
                P.dma("sp", sl["V"][:, t0:t0 + VSPL, dst_cols], v3[:, t0:t0 + VSPL, :], [rsrc], [sl["r"]], semkey=sl["key"])

        def load_unit(u, sl):
            r, key = sl["r"], sl["key"]
            if even:
                if u < 8:
                    h = u
                    P.dma("sp", sl["QA"][0][0:64, :], self.Qf[h * 64:(h + 1) * 64, :], [rsrc], [r], semkey=key)
                    P.dma("sp", sl["QA"][0][64:67, :], self.CqR[h], [rsrc], [r], semkey=key)
                    P.dma("sp", sl["KA"][0][0:64, :], self.Kf[h * 64:(h + 1) * 64, :], [rsrc], [r], semkey=key)
                    P.dma("sp", sl["KA"][0][64:67, :], self.onesrow_in, [rsrc], [r], semkey=key)
                    P.dma("sp", sl["QA"][0][67:70, :], self.onesrow_in, [rsrc], [r], semkey=key)
                    P.dma("sp", sl["KA"][0][67:70, :], self.CkR[h], [rsrc], [r], semkey=key)
                    load_v(sl, slice(0, 64), self.Vf[:, h * 64:(h + 1) * 64])
                else:
                    h = u - 8
                    P.dma("sp", sl["QA"][0][0:96, :], self.Qm[h], [rsrc], [r], semkey=key)
                    P.dma("sp", sl["KA"][0][0:64, :], self.Kmn[h * 64:(h + 1) * 64, :], [rsrc], [r], semkey=key)
                    P.dma("sp", sl["KA"][0][64:96, :], self.Kr, [rsrc], [r], semkey=key)
                    load_v(sl, slice(0, 64), self.Vm[:, h * 64:(h + 1) * 64])
            else:
                h = u
                for s in range(2):
                    P.dma("sp", sl["QA"][s][0:64, :], self.Qd[h * 128 + s * 64:h * 128 + (s + 1) * 64, :], [rsrc], [r], semkey=key)
                    P.dma("sp", sl["QA"][s][64:66, :], self.aq_in[h], [rsrc], [r], semkey=key)
                    P.dma("sp", sl["KA"][s][0:64, :], self.Kd[h * 128 + s * 64:h * 128 + (s + 1) * 64, :], [rsrc], [r], semkey=key)
                    P.dma("sp", sl["KA"][s][64:66, :], self.onesrow_in[0:2, :], [rsrc], [r], semkey=key)
                    P.dma("sp", sl["QA"][s][66:68, :], self.onesrow_in[0:2, :], [rsrc], [r], semkey=key)
                    P.dma("sp", sl["KA"][s][66:68, :], self.ak_in[h], [rsrc], [r], semkey=key)
                load_v(sl, slice(0, 128), self.Vd[:, h * 128:(h + 1) * 128])

        load_unit(units[0], slots[0])
        deferred = []
        zhold = [None, None]
        zq = []
        psz_i = 0
        PSZ = [A.alloc(512, BF16) for _ in range(3)]
        rPSZ = [P.R("PSZ", q) for q in range(3)]
        pt_i = 0
        sp_i = 0
        tail_i = 0
        for ui, u in enumerate(units):
            sl = slots[ui % NSLOT]
            if ui + 1 < len(units):
                load_unit(units[ui + 1], slots[(ui + 1) % NSLOT])
            if ui == 0:
                self.prefetch_T1(l, want_wout=even, after=[slots[0]["r"], slots[1]["r"]])
            kdim = (70 if u < 8 else 96) if even else 68
            QA, KA, V, rs_ = sl["QA"], sl["KA"], sl["V"], sl["r"]
            for c in range(NCH):
                cs = slice(c * TC, (c + 1) * TC)
                nkt = 4 * c + 4
                acc = [3 + (tail_i % 2)] if even else [3, 4, 5, 6]
                items = [(kt, s) for s in range(nsm) for kt in range(nkt)]
                LOOK = 2
                pend = []
                n = len(items)
                for t in range(n + LOOK):
                    if t < n:
                        kt, s = items[t]
                        spi = SPS[sp_i % 3]
                        sp_i += 1
                        j = kt - 4 * c
                        q0 = 0 if j <= 0 else 128 * j
                        ps = self.ps[spi]
                        ins = [MM(ps[:, q0:512], KA[s][0:kdim, kt * 128:(kt + 1) * 128], QA[s][0:kdim, c * TC + q0:(c + 1) * TC], True, j < 0)]
                        if j >= 0:
                            ins.append(MM(ps[:, q0:q0 + 128], self.ident_bf, self.maskT, False, True))
                        P.add("pe", ins, [rs_, self.rg], [self.psr[spi]])
                        pend.append((kt, s, spi, q0))
                    if t >= LOOK:
                        kt, s, spi, q0 = pend.pop(0)
                        pti = pt_i % NPT
                        pt_i += 1
                        ps = self.ps[spi]
                        rd = [self.psr[spi], self.rg]
                        P.add("act", ACT(PT[pti][:, q0:512], ps[:, q0:512], AF.Exp, bias=0.0), rd, [rPT[pti]])
                        first, lastk = (kt == 0), (kt == nkt - 1)
                        if even:
                            P.add("pe", MM(self.ps[acc[0]][:, q0:512], V[:, kt, :], PT[pti][:, q0:512], first, lastk),
                                  [rs_, rPT[pti]], [self.psr[acc[0]]])
                        else:
                            P.add("pe", MM(self.ps[acc[s]][:, q0:512], V[:, kt, :], PT[pti][:, q0:512], first, lastk),
                                  [rs_, rPT[pti]], [self.psr[acc[s]]])
                            while zq and zq[0][0] < t:
                                zq.pop(0)[1]()
                            if kt < 4 * c and kt % 2 == 0:
                                zhold[s] = pti
                            elif kt < 4 * c:
                                pa = zhold[s]
                                zi = psz_i % len(PSZ)
                                psz_i += 1
                                P.add("dve", TT(PSZ[zi], PT[pa], PT[pti], ALU.add), [rPT[pa], rPT[pti]], [rPSZ[zi]])
                                zq.append((t, lambda zi=zi, s=s, kt=kt: P.add(
                                    "pe", MM(self.ps[acc[2 + s]][:], self.ones_bf, PSZ[zi], kt == 1, False),
                                    [rPSZ[zi], self.rg], [self.psr[acc[2 + s]]])))
                            else:
                                P.add("pe", MM(self.ps[acc[2 + s]][:, q0:512], self.ones_bf, PT[pti][:, q0:512], first, lastk),
                                      [rPT[pti], self.rg], [self.psr[acc[2 + s]]])
                        if (not even) and lastk:
                            if s == 0:
                                while deferred:
                                    deferred.pop(0)()
                                P.add("dve", CP(o1, self.ps[3][:]), [self.psr[3]], [ro])
                                P.add("act", ACT(r1, self.ps[5][:], AF.Ln), [self.psr[5]], [rr1])
                            else:
                                P.add("dve", CP(o2, self.ps[4][:]), [self.psr[4]], [ro2])
                                P.add("act", ACT(r2, self.ps[6][:], AF.Ln), [self.psr[6]], [rr2])
                        if deferred and t >= LOOK + 1:
                            deferred.pop(0)()
                if even:
                    while deferred:
                        deferred.pop(0)()
                k = tail_i % 2
                tail_i += 1
                steps = []
                if even:
                    a = acc[0]
                    steps.append(lambda a=a: P.add("dve", RCP(zs[64:128, :], self.ps[a][64:128, :]), [self.psr[a]], [rzs]))
                    kc = (u // 2) if u < 8 else 4 + (u - 8) // 2
                    po = (u % 2) * 64

                    def fin(a=a, k=k, kc=kc, po=po, cs=cs):
                        P.add("dve", TT(ab[k][0:64, :], self.ps[a][0:64, :], zs[64:128, :], ALU.mult), [self.psr[a], rzs], [rab[k]])
                        P.dma("pool", self.AT[kc, po:po + 64, cs], ab[k][0:64, :], [rab[k]], [rAT], semkey=("ab", k))
                    steps.append(fin)
                else:
                    steps.append(lambda: P.add("act", ACT(r1, r1, AF.Exp, scale=-1.0), [rr1], [rr1]))
                    steps.append(lambda: P.add("act", ACT(r2, r2, AF.Exp, scale=-1.0), [rr2], [rr2]))
                    steps.append(lambda: P.add("dve", TT(o1, o1, r1, ALU.mult), [ro, rr1], [ro]))
                    steps.append(lambda: P.add("dve", TT(o2, o2, r2, ALU.mult), [ro2, rr2], [ro2]))
                    steps.append(lambda: P.add("dve", STT(o1, o2, neglam, o1, ALU.mult, ALU.add), [ro, ro2], [ro]))
                    steps.append(lambda: P.add("act", ACT(sqo, o1, AF.Square), [ro], [rsqo]))
                    steps.append(lambda: P.add("pe", MM(self.ps[7][:], self.ones_bf, sqo, True, True), [rsqo, self.rg], [self.psr[7]]))
                    steps.append(lambda: P.add("act", ACT(rstd, self.ps[7][:], AF.Ln, scale=1.0 / 128, bias=self.epsb), [self.psr[7], self.rg], [rrs]))
                    steps.append(lambda: P.add("act", ACT(rstd, rstd, AF.Exp, scale=-0.5), [rrs], [rrs]))

                    def fin(k=k, u=u, cs=cs):
                        P.add("dve", STT(ab[k], o1, subs, rstd, ALU.mult, ALU.mult), [ro, rrs], [rab[k]])
                        P.dma("pool", self.AT[u, :, cs], ab[k], [rab[k]], [rAT], semkey=("ab", k))
                    steps.append(fin)
                deferred.extend(steps)
        while deferred:
            deferred.pop(0)()

    def build(self, stop_after=None, only=None):
        P = self.P
        self.init_globals()
        P.barrier()
        seq = [("T0", 0)]
        for l in range(self.depth):
            seq += [("ATT", l), ("T1", l), ("T2", l)]
        for (ph, l) in seq:
            if only is not None and (ph, l) not in only:
                continue
            if ph == "T0":
                self.phase_T0()
            elif ph == "ATT":
                self.phase_ATT(l)
            elif ph == "T1":
                self.phase_T1(l)
            else:
                self.phase_T2(l)
            self.phase_reset()
            if stop_after == (ph, l):
                break
        P.lower()
        return self.nc


_CACHE = {}
SHARED_KEYS = ["even_w_in", "even_w_uq", "even_w_ukv", "even_w_out", "odd_w_in", "odd_w_out", "ffn_w_up", "ffn_w_down"]


def make_in_maps(inputs, S, B):
    consts = _CACHE.get(("consts", S))
    if consts is None:
        consts = make_consts(S)
        _CACHE[("consts", S)] = consts
    shared = {"vecs": pack_vecs(inputs)}
    for k in SHARED_KEYS:
        shared[k] = np.ascontiguousarray(np.asarray(inputs[k], np.float32))
    for k in ["cbf", "cf", "rope", "aq", "ak", "onesrow"]:
        shared[k] = consts[k]
    in_maps = []
    for b in range(B):
        m = dict(shared)
        x = np.asarray(inputs["x"][b], np.float32)
        m["xT"] = np.ascontiguousarray(x.T).reshape(8, 128, S)
        in_maps.append(m)
    return in_maps


def kernel(**inputs):
    x = np.asarray(inputs["x"])
    B, S, _ = x.shape
    bld = Builder(S)
    nc = bld.build()
    in_maps = make_in_maps(inputs, S, B)
    res = run_bass_kernel_spmd(nc, in_maps, core_ids=list(range(B)))
    out = np.empty((B, S, D), np.float32)
    for b in range(B):
        out[b] = np.asarray(res.results[b]["outT"]).reshape(D, S).T
    return out
```
